# Optimizing a Trainium2 kernel written in Bass

```python
import jax, jax.numpy as jnp
from jax import lax
import numpy as np

D_MODEL = 1024
BATCH = 2
SEQ = 8192
DEPTH = 1

HEAD_DIM = 64
ATTN_HEADS = 8
ATTN_W = ATTN_HEADS * HEAD_DIM
CONV_W = 256
MEM_HEADS = 4
MEM_W = MEM_HEADS * HEAD_DIM
MIX_W = ATTN_W + CONV_W + MEM_W
N_MEM = 256
DILATED_PATTERNS = ((128, 1), (512, 4), (2048, 16))
ROPE_THETA = 500000.0
ROT_DIM = HEAD_DIM // 4
CONV_K = 31
FFN_CONV_K = 3
D_FF = 2816
IN_COLS = 3 * ATTN_W + 2 * CONV_W + MEM_W
SPLITS = (ATTN_W, 2 * ATTN_W, 3 * ATTN_W, 3 * ATTN_W + 2 * CONV_W)
NORM_EPS = 1e-6
NEG_INF = -1e30

kernel_name = "hybrid_dilated_conformer_memory_block"


def rms_norm(x, g):
    xf = x.astype(jnp.float32)
    y = xf * lax.rsqrt(jnp.mean(xf * xf, axis=-1, keepdims=True) + NORM_EPS)
    return (y * g.astype(jnp.float32)).astype(x.dtype)


def layer_norm(x, g, b):
    xf = x.astype(jnp.float32)
    mu = jnp.mean(xf, axis=-1, keepdims=True)
    var = jnp.mean(jnp.square(xf - mu), axis=-1, keepdims=True)
    y = (xf - mu) * lax.rsqrt(var + NORM_EPS)
    return (y * g.astype(jnp.float32) + b.astype(jnp.float32)).astype(x.dtype)


def partial_rotary(x, positions):
    inv_freq = ROPE_THETA ** (-jnp.arange(0, ROT_DIM, 2, dtype=jnp.float32) / ROT_DIM)
    ang = positions.astype(jnp.float32)[..., None] * inv_freq
    cos = jnp.cos(ang)[:, :, None, :]
    sin = jnp.sin(ang)[:, :, None, :]
    xr = x[..., :ROT_DIM].astype(jnp.float32)
    x1, x2 = xr[..., : ROT_DIM // 2], xr[..., ROT_DIM // 2:]
    rot = jnp.concatenate([x1 * cos - x2 * sin, x2 * cos + x1 * sin], axis=-1)
    return jnp.concatenate([rot.astype(x.dtype), x[..., ROT_DIM:]], axis=-1)


def depthwise_conv(x, w, b):
    k = w.shape[0]
    y = lax.conv_general_dilated(
        x, w[:, None, :].astype(x.dtype), window_strides=(1,),
        padding=[(k // 2, k // 2)], dimension_numbers=("NWC", "WIO", "NWC"),
        feature_group_count=x.shape[-1])
    return y + b.astype(x.dtype)


def banded_attention(q, k, v, half):
    n, L, H, Dh = q.shape
    blk = half
    nb = -(-L // blk)
    Lp = nb * blk
    qb = jnp.pad(q, ((0, 0), (0, Lp - L), (0, 0), (0, 0))).reshape(n, nb, blk, H, Dh)

    def halo(t):
        tp = jnp.pad(t, ((0, 0), (blk, Lp - L + blk), (0, 0), (0, 0))).reshape(n, nb + 2, blk, H, Dh)
        return jnp.concatenate([tp[:, :-2], tp[:, 1:-1], tp[:, 2:]], axis=2)

    kb, vb = halo(k), halo(v)
    s = jnp.einsum("nbqhd,nbkhd->nbhqk", qb, kb).astype(jnp.float32) * (Dh ** -0.5)
    bidx = jnp.arange(nb)[:, None, None]
    qi = jnp.arange(blk)[None, :, None]
    kj = jnp.arange(3 * blk)[None, None, :]
    key_pos = bidx * blk - blk + kj
    valid = (jnp.abs(kj - blk - qi) <= half) & (key_pos >= 0) & (key_pos < L)
    s = jnp.where(valid[None, :, None], s, NEG_INF)
    m = jnp.max(s, axis=-1, keepdims=True)
    p = jnp.exp(s - m)
    den = jnp.sum(p, axis=-1, keepdims=True)
    o = jnp.einsum("nbhqk,nbkhd->nbqhd", p, vb.astype(jnp.float32))
    o = o / jnp.transpose(den, (0, 1, 3, 2, 4))
    lse = jnp.transpose((m + jnp.log(den))[..., 0], (0, 1, 3, 2))
    return o.reshape(n, Lp, H, Dh)[:, :L], lse.reshape(n, Lp, H)[:, :L]


def dilated_attention(q, k, v, window, dil):
    B, S, H, Dh = q.shape
    L = S // dil

    def to_sub(t):
        return t.reshape(B, L, dil, H, Dh).transpose(0, 2, 1, 3, 4).reshape(B * dil, L, H, Dh)

    o, lse = banded_attention(to_sub(q), to_sub(k), to_sub(v), window // (2 * dil))
    o = o.reshape(B, dil, L, H, Dh).transpose(0, 2, 1, 3, 4).reshape(B, S, H, Dh)
    lse = lse.reshape(B, dil, L, H).transpose(0, 2, 1, 3).reshape(B, S, H)
    return o, lse


def setup_inputs(seed: int = 0) -> dict:
    key = jax.random.key(seed)
    ks = jax.random.split(key, 24)
    f32 = jnp.float32

    def nrm(k, shape, scale):
        return jax.random.normal(k, shape, f32) * scale

    def gain(k, shape):
        return 1.0 + 0.02 * jax.random.normal(k, shape, f32)

    offsets = jax.random.randint(ks[2], (BATCH, 1), 0, 1024, dtype=jnp.int32)
    positions = (jnp.arange(SEQ, dtype=jnp.int32)[None, :] + offsets).astype(jnp.int32)
    return {
        "x": nrm(ks[0], (BATCH, SEQ, D_MODEL), 1.0),
        "mem": nrm(ks[1], (BATCH, N_MEM, D_MODEL), 1.0),
        "positions": positions,
        "mix_norm_g": gain(ks[3], (DEPTH, D_MODEL)),
        "mem_norm_g": gain(ks[4], (DEPTH, D_MODEL)),
        "w_in": nrm(ks[5], (DEPTH, D_MODEL, IN_COLS), D_MODEL ** -0.5),
        "w_mem_kv": nrm(ks[6], (DEPTH, D_MODEL, 2 * MEM_W), D_MODEL ** -0.5),
        "q_norm_g": gain(ks[7], (DEPTH, HEAD_DIM)),
        "k_norm_g": gain(ks[8], (DEPTH, HEAD_DIM)),
        "mq_norm_g": gain(ks[9], (DEPTH, HEAD_DIM)),
        "mk_norm_g": gain(ks[10], (DEPTH, HEAD_DIM)),
        "conv_dw_w": nrm(ks[11], (DEPTH, CONV_K, CONV_W), CONV_K ** -0.5),
        "conv_dw_b": nrm(ks[12], (DEPTH, CONV_W), 0.02),
        "conv_ln_g": gain(ks[13], (DEPTH, CONV_W)),
        "conv_ln_b": nrm(ks[14], (DEPTH, CONV_W), 0.02),
        "w_out": nrm(ks[15], (DEPTH, MIX_W, D_MODEL), MIX_W ** -0.5),
        "ffn_norm_g": gain(ks[16], (DEPTH, D_MODEL)),
        "w_up": nrm(ks[17], (DEPTH, D_MODEL, 2 * D_FF), D_MODEL ** -0.5),
        "ffn_dw_w": nrm(ks[18], (DEPTH, FFN_CONV_K, 2 * D_FF), FFN_CONV_K ** -0.5),
        "ffn_dw_b": nrm(ks[19], (DEPTH, 2 * D_FF), 0.02),
        "w_down": nrm(ks[20], (DEPTH, D_FF, D_MODEL), D_FF ** -0.5),
    }


def reference(x, mem, positions, mix_norm_g, mem_norm_g, w_in, w_mem_kv, q_norm_g, k_norm_g,
              mq_norm_g, mk_norm_g, conv_dw_w, conv_dw_b, conv_ln_g, conv_ln_b, w_out,
              ffn_norm_g, w_up, ffn_dw_w, ffn_dw_b, w_down):
    B, S, _ = x.shape
    h = x
    for l in range(DEPTH):
        hn = rms_norm(h, mix_norm_g[l])
        proj = hn @ w_in[l]
        q, k, v, glu, qm = jnp.split(proj, SPLITS, axis=-1)

        q = partial_rotary(rms_norm(q.reshape(B, S, ATTN_HEADS, HEAD_DIM), q_norm_g[l]), positions)
        k = partial_rotary(rms_norm(k.reshape(B, S, ATTN_HEADS, HEAD_DIM), k_norm_g[l]), positions)
        v = v.reshape(B, S, ATTN_HEADS, HEAD_DIM)
        outs, lses = [], []
        for window, dil in DILATED_PATTERNS:
            o, lse = dilated_attention(q, k, v, window, dil)
            outs.append(o)
            lses.append(lse)
        wts = jax.nn.softmax(jnp.stack(lses, axis=0), axis=0)
        attn = jnp.sum(wts[..., None] * jnp.stack(outs, axis=0), axis=0)
        attn = attn.astype(h.dtype).reshape(B, S, ATTN_W)

        ga, gg = jnp.split(glu, 2, axis=-1)
        c = ga * jax.nn.sigmoid(gg)
        c = depthwise_conv(c, conv_dw_w[l], conv_dw_b[l])
        c = jax.nn.silu(layer_norm(c, conv_ln_g[l], conv_ln_b[l]))

        qm = rms_norm(qm.reshape(B, S, MEM_HEADS, HEAD_DIM), mq_norm_g[l])
        kvm = rms_norm(mem, mem_norm_g[l]) @ w_mem_kv[l]
        km, vm = jnp.split(kvm, 2, axis=-1)
        km = rms_norm(km.reshape(B, N_MEM, MEM_HEADS, HEAD_DIM), mk_norm_g[l])
        vm = vm.reshape(B, N_MEM, MEM_HEADS, HEAD_DIM)
        sm = jnp.einsum("bshd,bmhd->bhsm", qm, km).astype(jnp.float32) * (HEAD_DIM ** -0.5)
        pm = jax.nn.softmax(sm, axis=-1)
        mo = jnp.einsum("bhsm,bmhd->bshd", pm, vm.astype(jnp.float32))
        mo = mo.astype(h.dtype).reshape(B, S, MEM_W)

        mixed = jnp.concatenate([attn, c, mo], axis=-1)
        h = h + mixed @ w_out[l]

        f = rms_norm(h, ffn_norm_g[l]) @ w_up[l]
        f = depthwise_conv(f, ffn_dw_w[l], ffn_dw_b[l])
        fg, fu = jnp.split(f, 2, axis=-1)
        h = h + (jax.nn.silu(fg) * fu) @ w_down[l]
    return h
```

```python
import contextlib
import numpy as np
import concourse.bass as bass
import concourse.mybir as mybir
from concourse.bass_utils import run_bass_kernel_spmd

F32 = mybir.dt.float32
BF16 = mybir.dt.bfloat16
I32 = mybir.dt.int32
ALU = mybir.AluOpType
AF = mybir.ActivationFunctionType
AX = mybir.AxisListType

S = 8192
NWIN = 4224
NT = 33
UQ0 = 1087
NQ = 2050
QC0 = 1024
NQC = 2176
EPS = 1e-6
DFF = 2816
NPAIR = 22
FFN_GROUPS = [(0, 2), (2, 6), (6, 10), (10, 14), (14, 18), (18, 22)]
DEBUG = False

ENGS = ("pe", "act", "dve", "pool", "sp")


class Buf:
    __slots__ = ("name", "lw", "rd")

    def __init__(self, name):
        self.name = name
        self.lw = None
        self.rd = []


class Op:
    __slots__ = ("eng", "fn", "deps", "signal", "semval", "dsem", "is_dma", "virtual")

    def __init__(self, eng, fn, is_dma=False, dsem=None):
        self.eng = eng
        self.fn = fn
        self.deps = []
        self.signal = False
        self.semval = None
        self.dsem = dsem
        self.is_dma = is_dma
        self.virtual = False


class DSem:
    def __init__(self, name):
        self.name = name
        self.count = 0
        self.h = None


class Prog:
    def __init__(self, nc):
        self.nc = nc
        self.ops = {e: [] for e in ENGS}
        self.dsems = []

    def dsem(self, name):
        d = DSem(name)
        self.dsems.append(d)
        return d

    def add(self, eng, fn, reads=(), writes=(), dsem=None):
        is_dma = dsem is not None
        op = Op(eng, fn, is_dma=is_dma, dsem=dsem)
        deps = []
        for b in reads:
            if b.lw is not None:
                deps.append(b.lw)
        for b in writes:
            if b.lw is not None:
                deps.append(b.lw)
            deps.extend(b.rd)
        seen = set()
        for d in deps:
            if d is op or id(d) in seen:
                continue
            seen.add(id(d))
            if (not d.is_dma) and d.eng == "pe" and eng == "pe" and not is_dma:
                continue
            op.deps.append(d)
        for b in reads:
            b.rd.append(op)
        for b in writes:
            b.lw = op
            b.rd = []
        self.ops[eng].append(op)
        if is_dma:
            dsem.count += 1
            op.semval = 16 * dsem.count
        return op

    def guard(self, writes=(), extra_deps=()):
        op = Op(None, None)
        op.virtual = True
        seen = set()
        for b in writes:
            for d in ([b.lw] if b.lw is not None else []) + list(b.rd):
                if id(d) not in seen:
                    seen.add(id(d))
                    op.deps.append(d)
        for d in extra_deps:
            if id(d) not in seen:
                seen.add(id(d))
                op.deps.append(d)
        for b in writes:
            b.lw = op
            b.rd = []
        return op

    def _flatten(self):
        memo = {}

        def flat(op):
            k = id(op)
            if k in memo:
                return memo[k]
            out, seen = [], set()
            for d in op.deps:
                for r in (flat(d) if getattr(d, "virtual", False) else [d]):
                    if id(r) not in seen:
                        seen.add(id(r))
                        out.append(r)
            memo[k] = out
            return out

        for e in ENGS:
            for op in self.ops[e]:
                if any(getattr(d, "virtual", False) for d in op.deps):
                    real = []
                    seen = set()
                    for d in op.deps:
                        for r in (flat(d) if getattr(d, "virtual", False) else [d]):
                            if r is op or id(r) in seen:
                                continue
                            if (not r.is_dma) and r.eng == "pe" and op.eng == "pe" and not op.is_dma:
                                continue
                            seen.add(id(r))
                            real.append(r)
                    op.deps = real

    def emit(self):
        nc = self.nc
        self._flatten()
        for e in ENGS:
            for op in self.ops[e]:
                for d in op.deps:
                    if not d.is_dma:
                        d.signal = True
        for e in ENGS:
            c = 0
            for op in self.ops[e]:
                if op.is_dma:
                    continue
                if op.signal:
                    c += 1
                    op.semval = c
        with contextlib.ExitStack() as st:
            esem = {e: st.enter_context(nc.semaphore("s_" + e)) for e in ENGS}
            for d in self.dsems:
                d.h = st.enter_context(nc.semaphore("d_" + d.name))
            block = st.enter_context(nc.Block())

            def run(eng_name):
                def body(eng):
                    waited = {}
                    for op in self.ops[eng_name]:
                        need = {}
                        for d in op.deps:
                            if d.is_dma:
                                key = ("d", id(d.dsem)); h = d.dsem.h
                            else:
                                key = ("e", d.eng); h = esem[d.eng]
                            if key not in need or need[key][1] < d.semval:
                                need[key] = (h, d.semval)
                        for key, (h, val) in need.items():
                            if waited.get(key, 0) >= val:
                                continue
                            eng.wait_ge(h, val)
                            waited[key] = val
                        ins = op.fn(eng)
                        if op.is_dma:
                            ins.then_inc(op.dsem.h, 16)
                        elif op.signal:
                            ins.then_inc(esem[eng_name], 1)
                return body

            block.sync(run("sp"))
            block.tensor(run("pe"))
            block.scalar(run("act"))
            block.vector(run("dve"))
            block.gpsimd(run("pool"))


def pattern_tiles():
    res = []
    idx = 0
    for D in (1, 4, 16):
        for rho in range(D):
            sq0 = -((-(UQ0 - rho)) // D)
            sq1 = (UQ0 + NQ - 1 - rho) // D
            nq = sq1 - sq0 + 1
            nqb = (nq + 127) // 128
            tiles = []
            for j in range(nqb + 1):
                a = sq0 - 64 + 128 * j
                nk = min(128, sq1 + 64 - a + 1)
                tiles.append((a, nk, idx))
                idx += 1
            res.append((D, rho, sq0, nq, nqb, tiles))
    return res, idx


def build_program():
    nc = bass.Bass("TRN2", target_bir_lowering=False)
    P = Prog(nc)
    PT, NVT = pattern_tiles()

    def din(name, shape, dt=F32):
        return nc.dram_tensor(name, list(shape), dt, kind="ExternalInput").ap()

    x_win = din("x_win", [NWIN, 1024])
    pos_win = din("pos_win", [128, NT], I32)
    valcols_d = din("valcols", [128, NVT])
    mem_d = din("mem", [256, 1024])
    ident_d = din("ident", [128, 128])
    bandmask_d = din("bandmask", [128, 256])
    invf_d = din("invf", [128, 8])
    g1_d = din("g1", [1, 1024])
    g2_d = din("g2", [1, 1024])
    gm_d = din("gm", [1, 1024])
    hg_d = din("hg", [1, 256])
    w_in_d = din("w_in", [1024, 2304])
    w_mkv_d = din("w_mkv", [1024, 512])
    w_out_d = din("w_out", [1024, 1024])
    w_up_d = din("w_up", [1024, 2 * DFF])
    w_down_d = din("w_down", [DFF, 1024])
    cw_d = din("cw", [128, 62])
    cp_d = din("cp", [128, 6])
    fw_d = din("fw", [128, 44 * 3])
    fb_d = din("fb", [128, 44])
    hv_d = din("hv", [128, 2])
    y_d = nc.dram_tensor("y", [2048, 1024], F32, kind="ExternalOutput").ap()
    dbg = {}
    if DEBUG:
        dbg["mixedT"] = nc.dram_tensor("dbg_mixedT", [128, 8 * NQ], F32, kind="ExternalOutput").ap()
        dbg["hmid"] = nc.dram_tensor("dbg_hmid", [128, 16 * 1024], F32, kind="ExternalOutput").ap()

    st = contextlib.ExitStack()
    with st:
        ARENA_KB = 204
        arena = st.enter_context(nc.sbuf_tensor("arena", [128, ARENA_KB * 512], BF16))
        pbanks = [st.enter_context(nc.psum_tensor("pq%d" % i, [128, 1024], F32)) for i in range(4)]

        class Region:
            def __init__(self, start, end):
                self.pos = start
                self.end = end

            def alloc(self, nbytes):
                nbytes = (nbytes + 31) // 32 * 32
                off = self.pos
                self.pos += nbytes
                assert self.pos <= self.end, ("arena overflow", self.pos, self.end)
                return off

        def view(off, n, dt):
            if dt == BF16:
                return arena[:, off // 2: off // 2 + n]
            return arena[:, off // 2: off // 2 + 2 * n].bitcast(dt)

        TOT = ARENA_KB * 1024
        RC = Region(0, 22528)
        RM = Region(172000 - 32, TOT)

        def bank(i, dt=F32):
            t = pbanks[i // 2][:, (i % 2) * 512:(i % 2) * 512 + 512]
            return t if dt == F32 else t.bitcast(dt)

        PB = [Buf("psum%d" % i) for i in range(8)]

        ident = view(RC.alloc(256), 128, BF16)
        bandmask = view(RC.alloc(512), 256, BF16)
        onesm = view(RC.alloc(256), 128, BF16)
        valcols = view(RC.alloc(NVT * 4), NVT, F32)
        invf = view(RC.alloc(32), 8, F32)
        hg = view(RC.alloc(1024), 256, F32)
        cs_tab = view(RC.alloc(NT * 8 * 4), NT * 8, F32)
        sn_tab = view(RC.alloc(NT * 8 * 4), NT * 8, F32)
        cw = view(RC.alloc(62 * 4), 62, F32)
        cp = view(RC.alloc(32), 6, F32)
        fw = view(RC.alloc(132 * 4), 132, F32)
        fb = view(RC.alloc(44 * 4), 44, F32)
        grep = view(RC.alloc(4096), 1024, F32)
        stat = view(RC.alloc(128 * 4), 128, F32)
        epsc = view(RC.alloc(32), 1, F32)
        onesb = view(RC.alloc(256), 128, BF16)
        hv = view(RC.alloc(32), 2, F32)
        junk = view(RC.alloc(2048), 1024, BF16)
        hnb = [view(RC.alloc(2048), 1024, BF16) for _ in range(3)]
        B_const = Buf("const")
        B_grep = Buf("grep")
        B_tab = Buf("tab")

        mixedT = view(RM.alloc(8 * NQ * 2), 8 * NQ, BF16).rearrange("p (c q) -> p c q", q=NQ)
        B_mixed = Buf("mixedT")

        const_ops = []
        for k_, (dst, src) in enumerate([(ident, ident_d), (bandmask, bandmask_d)]):
            const_ops.append(P.add("pool", lambda e, dst=dst, src=src: e.dma_start(out=dst, in_=src), writes=[Buf("c")], dsem=P.dsem("cp%d" % k_)))
        for k_, (dst, src) in enumerate([(valcols, valcols_d), (invf, invf_d), (cw, cw_d), (cp, cp_d), (fw, fw_d), (fb, fb_d), (hv, hv_d)]):
            const_ops.append(P.add("sp", lambda e, dst=dst, src=src: e.dma_start(out=dst, in_=src), writes=[Buf("c")], dsem=P.dsem("cs%d" % k_)))
        const_ops.append(P.add("sp", lambda e: e.dma_start(out=hg, in_=hg_d[0:1, :].to_broadcast([128, 256])), writes=[Buf("c")], dsem=P.dsem("chg")))
        P.add("sp", lambda e: e.dma_start(out=grep, in_=g1_d[0:1, :].to_broadcast([128, 1024])), writes=[B_grep], dsem=P.dsem("cg1"))
        P.guard(writes=[B_const], extra_deps=const_ops)
        P.add("pool", lambda e: e.memset(onesm, 1.0 / 256.0), writes=[B_const])
        P.add("pool", lambda e: e.memset(epsc, EPS), writes=[B_const])
        P.add("pool", lambda e: e.memset(onesb, 1.0), writes=[B_const])

        R1 = Region(22528, TOT)
        KT = view(R1.alloc(4 * NWIN * 2), 4 * NWIN, BF16).rearrange("p (c n) -> p c n", n=NWIN)
        VT = view(R1.alloc(4 * NWIN * 2), 4 * NWIN, BF16).rearrange("p (c n) -> p c n", n=NWIN)
        QT = view(R1.alloc(4 * NQC * 2), 4 * NQC, BF16).rearrange("p (c n) -> p c n", n=NQC)
        QmT = view(R1.alloc(2 * NQC * 2), 2 * NQC, BF16).rearrange("p (c n) -> p c n", n=NQC)
        cT = view(R1.alloc(2 * NQC * 2), 2 * NQC, BF16).rearrange("p (c n) -> p c n", n=NQC)
        KmT = view(R1.alloc(2 * 256 * 2), 512, BF16).rearrange("p (c n) -> p c n", n=256)
        VmA = view(R1.alloc(2 * 512 * 2), 1024, BF16).rearrange("p (t n) -> p t n", n=512)
        B_KmT = Buf("KmT"); B_VmA = Buf("VmA")
        p1_mark = R1.pos
        assert p1_mark == 128000, p1_mark
        w_in = view(R1.alloc(8 * 2304 * 2), 8 * 2304, BF16).rearrange("p (c n) -> p c n", n=2304)
        RW = Region(22528 + 2 * 33792, 22528 + 2 * 33792 + 17408)
        memT = view(RW.alloc(8 * 128 * 2), 1024, BF16).rearrange("p (c n) -> p c n", n=128)
        w_mkv = view(RW.alloc(8 * 512 * 2), 4096, BF16).rearrange("p (c n) -> p c n", n=512)
        gmrep = view(RW.alloc(4096), 1024, F32)
        B_memT = Buf("memT"); B_wmkv = Buf("wmkv"); B_gm = Buf("gmrep")
        hnT = [view(R1.alloc(8 * 256 * 2), 8 * 256, BF16).rearrange("p (c n) -> p c n", n=256) for _ in range(2)]
        xt = [view(R1.alloc(4096), 1024, F32) for _ in range(2)]
        class HN:
            pass
        def mk_ctx(name, W, rotary, scol):
            c = HN()
            c.name = name
            c.stg = view(R1.alloc(2048), 512, F32)
            c.qb = view(R1.alloc(1024), 512, BF16)
            c.rot = view(R1.alloc(1024), 256, F32) if rotary else None
            c.B_stg = Buf(name + "_stg"); c.B_qb = Buf(name + "_qb"); c.B_rot = Buf(name + "_rot"); c.B_st = Buf(name + "_st")
            c.ssq = stat[:, scol:scol + 8]
            c.rs = stat[:, scol + 8:scol + 16]
            return c
        ctx_k = [mk_ctx("k0", 512, True, 8), mk_ctx("k1", 512, True, 24)]
        ctx_qs = [mk_ctx("q0", 512, True, 40), mk_ctx("q1", 512, True, 72)]
        ctx_qms = [mk_ctx("qm0", 512, False, 56), mk_ctx("qm1", 512, False, 88)]
        ctx_q = ctx_qs[0]; ctx_qm = ctx_qms[0]
        ALL_CTX = ctx_k + ctx_qs + ctx_qms
        print("R1 end before misc", R1.pos, TOT)
        sig = view(R1.alloc(1024), 256, F32)
        posi = view(R1.alloc(NT * 4), NT, I32)
        posf = view(R1.alloc(NT * 4), NT, F32)
        ang = view(R1.alloc(NT * 8 * 4), NT * 8, F32)
        angi = view(R1.alloc(NT * 8 * 4), NT * 8, I32)
        angf = view(R1.alloc(NT * 8 * 4), NT * 8, F32)

        B_wins = [Buf("w_in%d" % i) for i in range(8)]; B_win = B_wins[0]; B_KT = Buf("KT"); B_VT = Buf("VT"); B_QT = Buf("QT"); B_QmT = Buf("QmT"); B_cT = Buf("cT")
        B_hnT = [[Buf("hnT00"), Buf("hnT01")], [Buf("hnT10"), Buf("hnT11")]]; B_xt = [Buf("xt0"), Buf("xt1")]; B_hnb = [Buf("hnb0"), Buf("hnb1"), Buf("hnb2")]
        B_sig = Buf("sig"); B_rst = [Buf("rst0"), Buf("rst1"), Buf("rst2")]; B_pos = Buf("pos")


        dpos = P.dsem("pos")
        P.add("sp", lambda e: e.dma_start(out=posi, in_=pos_win), writes=[B_pos], dsem=dpos)
        P.add("dve", lambda e: e.tensor_copy(out=posf, in_=posi), reads=[B_pos], writes=[B_pos])
        ang3 = ang.rearrange("p (t f) -> p t f", f=8)
        P.add("dve", lambda e: e.tensor_tensor(out=ang3, in0=posf.unsqueeze(2).to_broadcast([128, NT, 8]),
                                               in1=invf.unsqueeze(1).to_broadcast([128, NT, 8]), op=ALU.mult),
              reads=[B_pos, B_const], writes=[B_pos])
        for (tab, shift) in [(sn_tab, 0.0), (cs_tab, 0.25)]:
            P.add("dve", lambda e, shift=shift: e.tensor_scalar(out=angf, in0=ang, scalar1=1.0 / (2 * np.pi), scalar2=shift,
                                                                op0=ALU.mult, op1=ALU.add), reads=[B_pos], writes=[B_tab])
            P.add("dve", lambda e: e.tensor_copy(out=angi, in_=angf), reads=[B_tab], writes=[B_tab])
            P.add("dve", lambda e, tab=tab: e.tensor_copy(out=tab, in_=angi), reads=[B_tab], writes=[B_tab])
            P.add("dve", lambda e, tab=tab: e.tensor_tensor(out=tab, in0=angf, in1=tab, op=ALU.subtract), reads=[B_tab], writes=[B_tab])
            P.add("act", lambda e, tab=tab: e.activation(out=tab, in_=tab, func=AF.Sin, scale=6.283185005187988), reads=[B_tab], writes=[B_tab])

        def rms_stats_hn(src_ap, n, B_src, slot, grep_ap, B_grep_):
            ss = stat[0:n, 2 * slot:2 * slot + 1]
            rs = stat[0:n, 2 * slot + 1:2 * slot + 2]
            P.add("act", lambda e: e.activation(out=junk[0:n, :], in_=src_ap, func=AF.Square, accum_out=ss), reads=[B_src], writes=[B_rst[slot]])
            P.add("act", lambda e: e.activation(out=rs, in_=ss, func=AF.Sqrt, scale=1.0 / 1024, bias=epsc[0:n, :]), reads=[B_rst[slot], B_const], writes=[B_rst[slot]])
            P.add("dve", lambda e: e.reciprocal(out=rs, in_=rs), reads=[B_rst[slot]], writes=[B_rst[slot]])
            hb = hnb[slot]
            P.add("dve", lambda e: e.scalar_tensor_tensor(out=hb[0:n, :], in0=src_ap, scalar=rs, in1=grep_ap[0:n, :], op0=ALU.mult, op1=ALU.mult),
                  reads=[B_src, B_rst[slot], B_grep_], writes=[B_hnb[slot]])

        def hn_transpose(n, slot, dst_ap, B_dsts, tp_bank):
            hb = hnb[slot]
            tp = bank(tp_bank, BF16).rearrange("p (c n) -> p c n", n=128)
            for c in range(8):
                P.add("pe", lambda e, c=c: e.transpose(out=tp[:, c, 0:n], in_=hb[0:n, c * 128:(c + 1) * 128], identity=ident[0:n, 0:n]),
                      reads=[B_hnb[slot], B_const], writes=[PB[tp_bank]])
            P.add("act", lambda e: e.activation(out=dst_ap, in_=tp[:, :, 0:n], func=AF.Copy), reads=[PB[tp_bank]], writes=B_dsts)

        def rmsnorm_transpose(src_ap, n, B_src, slot, dstT, B_dst, col_ap_fn, tp_bank, grep=grep, B_grep=B_grep):
            rms_stats_hn(src_ap, n, B_src, slot, grep, B_grep)
            hn_transpose(n, slot, col_ap_fn(dstT), [B_dst], tp_bank)

        def headnorm_staged(items, stage_lo, stage_hi):
            n = 128
            for stg_i in range(stage_lo, stage_hi):
                for (c, nheads, gain_idx, T) in items:
                    W = nheads * 64
                    s3 = c.stg[:, 0:W].rearrange("p (h d) -> p h d", d=64)
                    o3 = c.qb[:, 0:W].rearrange("p (h d) -> p h d", d=64)
                    ssq = c.ssq[:, 0:nheads]; rs = c.rs[:, 0:nheads]
                    g3 = hg[:, gain_idx * 64:(gain_idx + 1) * 64].unsqueeze(1).to_broadcast([n, nheads, 64])
                    if stg_i == 1:
                        P.add("act", lambda e, c=c, W=W: e.activation(out=c.qb[:, 0:W], in_=c.stg[:, 0:W], func=AF.Square), reads=[c.B_stg], writes=[c.B_qb])
                    elif stg_i == 2:
                        P.add("dve", lambda e, o3=o3, ssq=ssq: e.tensor_reduce(out=ssq, in_=o3, axis=AX.X, op=ALU.add), reads=[c.B_qb], writes=[c.B_st])
                    elif stg_i == 3:
                        P.add("act", lambda e, ssq=ssq, rs=rs: e.activation(out=rs, in_=ssq, func=AF.Sqrt, scale=1.0 / 64, bias=epsc), reads=[c.B_st, B_const], writes=[c.B_st])
                    elif stg_i == 4:
                        P.add("dve", lambda e, rs=rs: e.reciprocal(out=rs, in_=rs), reads=[c.B_st], writes=[c.B_st])
                    elif stg_i == 5:
                        P.add("dve", lambda e, s3=s3, rs=rs, nheads=nheads: e.tensor_tensor(out=s3, in0=s3, in1=rs.unsqueeze(2).to_broadcast([n, nheads, 64]), op=ALU.mult),
                              reads=[c.B_stg, c.B_st], writes=[c.B_stg])
                    elif stg_i == 6:
                        if T is None:
                            P.add("dve", lambda e, s3=s3, o3=o3, g3=g3: e.tensor_tensor(out=o3, in0=s3, in1=g3, op=ALU.mult), reads=[c.B_stg, B_const, c.B_qb], writes=[c.B_qb])
                        else:
                            P.add("dve", lambda e, s3=s3, g3=g3: e.tensor_tensor(out=s3, in0=s3, in1=g3, op=ALU.mult), reads=[c.B_stg, B_const], writes=[c.B_stg])
                    elif stg_i == 7 and T is not None:
                        cosb = cs_tab[:, T * 8:T * 8 + 8].unsqueeze(1).to_broadcast([n, nheads, 8])
                        sinb = sn_tab[:, T * 8:T * 8 + 8].unsqueeze(1).to_broadcast([n, nheads, 8])
                        x1 = s3[:, :, 0:8]
                        x2 = s3[:, :, 8:16]
                        r4 = c.rot[:, :].rearrange("p (k h f) -> p k h f", k=4, f=8)
                        P.add("act", lambda e, o3=o3, s3=s3: e.activation(out=o3[:, :, 16:64], in_=s3[:, :, 16:64], func=AF.Copy), reads=[c.B_stg, c.B_qb], writes=[c.B_qb])
                        P.add("pool", lambda e, r4=r4, x1=x1, cosb=cosb: e.tensor_tensor(out=r4[:, 0], in0=x1, in1=cosb, op=ALU.mult), reads=[c.B_stg, B_tab], writes=[c.B_rot])
                        P.add("pool", lambda e, r4=r4, x2=x2, sinb=sinb: e.tensor_tensor(out=r4[:, 1], in0=x2, in1=sinb, op=ALU.mult), reads=[c.B_stg, B_tab], writes=[c.B_rot])
                        P.add("pool", lambda e, r4=r4, x2=x2, cosb=cosb: e.tensor_tensor(out=r4[:, 2], in0=x2, in1=cosb, op=ALU.mult), reads=[c.B_stg, B_tab], writes=[c.B_rot])
                        P.add("pool", lambda e, r4=r4, x1=x1, sinb=sinb: e.tensor_tensor(out=r4[:, 3], in0=x1, in1=sinb, op=ALU.mult), reads=[c.B_stg, B_tab], writes=[c.B_rot])
                        P.add("pool", lambda e, r4=r4, o3=o3: e.tensor_tensor(out=o3[:, :, 0:8], in0=r4[:, 0], in1=r4[:, 1], op=ALU.subtract), reads=[c.B_rot, c.B_qb], writes=[c.B_qb])
                        P.add("pool", lambda e, r4=r4, o3=o3: e.tensor_tensor(out=o3[:, :, 8:16], in0=r4[:, 2], in1=r4[:, 3], op=ALU.add), reads=[c.B_rot, c.B_qb], writes=[c.B_qb])

        def transpose_to(srcb, B_srcb, n, nblk, dstT, B_dst, col0, tq_bank):
            tq = bank(tq_bank, BF16).rearrange("p (c n) -> p c n", n=128)
            for j in range(nblk):
                P.add("pe", lambda e, j=j: e.transpose(out=tq[:, j, 0:n], in_=srcb[0:n, j * 128:(j + 1) * 128], identity=ident[0:n, 0:n]),
                      reads=[B_srcb, B_const], writes=[PB[tq_bank]])
            P.add("dve", lambda e: e.tensor_copy(out=dstT[:, 0:nblk, col0:col0 + n], in_=tq[:, 0:nblk, 0:n]),
                  reads=[PB[tq_bank]], writes=[B_dst])


        dmem = P.dsem("mem")
        wm_src = w_mkv_d.rearrange("(c p) n -> p c n", p=128)
        P.add("pool", lambda e: e.dma_start(out=w_mkv, in_=wm_src), writes=[B_wmkv], dsem=dmem)
        dws = [P.dsem("w_in%d" % i) for i in range(8)]
        w_in_src = w_in_d.rearrange("(c p) n -> p c n", p=128)
        for c in range(8):
            P.add("pool", lambda e, c=c: e.dma_start(out=w_in[:, c, :], in_=w_in_src[:, c, :]), writes=[B_wins[c]], dsem=dws[c])

        dx = [P.dsem("x0"), P.dsem("x1")]
        P.add("sp", lambda e: e.dma_start(out=gmrep, in_=gm_d[0:1, :].to_broadcast([128, 1024])), writes=[B_gm], dsem=P.dsem("gm"))
        for mt in range(2):
            P.add("sp", lambda e, mt=mt: e.dma_start(out=xt[mt], in_=mem_d[128 * mt:128 * mt + 128, :]), writes=[B_xt[mt]], dsem=dx[mt])
            rmsnorm_transpose(xt[mt], 128, B_xt[mt], mt, memT, B_memT, lambda d: d[:, :, 0:128], 4, grep=gmrep, B_grep=B_gm)
            pb = bank(3)
            cm = ctx_qm
            for c in range(8):
                P.add("pe", lambda e, c=c, pb=pb: e.matmul(pb[:, 0:512], lhsT=memT[:, c, 0:128], rhs=w_mkv[:, c, :], start=(c == 0), stop=(c == 7)),
                      reads=[B_memT, B_wmkv], writes=[PB[3]])
            P.add("act", lambda e, pb=pb: e.activation(out=cm.stg[:, 0:512], in_=pb[:, 0:512], func=AF.Copy), reads=[PB[3]], writes=[cm.B_stg])
            vsrc = cm.stg[:, 256:512].rearrange("p (hp two d) -> p hp two d", two=2, d=64)
            vdst = VmA[:, mt, :].rearrange("p (hp s d) -> p hp s d", s=4, d=64)
            P.add("dve", lambda e, vdst=vdst, vsrc=vsrc: e.tensor_copy(out=vdst[:, :, 0, :], in_=vsrc[:, :, 0, :]), reads=[cm.B_stg], writes=[B_VmA])
            P.add("dve", lambda e, vdst=vdst, vsrc=vsrc: e.tensor_copy(out=vdst[:, :, 3, :], in_=vsrc[:, :, 1, :]), reads=[cm.B_stg], writes=[B_VmA])
            P.add("pool", lambda e, vdst=vdst: e.memset(vdst[:, :, 1:3, :], 1.0), writes=[B_VmA])
            headnorm_staged([(cm, 4, 3, None)], 1, 8)
            transpose_to(cm.qb, cm.B_qb, 128, 2, KmT, B_KmT, 128 * mt, 5)

        fm_rot = [0]
        FM_BANKS = [6, 7]

        def hslot(T):
            return ((T // 2) % 2, T % 2)

        def A1(T):
            xs = T % 2
            P.add("sp", lambda e, T=T, xs=xs: e.dma_start(out=xt[xs], in_=x_win[128 * T:128 * T + 128, :]), writes=[B_xt[xs]], dsem=dx[xs])
            rms_stats_hn(xt[xs], 128, B_xt[xs], xs, grep, B_grep)

        def A2(T):
            hs, part = hslot(T)
            hn_transpose(128, T % 2, hnT[hs][:, :, 128 * part:128 * part + 128], [B_hnT[hs][part]], 4)

        def tile_items(T):
            items = [(ctx_k[T % 2], 8, 1, T)]
            if 8 <= T <= 24:
                items += [(ctx_qs[T % 2], 8, 0, T), (ctx_qms[T % 2], 4, 2, None)]
            return items

        def Bst(T):
            hs, part = hslot(T)
            lc = 128 * part
            jobs = [(0, 512, 1024, 512, ctx_k[T % 2])]
            if 8 <= T <= 24:
                jobs += [(2, 0, 512, 512, ctx_qs[T % 2]), (3, 2048, 2304, 256, ctx_qms[T % 2])]
            for (bi, c0, c1, W, cx) in jobs:
                pb = bank(bi)
                for c in range(8):
                    P.add("pe", lambda e, c=c, pb=pb, c0=c0, c1=c1, W=W, lc=lc, hs=hs: e.matmul(pb[:, 0:W], lhsT=hnT[hs][:, c, lc:lc + 128], rhs=w_in[:, c, c0:c1],
                                                                                                start=(c == 0), stop=(c == 7)),
                          reads=[B_hnT[hs][part], B_wins[c]], writes=[PB[bi]])
                P.add("act", lambda e, pb=pb, cx=cx, W=W: e.activation(out=cx.stg[:, 0:W], in_=pb[:, 0:W], func=AF.Copy), reads=[PB[bi]], writes=[cx.B_stg])

        def Dk(T):
            transpose_to(ctx_k[T % 2].qb, ctx_k[T % 2].B_qb, 128, 4, KT, B_KT, 128 * T, 5)

        def Dq(T):
            if T == 8:
                P.guard(writes=[B_QT, B_wmkv, B_gm, B_memT])
            if 8 <= T <= 24:
                tq = bank(1, BF16).rearrange("p (c n) -> p c n", n=128)
                for j in range(4):
                    P.add("pe", lambda e, j=j: e.transpose(out=tq[:, j, :], in_=ctx_qs[T % 2].qb[:, j * 128:(j + 1) * 128], identity=ident),
                          reads=[ctx_qs[T % 2].B_qb, B_const], writes=[PB[1]])
                for j in range(2):
                    P.add("pe", lambda e, j=j: e.transpose(out=tq[:, 4 + j, :], in_=ctx_qms[T % 2].qb[:, j * 128:(j + 1) * 128], identity=ident),
                          reads=[ctx_qms[T % 2].B_qb, B_const], writes=[PB[1]])
                col0 = 128 * T - QC0
                P.add("dve", lambda e: e.tensor_copy(out=QT[:, 0:4, col0:col0 + 128], in_=tq[:, 0:4, :]), reads=[PB[1]], writes=[B_QT])
                P.add("dve", lambda e: e.tensor_copy(out=QmT[:, 0:2, col0:col0 + 128], in_=tq[:, 4:6, :]), reads=[PB[1]], writes=[B_QmT])

        def Fst(ch):
            tiles = list(range(2 * ch, min(2 * ch + 2, NT)))
            hs = ch % 2
            ntok = 128 * len(tiles)
            rd0 = [B_hnT[hs][t % 2] for t in tiles]
            u0 = 256 * ch
            for fp in range(2):
                bi = FM_BANKS[fm_rot[0] % 2]; fm_rot[0] += 1
                pb = bank(bi)
                for sub in range(2):
                    ft = 2 * fp + sub
                    for c in range(8):
                        P.add("pe", lambda e, c=c, pb=pb, ft=ft, sub=sub: e.matmul(pb[:, 256 * sub:256 * sub + ntok], lhsT=w_in[:, c, 1024 + 128 * ft:1024 + 128 * ft + 128],
                                                                                   rhs=hnT[hs][:, c, 0:ntok], start=(c == 0 and sub == 0), stop=(c == 7), skip_group_check=True),
                              reads=rd0 + [B_wins[c]], writes=[PB[bi]])
                pv = pb.rearrange("p (s n) -> p s n", n=256)
                P.add("dve", lambda e, pv=pv, fp=fp: e.tensor_copy(out=VT[:, 2 * fp:2 * fp + 2, u0:u0 + ntok], in_=pv[:, :, 0:ntok]), reads=[PB[bi]], writes=[B_VT])
            if QC0 <= u0 < QC0 + NQC:
                ng = min(256, QC0 + NQC - u0)
                for ft in range(2):
                    bi = FM_BANKS[fm_rot[0] % 2]; fm_rot[0] += 1
                    pb = bank(bi)
                    for (sub, cbase) in [(0, 1536), (1, 1792)]:
                        for c in range(8):
                            P.add("pe", lambda e, c=c, pb=pb, cbase=cbase, ft=ft, sub=sub: e.matmul(pb[:, 256 * sub:256 * sub + ng], lhsT=w_in[:, c, cbase + 128 * ft:cbase + 128 * ft + 128],
                                                                                                     rhs=hnT[hs][:, c, 0:ng], start=(c == 0 and sub == 0), stop=(c == 7), skip_group_check=True),
                                  reads=rd0 + [B_wins[c]], writes=[PB[bi]])
                    P.add("act", lambda e, pb=pb: e.activation(out=sig[:, 0:ng], in_=pb[:, 256:256 + ng], func=AF.Sigmoid), reads=[PB[bi]], writes=[B_sig])
                    P.add("dve", lambda e, pb=pb, ft=ft: e.tensor_tensor(out=cT[:, ft, u0 - QC0:u0 - QC0 + ng], in0=pb[:, 0:ng], in1=sig[:, 0:ng], op=ALU.mult),
                          reads=[PB[bi], B_sig], writes=[B_cT])

        A1(0); A2(0); A1(1)
        for T in range(NT):
            if T + 1 < NT:
                A2(T + 1)
            if T + 2 < NT:
                A1(T + 2)
            Bst(T)
            items = tile_items(T)
            if T % 2 == 1 or T == NT - 1:
                Fst(T // 2)
            headnorm_staged(items, 1, 8)
            if T >= 1:
                Dq(T - 1)
                Dk(T - 1)
        Dq(NT - 1)
        Dk(NT - 1)

        R2 = Region(128000, 172000 - 32)
        acc = view(R2.alloc(2 * NQ * 4), 2 * NQ, F32).rearrange("p (h q) -> p h q", q=NQ)
        rden = view(R2.alloc(NQ * 4), NQ, F32)
        Pt = [view(R2.alloc(1024), 512, BF16).rearrange("p (h n) -> p h n", n=256) for _ in range(3)]
        Vts = [view(R2.alloc(512), 256, BF16) for _ in range(4)]
        B_accs = [Buf("acc%d" % i) for i in range(5)]; B_rdens = [Buf("rden%d" % i) for i in range(5)]
        B_acc = B_accs[0]; B_rden = B_rdens[0]
        B_Pt = [Buf("Pt0"), Buf("Pt1"), Buf("Pt2")]; B_Vts = [Buf("Vt%d" % i) for i in range(4)]; B_Vvs = [Buf("Vv%d" % i) for i in range(4)]

        def acc_bufs(isl):
            return [B_accs[k] for k in range(isl.start // 512, (isl.stop - 1) // 512 + 1)]
        P1_TEMPS = B_wins + B_hnT[0] + B_hnT[1] + B_xt + [B_sig, B_pos, B_tab]
        for c_ in ALL_CTX:
            P1_TEMPS += [c_.B_stg, c_.B_qb, c_.B_rot, c_.B_st]
        P.guard(writes=P1_TEMPS + B_accs + B_rdens + [B_mixed] + B_Pt + B_Vts + B_Vvs)

        for vs_ in range(4):
            P.add("pool", lambda e, vs_=vs_: e.memset(Vts[vs_][:, 64:192], 1.0), writes=[B_Vvs[vs_]])
        for hp in range(4):
            kts = []
            for (D, rho, sq0, nq, nqb, tiles) in PT:
                qbs = [min(128, nq - 128 * m) for m in range(nqb)]
                for j, (a, nk, idx) in enumerate(tiles):
                    kts.append((D, rho, sq0, nqb, qbs, j, a, nk, idx))
            NK = len(kts)

            def ksl_of(i):
                (D, rho, sq0, nqb, qbs, j, a, nk, idx) = kts[i]
                ku0 = rho + D * a
                return slice(ku0, ku0 + D * (nk - 1) + 1, D)

            def stV(i):
                (D, rho, sq0, nqb, qbs, j, a, nk, idx) = kts[i]
                vs = i % 4
                ksl = ksl_of(i)
                vt = bank(6, BF16)[:, 0:128]
                P.add("pe", lambda e, hp=hp: e.transpose(out=vt[0:nk, :], in_=VT[:, hp, ksl], identity=ident), reads=[B_VT, B_const], writes=[PB[6]])
                V4 = Vts[vs].rearrange("p (s d) -> p s d", d=64)
                P.add("act", lambda e: e.activation(out=V4[0:nk, 0:4:3, :], in_=vt[0:nk, :].rearrange("p (s d) -> p s d", d=64), func=AF.Copy),
                      reads=[PB[6]], writes=[B_Vts[vs]])

            def cols_of(i):
                (D, rho, sq0, nqb, qbs, j, a, nk, idx) = kts[i]
                c0 = 0 if j >= 1 else 128
                c1 = (128 + qbs[j]) if j < nqb else qbs[j - 1]
                return c0, c1

            def stS(i):
                (D, rho, sq0, nqb, qbs, j, a, nk, idx) = kts[i]
                si = i % 2
                pi_ = i % 3
                ksl = ksl_of(i)
                c0, c1 = cols_of(i)
                nc_ = c1 - c0
                qu0 = rho + D * (a - 64 + c0) - QC0
                qsl = slice(qu0, qu0 + D * (nc_ - 1) + 1, D)
                S2 = pbanks[si]
                for h in range(2):
                    P.add("pe", lambda e, h=h, hp=hp: e.matmul(S2[0:nk, h * 512 + c0:h * 512 + c1], lhsT=KT[h * 64:(h + 1) * 64, hp, ksl],
                                                        rhs=QT[h * 64:(h + 1) * 64, hp, qsl], start=True, stop=False, skip_group_check=True),
                          reads=[B_KT, B_QT], writes=[PB[2 * si + h]])
                for h in range(2):
                    P.add("pe", lambda e, h=h: e.matmul(S2[0:nk, h * 512 + c0:h * 512 + c1], lhsT=ident[0:nk, 0:nk], rhs=bandmask[0:nk, c0:c1],
                                                        start=False, stop=True, skip_group_check=True),
                          reads=[B_const], writes=[PB[2 * si + h]])
                S3 = S2.rearrange("p (h n) -> p h n", n=512)
                P.add("act", lambda e: e.activation(out=Pt[pi_][0:nk, :, c0:c1], in_=S3[0:nk, :, c0:c1], func=AF.Exp, scale=0.125,
                                                    bias=valcols[0:nk, idx:idx + 1]),
                      reads=[PB[2 * si], PB[2 * si + 1], B_const], writes=[B_Pt[pi_]])

            def stPV(i):
                (D, rho, sq0, nqb, qbs, j, a, nk, idx) = kts[i]
                pi_ = i % 3
                vs = i % 4
                halves = []
                if j >= 1:
                    halves.append((j - 1, 0, qbs[j - 1]))
                if j < nqb:
                    halves.append((j, 128, qbs[j]))
                for (m, cl, nqm) in halves:
                    ob = 4 + (m % 2)
                    O3 = bank(ob).rearrange("p (h n) -> p h n", n=256)
                    for h in range(2):
                        first = (m == j) and h == 0
                        P.add("pe", lambda e, O3=O3, h=h, nqm=nqm, cl=cl, first=first: e.matmul(
                            O3[:, h, 0:nqm], lhsT=Vts[vs][0:nk, h * 128:(h + 1) * 128], rhs=Pt[pi_][0:nk, h, cl:cl + nqm],
                            start=first, stop=False, skip_group_check=True),
                            reads=[B_Vts[vs], B_Vvs[vs], B_Pt[pi_]], writes=[PB[ob]])
                    if m == j - 1:
                        qi0 = rho + D * (sq0 + 128 * m) - UQ0
                        isl = slice(qi0, qi0 + D * (nqm - 1) + 1, D)
                        if D == 1:
                            P.add("dve", lambda e, O3=O3, isl=isl, nqm=nqm: e.tensor_copy(out=acc[:, :, isl], in_=O3[:, :, 0:nqm]), reads=[PB[ob]], writes=acc_bufs(isl))
                        else:
                            P.add("dve", lambda e, O3=O3, isl=isl, nqm=nqm: e.tensor_tensor(out=acc[:, :, isl], in0=O3[:, :, 0:nqm], in1=acc[:, :, isl], op=ALU.add),
                                  reads=[PB[ob]] + acc_bufs(isl), writes=acc_bufs(isl))

            stV(0); stV(1); stV(2); stS(0); stS(1)
            for i in range(NK):
                if i + 3 < NK:
                    stV(i + 3)
                if i + 2 < NK:
                    stS(i + 2)
                stPV(i)
            if hp == 3:
                R3 = Region(22528, 107520)
                w_out = view(R3.alloc(8 * 1024 * 2), 8192, BF16).rearrange("p (c n) -> p c n", n=1024)
                Dg = view(R3.alloc(62 * 128 * 2), 62 * 128, BF16).rearrange("p (k n) -> p k n", n=128)
                yf = [view(R3.alloc(2 * 512 * 4), 1024, F32).rearrange("p (c n) -> p c n", n=512) for _ in range(3)]
                ybf = [view(R3.alloc(2 * 512 * 2), 1024, BF16).rearrange("p (c n) -> p c n", n=512) for _ in range(3)]
                ysq = [view(R3.alloc(2 * 512 * 2), 1024, BF16).rearrange("p (c n) -> p c n", n=512) for _ in range(3)]
                mean_sb = [view(R3.alloc(2048), 512, F32) for _ in range(2)]
                var_sb = [view(R3.alloc(2048), 512, F32) for _ in range(2)]
                Pm = [view(R3.alloc(2 * 512 * 2), 1024, BF16).rearrange("p (h n) -> p h n", n=512) for _ in range(2)]
                rdm = [view(R3.alloc(2048), 512, F32) for _ in range(2)]
                osb = [view(R3.alloc(2048), 512, F32) for _ in range(2)]
                B_osb = [Buf("osb0"), Buf("osb1")]
                B_wout = Buf("wout"); B_Dg = Buf("Dg")
                B_yf = [[Buf("yf%d%d" % (i, j)) for j in range(2)] for i in range(3)]; B_ybf = [Buf("ybf0"), Buf("ybf1"), Buf("ybf2")]; B_ysq = [Buf("ysq0"), Buf("ysq1"), Buf("ysq2")]
                B_mean = [Buf("mean0"), Buf("mean1")]; B_var = [Buf("var0"), Buf("var1")]
                B_Pm = [Buf("Pm0"), Buf("Pm1")]; B_rdm = [Buf("rdm0"), Buf("rdm1")]
                B34 = B_osb + [B_Dg] + B_yf[0] + B_yf[1] + B_yf[2] + B_ybf + B_ysq + B_mean + B_var + B_Pm + B_rdm
                P.guard(writes=[B_KT, B_VT, B_QT, B_wout] + B34)
                dw2 = P.dsem("w_out")
                wo_src = w_out_d.rearrange("(c p) n -> p c n", p=128)
                for c in range(0, 8, 2):
                    P.add("pool", lambda e, c=c: e.dma_start(out=w_out[:, c:c + 2, :], in_=wo_src[:, c:c + 2, :]), writes=[B_wout], dsem=dw2)
                P.add("dve", lambda e: e.tensor_tensor(out=Dg, in0=ident.unsqueeze(1).to_broadcast([128, 62, 128]), in1=cw.unsqueeze(2).to_broadcast([128, 62, 128]), op=ALU.mult),
                      reads=[B_const], writes=[B_Dg])

            for k in range(5):
                cs_ = slice(512 * k, min(512 * k + 512, NQ))
                P.add("dve", lambda e, cs_=cs_: e.reciprocal(out=rden[0:64, cs_], in_=acc[64:128, 0, cs_]), reads=[B_accs[k]], writes=[B_rdens[k]])
                P.add("dve", lambda e, cs_=cs_: e.reciprocal(out=rden[64:128, cs_], in_=acc[0:64, 1, cs_]), reads=[B_accs[k]], writes=[B_rdens[k]])
                P.add("pool", lambda e, hp=hp, cs_=cs_: e.tensor_tensor(out=mixedT[0:64, hp, cs_], in0=acc[0:64, 0, cs_], in1=rden[0:64, cs_], op=ALU.mult),
                      reads=[B_accs[k], B_rdens[k]], writes=[B_mixed])
                P.add("pool", lambda e, hp=hp, cs_=cs_: e.tensor_tensor(out=mixedT[64:128, hp, cs_], in0=acc[64:128, 1, cs_], in1=rden[64:128, cs_], op=ALU.mult),
                      reads=[B_accs[k], B_rdens[k]], writes=[B_mixed])

        QBLK = [(0, 512), (512, 512), (1024, 512), (1536, 512), (2048, 2)]

        def conv_a(b, chns=(0, 1)):
            (qi0, n) = QBLK[b]
            s_ = b % 3
            for chn in chns:
                pb = bank(6 + chn)
                for k in range(31):
                    P.add("pe", lambda e, pb=pb, chn=chn, k=k: e.matmul(pb[:, 0:n], lhsT=Dg[:, chn * 31 + k, :], rhs=cT[:, chn, qi0 + 48 + k:qi0 + 48 + k + n],
                                                                        start=(k == 0), stop=(k == 30)),
                          reads=[B_Dg, B_cT], writes=[PB[6 + chn]])
                P.add("act", lambda e, pb=pb, chn=chn: e.activation(out=yf[s_][:, chn, 0:n], in_=pb[:, 0:n], func=AF.Identity, bias=cp[:, chn:chn + 1]),
                      reads=[PB[6 + chn], B_const], writes=[B_yf[s_][chn]])
                P.add("pool", lambda e, chn=chn: e.tensor_copy(out=ybf[s_][:, chn, 0:n], in_=yf[s_][:, chn, 0:n]), reads=[B_yf[s_][chn]], writes=[B_ybf[s_]])
                P.add("act", lambda e, chn=chn: e.activation(out=ysq[s_][:, chn, 0:n], in_=yf[s_][:, chn, 0:n], func=AF.Square), reads=[B_yf[s_][chn]], writes=[B_ysq[s_]])

        def conv_b(b):
            (qi0, n) = QBLK[b]
            s_ = b % 3
            m_ = b % 2
            for chn in range(2):
                P.add("pe", lambda e, chn=chn: e.matmul(bank(6)[:, 0:n], lhsT=onesm, rhs=ybf[s_][:, chn, 0:n], start=(chn == 0), stop=(chn == 1)),
                      reads=[B_ybf[s_], B_const], writes=[PB[6]])
            for chn in range(2):
                P.add("pe", lambda e, chn=chn: e.matmul(bank(7)[:, 0:n], lhsT=onesm, rhs=ysq[s_][:, chn, 0:n], start=(chn == 0), stop=(chn == 1)),
                      reads=[B_ysq[s_], B_const], writes=[PB[7]])
            mean = mean_sb[m_]; var = var_sb[m_]
            P.add("act", lambda e: e.activation(out=mean[:, 0:n], in_=bank(6)[:, 0:n], func=AF.Copy), reads=[PB[6]], writes=[B_mean[m_]])
            P.add("dve", lambda e: e.tensor_tensor(out=var[:, 0:n], in0=mean[:, 0:n], in1=mean[:, 0:n], op=ALU.mult), reads=[B_mean[m_]], writes=[B_var[m_]])
            P.add("dve", lambda e: e.tensor_tensor(out=var[:, 0:n], in0=bank(7)[:, 0:n], in1=var[:, 0:n], op=ALU.subtract), reads=[PB[7], B_var[m_]], writes=[B_var[m_]])
            P.add("act", lambda e: e.activation(out=var[:, 0:n], in_=var[:, 0:n], func=AF.Sqrt, bias=epsc), reads=[B_var[m_], B_const], writes=[B_var[m_]])
            P.add("dve", lambda e: e.reciprocal(out=var[:, 0:n], in_=var[:, 0:n]), reads=[B_var[m_]], writes=[B_var[m_]])
            for chn in range(2):
                eng = "dve" if chn == 0 else "pool"
                P.add(eng, lambda e, chn=chn: e.tensor_tensor(out=yf[s_][:, chn, 0:n], in0=yf[s_][:, chn, 0:n], in1=mean[:, 0:n], op=ALU.subtract),
                      reads=[B_yf[s_][chn], B_mean[m_]], writes=[B_yf[s_][chn]])
                P.add(eng, lambda e, chn=chn: e.tensor_tensor(out=yf[s_][:, chn, 0:n], in0=yf[s_][:, chn, 0:n], in1=var[:, 0:n], op=ALU.mult),
                      reads=[B_yf[s_][chn], B_var[m_]], writes=[B_yf[s_][chn]])
                P.add("act", lambda e, chn=chn: e.activation(out=mixedT[:, 4 + chn, qi0:qi0 + n], in_=yf[s_][:, chn, 0:n], func=AF.Silu,
                                                             scale=cp[:, 2 + chn:3 + chn], bias=cp[:, 4 + chn:5 + chn]),
                      reads=[B_yf[s_][chn], B_const], writes=[B_mixed])

        def memS(hp, b):
            (qi0, n) = QBLK[b]
            for mt in range(2):
                S2 = pbanks[mt]
                for h in range(2):
                    P.add("pe", lambda e, S2=S2, h=h, mt=mt: e.matmul(
                        S2[:, h * 512:h * 512 + n], lhsT=KmT[h * 64:(h + 1) * 64, hp, 128 * mt:128 * mt + 128],
                        rhs=QmT[h * 64:(h + 1) * 64, hp, qi0 + UQ0 - QC0:qi0 + UQ0 - QC0 + n], start=True, stop=True),
                        reads=[B_KmT, B_QmT], writes=[PB[2 * mt + h]])
                S3 = S2.rearrange("p (h n) -> p h n", n=512)
                P.add("act", lambda e, S3=S3, mt=mt: e.activation(out=Pm[mt][:, :, 0:n], in_=S3[:, :, 0:n], func=AF.Exp, scale=0.125),
                      reads=[PB[2 * mt], PB[2 * mt + 1]], writes=[B_Pm[mt]])

        def memPV(hp, b):
            (qi0, n) = QBLK[b]
            for h in range(2):
                for mt in range(2):
                    P.add("pe", lambda e, h=h, mt=mt: e.matmul(bank(4 + h)[:, 0:n], lhsT=VmA[:, mt, hp * 256 + h * 128:hp * 256 + h * 128 + 128],
                                                               rhs=Pm[mt][:, h, 0:n], start=(mt == 0), stop=(mt == 1)),
                          reads=[B_VmA, B_Pm[mt]], writes=[PB[4 + h]])
            for h in range(2):
                P.add("act", lambda e, h=h: e.activation(out=osb[h][:, 0:n], in_=bank(4 + h)[:, 0:n], func=AF.Copy), reads=[PB[4 + h]], writes=[B_osb[h]])
            P.add("dve", lambda e: e.reciprocal(out=rdm[0][0:64, 0:n], in_=osb[0][64:128, 0:n]), reads=[B_osb[0]], writes=[B_rdm[0]])
            P.add("dve", lambda e: e.reciprocal(out=rdm[1][64:128, 0:n], in_=osb[1][0:64, 0:n]), reads=[B_osb[1]], writes=[B_rdm[1]])
            P.add("pool", lambda e: e.tensor_tensor(out=mixedT[0:64, 6 + hp, qi0:qi0 + n], in0=osb[0][0:64, 0:n], in1=rdm[0][0:64, 0:n], op=ALU.mult),
                  reads=[B_osb[0], B_rdm[0]], writes=[B_mixed])
            P.add("pool", lambda e: e.tensor_tensor(out=mixedT[64:128, 6 + hp, qi0:qi0 + n], in0=osb[1][64:128, 0:n], in1=rdm[1][64:128, 0:n], op=ALU.mult),
                  reads=[B_osb[1], B_rdm[1]], writes=[B_mixed])

        conv_a(0)
        conv_a(1)
        for b in range(5):
            memS(0, b)
            if b + 2 < 5:
                conv_a(b + 2, (0,))
            memPV(0, b)
            memS(1, b)
            if b + 2 < 5:
                conv_a(b + 2, (1,))
            memPV(1, b)
            conv_b(b)

        if DEBUG:
            dd = P.dsem("dbg")
            dbgf = view(128000, 8 * NQ, F32)
            P.add("dve", lambda e: e.tensor_copy(out=dbgf, in_=mixedT.rearrange("p c q -> p (c q)")), reads=[B_mixed, B_acc, B_rden], writes=[B_acc, B_rden])
            P.add("sp", lambda e: e.dma_start(out=dbg["mixedT"], in_=dbgf), reads=[B_acc], dsem=dd)

        R5 = Region(38912, 172000 - 32)
        xr = [view(R5.alloc(4096), 1024, F32) for _ in range(2)] + [view(165984, 1024, F32)]
        hmh = view(R5.alloc(4096), 1024, F32)
        hn2T = view(R5.alloc(8 * NQ * 2), 8 * NQ, BF16).rearrange("p (c q) -> p c q", q=NQ)
        hmid = view(R5.alloc(16 * 4096), 16 * 1024, F32).rearrange("p (t n) -> p t n", n=1024)
        B_xr = [Buf("xr0"), Buf("xr1"), Buf("xr2")]; B_hmh = Buf("hmh"); B_hn2T = Buf("hn2T"); B_hmid = [Buf("hmid%d" % i) for i in range(16)]
        dead34 = B34 + [B_QmT, B_cT, B_KmT, B_VmA] + B_accs + B_rdens + B_Pt + B_Vts + B_Vvs
        P.guard(writes=dead34 + B_xr + [B_hmh, B_hn2T] + B_hmid)
        R6w = Region(149600, 149600 + 16384)
        wup = [view(R6w.alloc(8 * 512 * 2), 4096, BF16).rearrange("p (c n) -> p c n", n=512) for _ in range(2)]
        B_wup = [Buf("wup%d" % i) for i in range(2)]
        dwu = [P.dsem("wup%d" % i) for i in range(2)]
        wu_src = w_up_d.rearrange("(c p) n -> p c n", p=128)

        def load_pp(pp):
            s_ = pp % 2
            P.add("pool", lambda e, s_=s_, pp=pp: e.dma_start(out=wup[s_][:, :, 0:256], in_=wu_src[:, :, 256 * pp:256 * pp + 256]), writes=[B_wup[s_]], dsem=dwu[s_])
            P.add("pool", lambda e, s_=s_, pp=pp: e.dma_start(out=wup[s_][:, :, 256:512], in_=wu_src[:, :, DFF + 256 * pp:DFF + 256 * pp + 256]), writes=[B_wup[s_]], dsem=dwu[s_])

        P.guard(writes=B_accs + B_rdens + B_Pt + B_Vts + B_Vvs + B_wup)
        load_pp(0)
        load_pp(1)
        dg2 = P.dsem("g2")
        P.add("sp", lambda e: e.dma_start(out=grep, in_=g2_d[0:1, :].to_broadcast([128, 1024])), writes=[B_grep], dsem=dg2)
        dxr = [P.dsem("xr0"), P.dsem("xr1"), P.dsem("xr2")]
        def p5_geom(i):
            if i < 16:
                q0 = 1 + 128 * i
                return 128, slice(q0, q0 + 128), hmid[:, i, :], B_hmid[i]
            return 2, slice(0, NQ, NQ - 1), hmh, B_hmh

        P5B = [(0, 1), (2, 3), (6, 7)]
        def p5_mm(i):
            xs = i % 3
            n, lsl, hdst, B_h = p5_geom(i)
            if i < 16:
                P.add("sp", lambda e: e.dma_start(out=xr[xs], in_=x_win[UQ0 + 1 + 128 * i:UQ0 + 1 + 128 * i + 128, :]), writes=[B_xr[xs]], dsem=dxr[xs])
            else:
                P.add("sp", lambda e: e.dma_start(out=xr[xs][0:1, :], in_=x_win[UQ0:UQ0 + 1, :]), writes=[B_xr[xs]], dsem=dxr[xs])
                P.add("sp", lambda e: e.dma_start(out=xr[xs][1:2, :], in_=x_win[UQ0 + NQ - 1:UQ0 + NQ, :]), writes=[B_xr[xs]], dsem=dxr[xs])
            for half in range(2):
                pb = bank(P5B[i % 3][half])
                for c in range(8):
                    P.add("pe", lambda e, pb=pb, c=c, half=half: e.matmul(pb[0:n, :], lhsT=mixedT[:, c, lsl], rhs=w_out[:, c, half * 512:half * 512 + 512],
                                                                         start=(c == 0), stop=(c == 7)),
                          reads=[B_mixed, B_wout], writes=[PB[P5B[i % 3][half]]])
                P.add("dve", lambda e, pb=pb, half=half: e.tensor_tensor(out=hdst[0:n, half * 512:half * 512 + 512], in0=pb[0:n, :],
                                                                        in1=xr[xs][0:n, half * 512:half * 512 + 512], op=ALU.add),
                      reads=[PB[P5B[i % 3][half]], B_xr[xs]], writes=[B_h])
            rms_stats_hn(hdst[0:n, :], n, B_h, xs, grep, B_grep)

        def p5_tr(i):
            xs = i % 3
            n, lsl, hdst, B_h = p5_geom(i)
            hn_transpose(n, xs, hn2T[:, :, lsl], [B_hn2T], 4 + (i % 2))
            if i == 16:
                P.add("pool", lambda e: e.tensor_tensor(out=hn2T[:, :, lsl], in0=hn2T[:, :, lsl], in1=hv.unsqueeze(1).to_broadcast([128, 8, 2]), op=ALU.mult),
                      reads=[B_hn2T, B_const], writes=[B_hn2T])

        p5_mm(0)
        p5_mm(1)
        for i in range(17):
            if i + 2 < 17:
                p5_mm(i + 2)
            p5_tr(i)

        if DEBUG:
            P.add("sp", lambda e: e.dma_start(out=dbg["hmid"], in_=hmid.rearrange("p t n -> p (t n)")), reads=B_hmid, dsem=dd)

        R6a = Region(22528, 51200)
        R6b = Region(171968, TOT)
        tg = [view(R6a.alloc(416 * 4), 416, F32) for _ in range(2)]
        tu = [view(R6a.alloc(416 * 4), 416, F32) for _ in range(2)]
        sg = [view(R6a.alloc(416 * 4), 416, F32) for _ in range(2)]
        actT = [None, None]
        actT[0] = view(R6a.alloc(4 * 2048 * 2), 4 * 2048, BF16).rearrange("p (k n) -> p k n", n=2048)
        actT[1] = view(R6b.alloc(4 * 2048 * 2), 4 * 2048, BF16).rearrange("p (k n) -> p k n", n=2048)
        wdn = [view(R6b.alloc(4 * 1024 * 2), 4 * 1024, BF16).rearrange("p (k n) -> p k n", n=1024) for _ in range(2)]
        B_tg = [Buf("tg0"), Buf("tg1")]; B_tu = [Buf("tu0"), Buf("tu1")]; B_sg = [Buf("sg0"), Buf("sg1")]
        B_actT = [Buf("actT0"), Buf("actT1")]; B_wdn = [Buf("wdn0"), Buf("wdn1")]
        P.guard(writes=[B_wout, B_mixed, B_hmh] + B_xr + B_tg + B_tu + B_sg + B_actT + B_wdn)
        dwd = [P.dsem("wdn%d" % i) for i in range(2)]
        wd_src = w_down_d.rearrange("(k p) n -> p k n", p=128)
        CHK = [(0, 410), (410, 410), (820, 410), (1230, 410), (1640, 408)]
        dout = P.dsem("out")
        out_ops = []
        pair_no = 0
        tcnt = 0

        def up_group(gi, pend=()):
            nonlocal tcnt
            pend = list(pend)
            (p0, p1) = FFN_GROUPS[gi]
            gs = gi % 2
            P.add("pool", lambda e, gs=gs, p0=p0, p1=p1: e.dma_start(out=wdn[gs][:, 0:p1 - p0, :], in_=wd_src[:, p0:p1, :]), writes=[B_wdn[gs]], dsem=dwd[gs])
            for pn in range(p0, p1):
                s_ = (pn // 2) % 2
                wo_ = 128 * (pn % 2)
                if pn % 2 == 0 and pn >= 2 and pn // 2 + 1 < NPAIR // 2:
                    load_pp(pn // 2 + 1)
                for (off, n) in CHK:
                    ts = tcnt % 2
                    tcnt += 1
                    bgk, buk = [(4, 5), (6, 7)][(tcnt - 1) % 2]
                    for (bk, wc) in [(bgk, wo_), (buk, 256 + wo_)]:
                        for c in range(8):
                            P.add("pe", lambda e, bk=bk, wc=wc, c=c, s_=s_, off=off, n=n: e.matmul(bank(bk)[:, 0:n + 2], lhsT=wup[s_][:, c, wc:wc + 128],
                                                                                                rhs=hn2T[:, c, off:off + n + 2], start=(c == 0), stop=(c == 7)),
                                  reads=[B_wup[s_], B_hn2T], writes=[PB[bk]])
                    halves_ = [(bgk, tg[ts], B_tg[ts], pn), (buk, tu[ts], B_tu[ts], NPAIR + pn)]
                    for (bk, tbuf, B_t, widx) in halves_:
                        P.add("act", lambda e, bk=bk, tbuf=tbuf, widx=widx, n=n: e.activation(out=tbuf[:, 0:n], in_=bank(bk)[:, 0:n], func=AF.Identity,
                                                                                           scale=fw[:, 3 * widx:3 * widx + 1], bias=fb[:, widx:widx + 1]),
                              reads=[PB[bk], B_const], writes=[B_t])
                    for tap in (1, 2):
                        for (bk, tbuf, B_t, widx) in halves_:
                            P.add("dve", lambda e, bk=bk, tbuf=tbuf, widx=widx, n=n, tap=tap: e.scalar_tensor_tensor(
                                out=tbuf[:, 0:n], in0=bank(bk)[:, tap:n + tap], scalar=fw[:, 3 * widx + tap:3 * widx + tap + 1],
                                in1=tbuf[:, 0:n], op0=ALU.mult, op1=ALU.add),
                                reads=[PB[bk], B_const, B_t], writes=[B_t])
                    P.add("act", lambda e, ts=ts, n=n: e.activation(out=sg[ts][:, 0:n], in_=tg[ts][:, 0:n], func=AF.Silu), reads=[B_tg[ts]], writes=[B_sg[ts]])
                    P.add("dve", lambda e, ts=ts, n=n, gs=gs, pn=pn, p0=p0, off=off: e.tensor_tensor(out=actT[gs][:, pn - p0, off:off + n], in0=sg[ts][:, 0:n], in1=tu[ts][:, 0:n], op=ALU.mult),
                          reads=[B_sg[ts], B_tu[ts]], writes=[B_actT[gs]])
                    if pend:
                        down_tile(*pend.pop(0))
            for rest_ in pend:
                down_tile(*rest_)

        def down_group(gi):
            for i in range(16):
                down_tile(gi, i)

        def down_tile(gi, i):
            (p0, p1) = FFN_GROUPS[gi]
            gs = gi % 2
            last = (gi == len(FFN_GROUPS) - 1)
            if True:
                for half in range(2):
                    bk = 2 * (i % 2) + half
                    for k in range(p1 - p0):
                        P.add("pe", lambda e, bk=bk, k=k, gs=gs, i=i, half=half, p0=p0, p1=p1: e.matmul(bank(bk)[:, :], lhsT=actT[gs][:, k, 128 * i:128 * i + 128],
                                                                                                   rhs=wdn[gs][:, k, half * 512:half * 512 + 512],
                                                                                                   start=(k == 0), stop=(k == p1 - p0 - 1)),
                              reads=[B_actT[gs], B_wdn[gs]], writes=[PB[bk]])
                    P.add("dve", lambda e, bk=bk, i=i, half=half: e.tensor_tensor(out=hmid[:, i, half * 512:half * 512 + 512], in0=bank(bk)[:, :],
                                                                                  in1=hmid[:, i, half * 512:half * 512 + 512], op=ALU.add),
                          reads=[PB[bk], B_hmid[i]], writes=[B_hmid[i]])
                if last:
                    out_ops.append(P.add("sp", lambda e, i=i: e.dma_start(out=y_d[128 * i:128 * i + 128, :], in_=hmid[:, i, :]), reads=[B_hmid[i]], dsem=dout))

        ng_ = len(FFN_GROUPS)
        up_group(0)
        for gi in range(1, ng_):
            up_group(gi, [(gi - 1, i) for i in range(16)])
        down_group(ng_ - 1)
        fin = P.add("sp", lambda e: e.nop())
        fin.deps.extend(out_ops)

        P.emit()
    return nc


_NC_CACHE = {}


def _get_nc():
    if "nc" not in _NC_CACHE:
        _NC_CACHE["nc"] = build_program()
    return _NC_CACHE["nc"]


def kernel(x, mem, positions, mix_norm_g, mem_norm_g, w_in, w_mem_kv, q_norm_g, k_norm_g, mq_norm_g, mk_norm_g,
           conv_dw_w, conv_dw_b, conv_ln_g, conv_ln_b, w_out, ffn_norm_g, w_up, ffn_dw_w, ffn_dw_b, w_down):
    f32 = np.float32
    x = np.asarray(x, f32); mem = np.asarray(mem, f32); positions = np.asarray(positions, np.int32)
    PT, NVT = pattern_tiles()
    ident = np.eye(128, dtype=f32)
    pp = np.arange(128)[:, None]; cc = np.arange(256)[None, :]
    bandmask = np.where((cc >= pp) & (cc <= pp + 128), 0.0, -30000.0).astype(f32)
    invf = (f32(500000.0) ** (-(np.arange(0, 16, 2, dtype=f32)) / f32(16))).astype(f32)
    invf = np.ascontiguousarray(np.broadcast_to(invf[None, :], (128, 8)))
    hg = np.concatenate([np.asarray(g, f32).reshape(-1) for g in (q_norm_g, k_norm_g, mq_norm_g, mk_norm_g)])[None, :]
    cwv = np.asarray(conv_dw_w, f32)[0]
    cw = np.ascontiguousarray(cwv.T.reshape(2, 128, 31).transpose(1, 0, 2).reshape(128, 62))
    def pc(v):
        return np.asarray(v, f32).reshape(2, 128).T
    cp = np.ascontiguousarray(np.concatenate([pc(conv_dw_b[0]), pc(conv_ln_g[0]), pc(conv_ln_b[0])], axis=1))
    fwv = np.asarray(ffn_dw_w, f32)[0]
    fw = np.ascontiguousarray(fwv.T.reshape(44, 128, 3).transpose(1, 0, 2).reshape(128, 132))
    fb = np.ascontiguousarray(np.asarray(ffn_dw_b, f32)[0].reshape(44, 128).T)
    shared = {
        "ident": ident, "bandmask": bandmask, "invf": invf,
        "g1": np.asarray(mix_norm_g, f32).reshape(1, 1024), "g2": np.asarray(ffn_norm_g, f32).reshape(1, 1024),
        "gm": np.asarray(mem_norm_g, f32).reshape(1, 1024), "hg": np.ascontiguousarray(hg),
        "w_in": np.asarray(w_in, f32)[0], "w_mkv": np.asarray(w_mem_kv, f32)[0], "w_out": np.asarray(w_out, f32)[0],
        "w_up": np.asarray(w_up, f32)[0], "w_down": np.asarray(w_down, f32)[0],
        "cw": cw, "cp": cp, "fw": fw, "fb": fb,
    }
    in_maps = []
    for ci in range(8):
        b = ci // 4
        T0 = (ci % 4) * 2048
        t_start = T0 - 1088
        tt = t_start + np.arange(NWIN)
        ok = (tt >= 0) & (tt < S)
        xw = np.zeros((NWIN, 1024), f32)
        xw[ok] = x[b, tt[ok]]
        pw = np.zeros((NWIN,), np.int32)
        pw[ok] = positions[b, tt[ok]]
        pos_win = np.ascontiguousarray(pw.reshape(NT, 128).T)
        val = np.zeros((128, NVT), f32)
        for (D, rho, sq0, nq, nqb, tiles) in PT:
            for (a, nk, idx) in tiles:
                u = rho + D * (a + np.arange(128))
                t = t_start + u
                val[:, idx] = np.where((t >= 0) & (t < S) & (np.arange(128) < nk), 0.0, -30000.0).astype(f32)
        m = dict(shared)
        hvv = np.zeros((128, 2), f32)
        hvv[:, 0] = 1.0 if (T0 - 1) >= 0 else 0.0
        hvv[:, 1] = 1.0 if (T0 + 2048) < S else 0.0
        m.update({"x_win": xw, "pos_win": pos_win, "valcols": val, "mem": np.ascontiguousarray(mem[b]), "hv": hvv})
        in_maps.append(m)
    nc = _get_nc()
    res = run_bass_kernel_spmd(nc, in_maps, core_ids=list(range(8)))
    out = np.zeros((2, S, 1024), f32)
    for ci in range(8):
        b = ci // 4
        T0 = (ci % 4) * 2048
        out[b, T0:T0 + 2048] = res.results[ci]["y"]
    if DEBUG:
        kernel.last_results = res.results
    return out
```

```python
import contextlib
import numpy as np
import concourse.bass as bass
import concourse.mybir as mybir
from concourse.bass_utils import run_bass_kernel_spmd

F32 = mybir.dt.float32
BF16 = mybir.dt.bfloat16
I32 = mybir.dt.int32
ALU = mybir.AluOpType
AF = mybir.ActivationFunctionType
AX = mybir.AxisListType

S = 8192
NWIN = 4224
NT = 33
UQ0 = 1087
NQ = 2050
QC0 = 1024
NQC = 2176
EPS = 1e-6
DFF = 2816
NPAIR = 22
FFN_GROUPS = [(0, 3), (3, 7), (7, 11), (11, 15), (15, 19), (19, 22)]
DEBUG = False

ENGS = ("pe", "act", "dve", "pool", "sp")


class Buf:
    __slots__ = ("name", "lw", "rd")

    def __init__(self, name):
        self.name = name
        self.lw = None
        self.rd = []


class Op:
    __slots__ = ("eng", "fn", "deps", "signal", "semval", "dsem", "is_dma", "virtual")

    def __init__(self, eng, fn, is_dma=False, dsem=None):
        self.eng = eng
        self.fn = fn
        self.deps = []
        self.signal = False
        self.semval = None
        self.dsem = dsem
        self.is_dma = is_dma
        self.virtual = False


class DSem:
    def __init__(self, name):
        self.name = name
        self.count = 0
        self.h = None


class Prog:
    def __init__(self, nc):
        self.nc = nc
        self.ops = {e: [] for e in ENGS}
        self.dsems = []

    def dsem(self, name):
        d = DSem(name)
        self.dsems.append(d)
        return d

    def add(self, eng, fn, reads=(), writes=(), dsem=None):
        is_dma = dsem is not None
        op = Op(eng, fn, is_dma=is_dma, dsem=dsem)
        deps = []
        for b in reads:
            if b.lw is not None:
                deps.append(b.lw)
        for b in writes:
            if b.lw is not None:
                deps.append(b.lw)
            deps.extend(b.rd)
        seen = set()
        for d in deps:
            if d is op or id(d) in seen:
                continue
            seen.add(id(d))
            if (not d.is_dma) and d.eng == "pe" and eng == "pe" and not is_dma:
                continue
            op.deps.append(d)
        for b in reads:
            b.rd.append(op)
        for b in writes:
            b.lw = op
            b.rd = []
        self.ops[eng].append(op)
        if is_dma:
            dsem.count += 1
            op.semval = 16 * dsem.count
        return op

    def guard(self, writes=(), extra_deps=()):
        op = Op(None, None)
        op.virtual = True
        seen = set()
        for b in writes:
            for d in ([b.lw] if b.lw is not None else []) + list(b.rd):
                if id(d) not in seen:
                    seen.add(id(d))
                    op.deps.append(d)
        for d in extra_deps:
            if id(d) not in seen:
                seen.add(id(d))
                op.deps.append(d)
        for b in writes:
            b.lw = op
            b.rd = []
        return op

    def _flatten(self):
        memo = {}

        def flat(op):
            k = id(op)
            if k in memo:
                return memo[k]
            out, seen = [], set()
            for d in op.deps:
                for r in (flat(d) if getattr(d, "virtual", False) else [d]):
                    if id(r) not in seen:
                        seen.add(id(r))
                        out.append(r)
            memo[k] = out
            return out

        for e in ENGS:
            for op in self.ops[e]:
                if any(getattr(d, "virtual", False) for d in op.deps):
                    real = []
                    seen = set()
                    for d in op.deps:
                        for r in (flat(d) if getattr(d, "virtual", False) else [d]):
                            if r is op or id(r) in seen:
                                continue
                            if (not r.is_dma) and r.eng == "pe" and op.eng == "pe" and not op.is_dma:
                                continue
                            seen.add(id(r))
                            real.append(r)
                    op.deps = real

    def emit(self):
        nc = self.nc
        self._flatten()
        for e in ENGS:
            for op in self.ops[e]:
                for d in op.deps:
                    if not d.is_dma:
                        d.signal = True
        for e in ENGS:
            c = 0
            for op in self.ops[e]:
                if op.is_dma:
                    continue
                if op.signal:
                    c += 1
                    op.semval = c
        with contextlib.ExitStack() as st:
            esem = {e: st.enter_context(nc.semaphore("s_" + e)) for e in ENGS}
            for d in self.dsems:
                d.h = st.enter_context(nc.semaphore("d_" + d.name))
            block = st.enter_context(nc.Block())

            def run(eng_name):
                def body(eng):
                    waited = {}
                    for op in self.ops[eng_name]:
                        need = {}
                        for d in op.deps:
                            if d.is_dma:
                                key = ("d", id(d.dsem)); h = d.dsem.h
                            else:
                                key = ("e", d.eng); h = esem[d.eng]
                            if key not in need or need[key][1] < d.semval:
                                need[key] = (h, d.semval)
                        for key, (h, val) in need.items():
                            if waited.get(key, 0) >= val:
                                continue
                            eng.wait_ge(h, val)
                            waited[key] = val
                        ins = op.fn(eng)
                        if op.is_dma:
                            ins.then_inc(op.dsem.h, 16)
                        elif op.signal:
                            ins.then_inc(esem[eng_name], 1)
                return body

            block.sync(run("sp"))
            block.tensor(run("pe"))
            block.scalar(run("act"))
            block.vector(run("dve"))
            block.gpsimd(run("pool"))


def pattern_tiles():
    res = []
    idx = 0
    for D in (1, 4, 16):
        for rho in range(D):
            sq0 = -((-(UQ0 - rho)) // D)
            sq1 = (UQ0 + NQ - 1 - rho) // D
            nq = sq1 - sq0 + 1
            nqb = (nq + 127) // 128
            tiles = []
            for j in range(nqb + 1):
                a = sq0 - 64 + 128 * j
                nk = min(128, sq1 + 64 - a + 1)
                tiles.append((a, nk, idx))
                idx += 1
            res.append((D, rho, sq0, nq, nqb, tiles))
    return res, idx


def build_program():
    nc = bass.Bass("TRN2", target_bir_lowering=False)
    P = Prog(nc)
    PT, NVT = pattern_tiles()

    def din(name, shape, dt=F32):
        return nc.dram_tensor(name, list(shape), dt, kind="ExternalInput").ap()

    x_win = din("x_win", [NWIN, 1024])
    pos_win = din("pos_win", [128, NT], I32)
    valcols_d = din("valcols", [128, NVT])
    mem_d = din("mem", [256, 1024])
    ident_d = din("ident", [128, 128])
    bandmask_d = din("bandmask", [128, 256])
    invf_d = din("invf", [128, 8])
    g1_d = din("g1", [1, 1024])
    g2_d = din("g2", [1, 1024])
    gm_d = din("gm", [1, 1024])
    hg_d = din("hg", [1, 256])
    w_in_d = din("w_in", [1024, 2304])
    w_mkv_d = din("w_mkv", [1024, 512])
    w_out_d = din("w_out", [1024, 1024])
    w_up_d = din("w_up", [1024, 2 * DFF])
    w_down_d = din("w_down", [DFF, 1024])
    cw_d = din("cw", [128, 62])
    cp_d = din("cp", [128, 6])
    fw_d = din("fw", [128, 44 * 3])
    fb_d = din("fb", [128, 44])
    hv_d = din("hv", [128, 2])
    y_d = nc.dram_tensor("y", [2048, 1024], F32, kind="ExternalOutput").ap()
    dbg = {}
    if DEBUG:
        dbg["mixedT"] = nc.dram_tensor("dbg_mixedT", [128, 8 * NQ], F32, kind="ExternalOutput").ap()
        dbg["hmid"] = nc.dram_tensor("dbg_hmid", [128, 16 * 1024], F32, kind="ExternalOutput").ap()

    st = contextlib.ExitStack()
    with st:
        ARENA_KB = 204
        arena = st.enter_context(nc.sbuf_tensor("arena", [128, ARENA_KB * 512], BF16))
        pbanks = [st.enter_context(nc.psum_tensor("pq%d" % i, [128, 1024], F32)) for i in range(4)]

        class Region:
            def __init__(self, start, end):
                self.pos = start
                self.end = end

            def alloc(self, nbytes):
                nbytes = (nbytes + 31) // 32 * 32
                off = self.pos
                self.pos += nbytes
                assert self.pos <= self.end, ("arena overflow", self.pos, self.end)
                return off

        def view(off, n, dt):
            if dt == BF16:
                return arena[:, off // 2: off // 2 + n]
            return arena[:, off // 2: off // 2 + 2 * n].bitcast(dt)

        TOT = ARENA_KB * 1024
        RC = Region(0, 22528)
        RM = Region(172000 - 32, TOT)

        def bank(i, dt=F32):
            t = pbanks[i // 2][:, (i % 2) * 512:(i % 2) * 512 + 512]
            return t if dt == F32 else t.bitcast(dt)

        PB = [Buf("psum%d" % i) for i in range(8)]

        ident = view(RC.alloc(256), 128, BF16)
        bandmask = view(RC.alloc(512), 256, BF16)
        onesm = view(RC.alloc(256), 128, BF16)
        valcols = view(RC.alloc(NVT * 4), NVT, F32)
        invf = view(RC.alloc(32), 8, F32)
        hg = view(RC.alloc(1024), 256, F32)
        cs_tab = view(RC.alloc(NT * 8 * 4), NT * 8, F32)
        sn_tab = view(RC.alloc(NT * 8 * 4), NT * 8, F32)
        cw = view(RC.alloc(62 * 4), 62, F32)
        cp = view(RC.alloc(32), 6, F32)
        fw = view(RC.alloc(132 * 4), 132, F32)
        fb = view(RC.alloc(44 * 4), 44, F32)
        grep = view(RC.alloc(4096), 1024, F32)
        stat = view(RC.alloc(128 * 4), 128, F32)
        epsc = view(RC.alloc(32), 1, F32)
        onesb = view(RC.alloc(256), 128, BF16)
        hv = view(RC.alloc(32), 2, F32)
        junk = view(RC.alloc(2048), 1024, BF16)
        hnb = [view(RC.alloc(2048), 1024, BF16) for _ in range(3)]
        B_const = Buf("const")
        B_grep = Buf("grep")
        B_tab = Buf("tab")

        mixedT = view(RM.alloc(8 * NQ * 2), 8 * NQ, BF16).rearrange("p (c q) -> p c q", q=NQ)
        B_mixed = Buf("mixedT")

        const_ops = []
        for k_, (dst, src) in enumerate([(ident, ident_d), (bandmask, bandmask_d)]):
            const_ops.append(P.add("pool", lambda e, dst=dst, src=src: e.dma_start(out=dst, in_=src), writes=[Buf("c")], dsem=P.dsem("cp%d" % k_)))
        for k_, (dst, src) in enumerate([(valcols, valcols_d), (invf, invf_d), (cw, cw_d), (cp, cp_d), (fw, fw_d), (fb, fb_d), (hv, hv_d)]):
            const_ops.append(P.add("sp", lambda e, dst=dst, src=src: e.dma_start(out=dst, in_=src), writes=[Buf("c")], dsem=P.dsem("cs%d" % k_)))
        const_ops.append(P.add("sp", lambda e: e.dma_start(out=hg, in_=hg_d[0:1, :].to_broadcast([128, 256])), writes=[Buf("c")], dsem=P.dsem("chg")))
        P.add("sp", lambda e: e.dma_start(out=grep, in_=g1_d[0:1, :].to_broadcast([128, 1024])), writes=[B_grep], dsem=P.dsem("cg1"))
        P.guard(writes=[B_const], extra_deps=const_ops)
        P.add("pool", lambda e: e.memset(onesm, 1.0 / 256.0), writes=[B_const])
        P.add("pool", lambda e: e.memset(epsc, EPS), writes=[B_const])
        P.add("pool", lambda e: e.memset(onesb, 1.0), writes=[B_const])

        R1 = Region(22528, TOT)
        KT = view(R1.alloc(4 * NWIN * 2), 4 * NWIN, BF16).rearrange("p (c n) -> p c n", n=NWIN)
        VT = view(R1.alloc(4 * NWIN * 2), 4 * NWIN, BF16).rearrange("p (c n) -> p c n", n=NWIN)
        QT = view(R1.alloc(4 * NQC * 2), 4 * NQC, BF16).rearrange("p (c n) -> p c n", n=NQC)
        QmT = view(R1.alloc(2 * NQC * 2), 2 * NQC, BF16).rearrange("p (c n) -> p c n", n=NQC)
        cT = view(R1.alloc(2 * NQC * 2), 2 * NQC, BF16).rearrange("p (c n) -> p c n", n=NQC)
        KmT = view(R1.alloc(2 * 256 * 2), 512, BF16).rearrange("p (c n) -> p c n", n=256)
        VmA = view(R1.alloc(2 * 512 * 2), 1024, BF16).rearrange("p (t n) -> p t n", n=512)
        B_KmT = Buf("KmT"); B_VmA = Buf("VmA")
        p1_mark = R1.pos
        assert p1_mark == 128000, p1_mark
        w_in = view(R1.alloc(8 * 2304 * 2), 8 * 2304, BF16).rearrange("p (c n) -> p c n", n=2304)
        RW = Region(22528 + 2 * 33792, 22528 + 2 * 33792 + 17408)
        memT = view(RW.alloc(8 * 128 * 2), 1024, BF16).rearrange("p (c n) -> p c n", n=128)
        w_mkv = view(RW.alloc(8 * 512 * 2), 4096, BF16).rearrange("p (c n) -> p c n", n=512)
        gmrep = view(RW.alloc(4096), 1024, F32)
        B_memT = Buf("memT"); B_wmkv = Buf("wmkv"); B_gm = Buf("gmrep")
        hnT = [view(R1.alloc(8 * 256 * 2), 8 * 256, BF16).rearrange("p (c n) -> p c n", n=256) for _ in range(2)]
        xt = [view(R1.alloc(4096), 1024, F32) for _ in range(2)]
        class HN:
            pass
        def mk_ctx(name, W, rotary, scol):
            c = HN()
            c.name = name
            c.stg = view(R1.alloc(2048), 512, F32)
            c.qb = view(R1.alloc(1024), 512, BF16)
            c.rot = view(R1.alloc(1024), 256, F32) if rotary else None
            c.B_stg = Buf(name + "_stg"); c.B_qb = Buf(name + "_qb"); c.B_rot = Buf(name + "_rot"); c.B_st = Buf(name + "_st")
            c.ssq = stat[:, scol:scol + 8]
            c.rs = stat[:, scol + 8:scol + 16]
            return c
        ctx_k = [mk_ctx("k0", 512, True, 8), mk_ctx("k1", 512, True, 24)]
        ctx_qs = [mk_ctx("q0", 512, True, 40), mk_ctx("q1", 512, True, 72)]
        ctx_qms = [mk_ctx("qm0", 512, False, 56), mk_ctx("qm1", 512, False, 88)]
        ctx_q = ctx_qs[0]; ctx_qm = ctx_qms[0]
        ALL_CTX = ctx_k + ctx_qs + ctx_qms
        print("R1 end before misc", R1.pos, TOT)
        sig = view(R1.alloc(1024), 256, F32)
        posi = view(R1.alloc(NT * 4), NT, I32)
        posf = view(R1.alloc(NT * 4), NT, F32)
        ang = view(R1.alloc(NT * 8 * 4), NT * 8, F32)
        angi = view(R1.alloc(NT * 8 * 4), NT * 8, I32)
        angf = view(R1.alloc(NT * 8 * 4), NT * 8, F32)

        B_wins = [Buf("w_in%d" % i) for i in range(8)]; B_win = B_wins[0]; B_KT = Buf("KT"); B_VT = Buf("VT"); B_QT = Buf("QT"); B_QmT = Buf("QmT"); B_cT = Buf("cT")
        B_hnT = [[Buf("hnT00"), Buf("hnT01")], [Buf("hnT10"), Buf("hnT11")]]; B_xt = [Buf("xt0"), Buf("xt1")]; B_hnb = [Buf("hnb0"), Buf("hnb1"), Buf("hnb2")]
        B_sig = Buf("sig"); B_rst = [Buf("rst0"), Buf("rst1"), Buf("rst2")]; B_pos = Buf("pos")


        dpos = P.dsem("pos")
        P.add("sp", lambda e: e.dma_start(out=posi, in_=pos_win), writes=[B_pos], dsem=dpos)
        P.add("dve", lambda e: e.tensor_copy(out=posf, in_=posi), reads=[B_pos], writes=[B_pos])
        ang3 = ang.rearrange("p (t f) -> p t f", f=8)
        P.add("dve", lambda e: e.tensor_tensor(out=ang3, in0=posf.unsqueeze(2).to_broadcast([128, NT, 8]),
                                               in1=invf.unsqueeze(1).to_broadcast([128, NT, 8]), op=ALU.mult),
              reads=[B_pos, B_const], writes=[B_pos])
        for (tab, shift) in [(sn_tab, 0.0), (cs_tab, 0.25)]:
            P.add("dve", lambda e, shift=shift: e.tensor_scalar(out=angf, in0=ang, scalar1=1.0 / (2 * np.pi), scalar2=shift,
                                                                op0=ALU.mult, op1=ALU.add), reads=[B_pos], writes=[B_tab])
            P.add("dve", lambda e: e.tensor_copy(out=angi, in_=angf), reads=[B_tab], writes=[B_tab])
            P.add("dve", lambda e, tab=tab: e.tensor_copy(out=tab, in_=angi), reads=[B_tab], writes=[B_tab])
            P.add("dve", lambda e, tab=tab: e.tensor_tensor(out=tab, in0=angf, in1=tab, op=ALU.subtract), reads=[B_tab], writes=[B_tab])
            P.add("act", lambda e, tab=tab: e.activation(out=tab, in_=tab, func=AF.Sin, scale=6.283185005187988), reads=[B_tab], writes=[B_tab])

        def rms_stats_hn(src_ap, n, B_src, slot, grep_ap, B_grep_):
            ss = stat[0:n, 2 * slot:2 * slot + 1]
            rs = stat[0:n, 2 * slot + 1:2 * slot + 2]
            P.add("act", lambda e: e.activation(out=junk[0:n, :], in_=src_ap, func=AF.Square, accum_out=ss), reads=[B_src], writes=[B_rst[slot]])
            P.add("act", lambda e: e.activation(out=rs, in_=ss, func=AF.Sqrt, scale=1.0 / 1024, bias=epsc[0:n, :]), reads=[B_rst[slot], B_const], writes=[B_rst[slot]])
            P.add("dve", lambda e: e.reciprocal(out=rs, in_=rs), reads=[B_rst[slot]], writes=[B_rst[slot]])
            hb = hnb[slot]
            P.add("dve", lambda e: e.scalar_tensor_tensor(out=hb[0:n, :], in0=src_ap, scalar=rs, in1=grep_ap[0:n, :], op0=ALU.mult, op1=ALU.mult),
                  reads=[B_src, B_rst[slot], B_grep_], writes=[B_hnb[slot]])

        def hn_transpose(n, slot, dst_ap, B_dsts, tp_bank):
            hb = hnb[slot]
            tp = bank(tp_bank, BF16).rearrange("p (c n) -> p c n", n=128)
            for c in range(8):
                P.add("pe", lambda e, c=c: e.transpose(out=tp[:, c, 0:n], in_=hb[0:n, c * 128:(c + 1) * 128], identity=ident[0:n, 0:n]),
                      reads=[B_hnb[slot], B_const], writes=[PB[tp_bank]])
            P.add("act", lambda e: e.activation(out=dst_ap, in_=tp[:, :, 0:n], func=AF.Copy), reads=[PB[tp_bank]], writes=B_dsts)

        def rmsnorm_transpose(src_ap, n, B_src, slot, dstT, B_dst, col_ap_fn, tp_bank, grep=grep, B_grep=B_grep):
            rms_stats_hn(src_ap, n, B_src, slot, grep, B_grep)
            hn_transpose(n, slot, col_ap_fn(dstT), [B_dst], tp_bank)

        def headnorm_staged(items, stage_lo, stage_hi):
            n = 128
            for stg_i in range(stage_lo, stage_hi):
                for (c, nheads, gain_idx, T) in items:
                    W = nheads * 64
                    s3 = c.stg[:, 0:W].rearrange("p (h d) -> p h d", d=64)
                    o3 = c.qb[:, 0:W].rearrange("p (h d) -> p h d", d=64)
                    ssq = c.ssq[:, 0:nheads]; rs = c.rs[:, 0:nheads]
                    g3 = hg[:, gain_idx * 64:(gain_idx + 1) * 64].unsqueeze(1).to_broadcast([n, nheads, 64])
                    if stg_i == 1:
                        P.add("act", lambda e, c=c, W=W: e.activation(out=c.qb[:, 0:W], in_=c.stg[:, 0:W], func=AF.Square), reads=[c.B_stg], writes=[c.B_qb])
                    elif stg_i == 2:
                        P.add("dve", lambda e, o3=o3, ssq=ssq: e.tensor_reduce(out=ssq, in_=o3, axis=AX.X, op=ALU.add), reads=[c.B_qb], writes=[c.B_st])
                    elif stg_i == 3:
                        P.add("act", lambda e, ssq=ssq, rs=rs: e.activation(out=rs, in_=ssq, func=AF.Sqrt, scale=1.0 / 64, bias=epsc), reads=[c.B_st, B_const], writes=[c.B_st])
                    elif stg_i == 4:
                        P.add("dve", lambda e, rs=rs: e.reciprocal(out=rs, in_=rs), reads=[c.B_st], writes=[c.B_st])
                    elif stg_i == 5:
                        P.add("dve", lambda e, s3=s3, rs=rs, nheads=nheads: e.tensor_tensor(out=s3, in0=s3, in1=rs.unsqueeze(2).to_broadcast([n, nheads, 64]), op=ALU.mult),
                              reads=[c.B_stg, c.B_st], writes=[c.B_stg])
                    elif stg_i == 6:
                        if T is None:
                            P.add("dve", lambda e, s3=s3, o3=o3, g3=g3: e.tensor_tensor(out=o3, in0=s3, in1=g3, op=ALU.mult), reads=[c.B_stg, B_const, c.B_qb], writes=[c.B_qb])
                        else:
                            P.add("dve", lambda e, s3=s3, g3=g3: e.tensor_tensor(out=s3, in0=s3, in1=g3, op=ALU.mult), reads=[c.B_stg, B_const], writes=[c.B_stg])
                    elif stg_i == 7 and T is not None:
                        cosb = cs_tab[:, T * 8:T * 8 + 8].unsqueeze(1).to_broadcast([n, nheads, 8])
                        sinb = sn_tab[:, T * 8:T * 8 + 8].unsqueeze(1).to_broadcast([n, nheads, 8])
                        x1 = s3[:, :, 0:8]
                        x2 = s3[:, :, 8:16]
                        r4 = c.rot[:, :].rearrange("p (k h f) -> p k h f", k=4, f=8)
                        P.add("act", lambda e, o3=o3, s3=s3: e.activation(out=o3[:, :, 16:64], in_=s3[:, :, 16:64], func=AF.Copy), reads=[c.B_stg, c.B_qb], writes=[c.B_qb])
                        P.add("pool", lambda e, r4=r4, x1=x1, cosb=cosb: e.tensor_tensor(out=r4[:, 0], in0=x1, in1=cosb, op=ALU.mult), reads=[c.B_stg, B_tab], writes=[c.B_rot])
                        P.add("pool", lambda e, r4=r4, x2=x2, sinb=sinb: e.tensor_tensor(out=r4[:, 1], in0=x2, in1=sinb, op=ALU.mult), reads=[c.B_stg, B_tab], writes=[c.B_rot])
                        P.add("pool", lambda e, r4=r4, x2=x2, cosb=cosb: e.tensor_tensor(out=r4[:, 2], in0=x2, in1=cosb, op=ALU.mult), reads=[c.B_stg, B_tab], writes=[c.B_rot])
                        P.add("pool", lambda e, r4=r4, x1=x1, sinb=sinb: e.tensor_tensor(out=r4[:, 3], in0=x1, in1=sinb, op=ALU.mult), reads=[c.B_stg, B_tab], writes=[c.B_rot])
                        P.add("pool", lambda e, r4=r4, o3=o3: e.tensor_tensor(out=o3[:, :, 0:8], in0=r4[:, 0], in1=r4[:, 1], op=ALU.subtract), reads=[c.B_rot, c.B_qb], writes=[c.B_qb])
                        P.add("pool", lambda e, r4=r4, o3=o3: e.tensor_tensor(out=o3[:, :, 8:16], in0=r4[:, 2], in1=r4[:, 3], op=ALU.add), reads=[c.B_rot, c.B_qb], writes=[c.B_qb])

        def transpose_to(srcb, B_srcb, n, nblk, dstT, B_dst, col0, tq_bank):
            tq = bank(tq_bank, BF16).rearrange("p (c n) -> p c n", n=128)
            for j in range(nblk):
                P.add("pe", lambda e, j=j: e.transpose(out=tq[:, j, 0:n], in_=srcb[0:n, j * 128:(j + 1) * 128], identity=ident[0:n, 0:n]),
                      reads=[B_srcb, B_const], writes=[PB[tq_bank]])
            P.add("dve", lambda e: e.tensor_copy(out=dstT[:, 0:nblk, col0:col0 + n], in_=tq[:, 0:nblk, 0:n]),
                  reads=[PB[tq_bank]], writes=[B_dst])


        dmem = P.dsem("mem")
        wm_src = w_mkv_d.rearrange("(c p) n -> p c n", p=128)
        P.add("pool", lambda e: e.dma_start(out=w_mkv, in_=wm_src), writes=[B_wmkv], dsem=dmem)
        dws = [P.dsem("w_in%d" % i) for i in range(8)]
        w_in_src = w_in_d.rearrange("(c p) n -> p c n", p=128)
        for c in range(8):
            P.add("pool", lambda e, c=c: e.dma_start(out=w_in[:, c, :], in_=w_in_src[:, c, :]), writes=[B_wins[c]], dsem=dws[c])

        dx = [P.dsem("x0"), P.dsem("x1")]
        P.add("sp", lambda e: e.dma_start(out=gmrep, in_=gm_d[0:1, :].to_broadcast([128, 1024])), writes=[B_gm], dsem=P.dsem("gm"))
        for mt in range(2):
            P.add("sp", lambda e, mt=mt: e.dma_start(out=xt[mt], in_=mem_d[128 * mt:128 * mt + 128, :]), writes=[B_xt[mt]], dsem=dx[mt])
            rmsnorm_transpose(xt[mt], 128, B_xt[mt], mt, memT, B_memT, lambda d: d[:, :, 0:128], 4, grep=gmrep, B_grep=B_gm)
            pb = bank(3)
            cm = ctx_qm
            for c in range(8):
                P.add("pe", lambda e, c=c, pb=pb: e.matmul(pb[:, 0:512], lhsT=memT[:, c, 0:128], rhs=w_mkv[:, c, :], start=(c == 0), stop=(c == 7)),
                      reads=[B_memT, B_wmkv], writes=[PB[3]])
            P.add("act", lambda e, pb=pb: e.activation(out=cm.stg[:, 0:512], in_=pb[:, 0:512], func=AF.Copy), reads=[PB[3]], writes=[cm.B_stg])
            vsrc = cm.stg[:, 256:512].rearrange("p (hp two d) -> p hp two d", two=2, d=64)
            vdst = VmA[:, mt, :].rearrange("p (hp s d) -> p hp s d", s=4, d=64)
            P.add("dve", lambda e, vdst=vdst, vsrc=vsrc: e.tensor_copy(out=vdst[:, :, 0, :], in_=vsrc[:, :, 0, :]), reads=[cm.B_stg], writes=[B_VmA])
            P.add("dve", lambda e, vdst=vdst, vsrc=vsrc: e.tensor_copy(out=vdst[:, :, 3, :], in_=vsrc[:, :, 1, :]), reads=[cm.B_stg], writes=[B_VmA])
            P.add("pool", lambda e, vdst=vdst: e.memset(vdst[:, :, 1:3, :], 1.0), writes=[B_VmA])
            headnorm_staged([(cm, 4, 3, None)], 1, 8)
            transpose_to(cm.qb, cm.B_qb, 128, 2, KmT, B_KmT, 128 * mt, 5)

        fm_rot = [0]
        FM_BANKS = [6, 7]

        def hslot(T):
            return ((T // 2) % 2, T % 2)

        def A1(T):
            xs = T % 2
            P.add("sp", lambda e, T=T, xs=xs: e.dma_start(out=xt[xs], in_=x_win[128 * T:128 * T + 128, :]), writes=[B_xt[xs]], dsem=dx[xs])
            rms_stats_hn(xt[xs], 128, B_xt[xs], xs, grep, B_grep)

        def A2(T):
            hs, part = hslot(T)
            hn_transpose(128, T % 2, hnT[hs][:, :, 128 * part:128 * part + 128], [B_hnT[hs][part]], 4)

        def tile_items(T):
            items = [(ctx_k[T % 2], 8, 1, T)]
            if 8 <= T <= 24:
                items += [(ctx_qs[T % 2], 8, 0, T), (ctx_qms[T % 2], 4, 2, None)]
            return items

        def Bst(T):
            hs, part = hslot(T)
            lc = 128 * part
            jobs = [(0, 512, 1024, 512, ctx_k[T % 2])]
            if 8 <= T <= 24:
                jobs += [(2, 0, 512, 512, ctx_qs[T % 2]), (3, 2048, 2304, 256, ctx_qms[T % 2])]
            for (bi, c0, c1, W, cx) in jobs:
                pb = bank(bi)
                for c in range(8):
                    P.add("pe", lambda e, c=c, pb=pb, c0=c0, c1=c1, W=W, lc=lc, hs=hs: e.matmul(pb[:, 0:W], lhsT=hnT[hs][:, c, lc:lc + 128], rhs=w_in[:, c, c0:c1],
                                                                                                start=(c == 0), stop=(c == 7)),
                          reads=[B_hnT[hs][part], B_wins[c]], writes=[PB[bi]])
                P.add("act", lambda e, pb=pb, cx=cx, W=W: e.activation(out=cx.stg[:, 0:W], in_=pb[:, 0:W], func=AF.Copy), reads=[PB[bi]], writes=[cx.B_stg])

        def Dk(T):
            transpose_to(ctx_k[T % 2].qb, ctx_k[T % 2].B_qb, 128, 4, KT, B_KT, 128 * T, 5)

        def Dq(T):
            if T == 8:
                P.guard(writes=[B_QT, B_wmkv, B_gm, B_memT])
            if 8 <= T <= 24:
                tq = bank(1, BF16).rearrange("p (c n) -> p c n", n=128)
                for j in range(4):
                    P.add("pe", lambda e, j=j: e.transpose(out=tq[:, j, :], in_=ctx_qs[T % 2].qb[:, j * 128:(j + 1) * 128], identity=ident),
                          reads=[ctx_qs[T % 2].B_qb, B_const], writes=[PB[1]])
                for j in range(2):
                    P.add("pe", lambda e, j=j: e.transpose(out=tq[:, 4 + j, :], in_=ctx_qms[T % 2].qb[:, j * 128:(j + 1) * 128], identity=ident),
                          reads=[ctx_qms[T % 2].B_qb, B_const], writes=[PB[1]])
                col0 = 128 * T - QC0
                P.add("dve", lambda e: e.tensor_copy(out=QT[:, 0:4, col0:col0 + 128], in_=tq[:, 0:4, :]), reads=[PB[1]], writes=[B_QT])
                P.add("dve", lambda e: e.tensor_copy(out=QmT[:, 0:2, col0:col0 + 128], in_=tq[:, 4:6, :]), reads=[PB[1]], writes=[B_QmT])

        def Fst(ch):
            tiles = list(range(2 * ch, min(2 * ch + 2, NT)))
            hs = ch % 2
            ntok = 128 * len(tiles)
            rd0 = [B_hnT[hs][t % 2] for t in tiles]
            u0 = 256 * ch
            for fp in range(2):
                bi = FM_BANKS[fm_rot[0] % 2]; fm_rot[0] += 1
                pb = bank(bi)
                for sub in range(2):
                    ft = 2 * fp + sub
                    for c in range(8):
                        P.add("pe", lambda e, c=c, pb=pb, ft=ft, sub=sub: e.matmul(pb[:, 256 * sub:256 * sub + ntok], lhsT=w_in[:, c, 1024 + 128 * ft:1024 + 128 * ft + 128],
                                                                                   rhs=hnT[hs][:, c, 0:ntok], start=(c == 0 and sub == 0), stop=(c == 7), skip_group_check=True),
                              reads=rd0 + [B_wins[c]], writes=[PB[bi]])
                pv = pb.rearrange("p (s n) -> p s n", n=256)
                P.add("dve", lambda e, pv=pv, fp=fp: e.tensor_copy(out=VT[:, 2 * fp:2 * fp + 2, u0:u0 + ntok], in_=pv[:, :, 0:ntok]), reads=[PB[bi]], writes=[B_VT])
            if QC0 <= u0 < QC0 + NQC:
                ng = min(256, QC0 + NQC - u0)
                for ft in range(2):
                    bi = FM_BANKS[fm_rot[0] % 2]; fm_rot[0] += 1
                    pb = bank(bi)
                    for (sub, cbase) in [(0, 1536), (1, 1792)]:
                        for c in range(8):
                            P.add("pe", lambda e, c=c, pb=pb, cbase=cbase, ft=ft, sub=sub: e.matmul(pb[:, 256 * sub:256 * sub + ng], lhsT=w_in[:, c, cbase + 128 * ft:cbase + 128 * ft + 128],
                                                                                                     rhs=hnT[hs][:, c, 0:ng], start=(c == 0 and sub == 0), stop=(c == 7), skip_group_check=True),
                                  reads=rd0 + [B_wins[c]], writes=[PB[bi]])
                    P.add("act", lambda e, pb=pb: e.activation(out=sig[:, 0:ng], in_=pb[:, 256:256 + ng], func=AF.Sigmoid), reads=[PB[bi]], writes=[B_sig])
                    P.add("dve", lambda e, pb=pb, ft=ft: e.tensor_tensor(out=cT[:, ft, u0 - QC0:u0 - QC0 + ng], in0=pb[:, 0:ng], in1=sig[:, 0:ng], op=ALU.mult),
                          reads=[PB[bi], B_sig], writes=[B_cT])

        A1(0); A2(0); A1(1)
        for T in range(NT):
            if T + 1 < NT:
                A2(T + 1)
            if T + 2 < NT:
                A1(T + 2)
            Bst(T)
            items = tile_items(T)
            if T % 2 == 1 or T == NT - 1:
                Fst(T // 2)
            headnorm_staged(items, 1, 8)
            if T >= 1:
                Dq(T - 1)
                Dk(T - 1)
        Dq(NT - 1)
        Dk(NT - 1)

        R2 = Region(128000, 172000 - 32)
        acc = view(R2.alloc(2 * NQ * 4), 2 * NQ, F32).rearrange("p (h q) -> p h q", q=NQ)
        rden = view(R2.alloc(NQ * 4), NQ, F32)
        Pt = [view(R2.alloc(1024), 512, BF16).rearrange("p (h n) -> p h n", n=256) for _ in range(3)]
        Vts = [view(R2.alloc(512), 256, BF16) for _ in range(4)]
        B_accs = [Buf("acc%d" % i) for i in range(5)]; B_rdens = [Buf("rden%d" % i) for i in range(5)]
        B_acc = B_accs[0]; B_rden = B_rdens[0]
        B_Pt = [Buf("Pt0"), Buf("Pt1"), Buf("Pt2")]; B_Vts = [Buf("Vt%d" % i) for i in range(4)]; B_Vvs = [Buf("Vv%d" % i) for i in range(4)]

        def acc_bufs(isl):
            return [B_accs[k] for k in range(isl.start // 512, (isl.stop - 1) // 512 + 1)]
        P1_TEMPS = B_wins + B_hnT[0] + B_hnT[1] + B_xt + [B_sig, B_pos, B_tab]
        for c_ in ALL_CTX:
            P1_TEMPS += [c_.B_stg, c_.B_qb, c_.B_rot, c_.B_st]
        P.guard(writes=P1_TEMPS + B_accs + B_rdens + [B_mixed] + B_Pt + B_Vts + B_Vvs)

        for vs_ in range(4):
            P.add("pool", lambda e, vs_=vs_: e.memset(Vts[vs_][:, 64:192], 1.0), writes=[B_Vvs[vs_]])
        for hp in range(4):
            kts = []
            for (D, rho, sq0, nq, nqb, tiles) in PT:
                qbs = [min(128, nq - 128 * m) for m in range(nqb)]
                for j, (a, nk, idx) in enumerate(tiles):
                    kts.append((D, rho, sq0, nqb, qbs, j, a, nk, idx))
            NK = len(kts)

            def ksl_of(i):
                (D, rho, sq0, nqb, qbs, j, a, nk, idx) = kts[i]
                ku0 = rho + D * a
                return slice(ku0, ku0 + D * (nk - 1) + 1, D)

            def stV(i):
                (D, rho, sq0, nqb, qbs, j, a, nk, idx) = kts[i]
                vs = i % 4
                ksl = ksl_of(i)
                vt = bank(6, BF16)[:, 0:128]
                P.add("pe", lambda e, hp=hp: e.transpose(out=vt[0:nk, :], in_=VT[:, hp, ksl], identity=ident), reads=[B_VT, B_const], writes=[PB[6]])
                V4 = Vts[vs].rearrange("p (s d) -> p s d", d=64)
                P.add("act", lambda e: e.activation(out=V4[0:nk, 0:4:3, :], in_=vt[0:nk, :].rearrange("p (s d) -> p s d", d=64), func=AF.Copy),
                      reads=[PB[6]], writes=[B_Vts[vs]])

            def cols_of(i):
                (D, rho, sq0, nqb, qbs, j, a, nk, idx) = kts[i]
                c0 = 0 if j >= 1 else 128
                c1 = (128 + qbs[j]) if j < nqb else qbs[j - 1]
                return c0, c1

            def stS(i):
                (D, rho, sq0, nqb, qbs, j, a, nk, idx) = kts[i]
                si = i % 2
                pi_ = i % 3
                ksl = ksl_of(i)
                c0, c1 = cols_of(i)
                nc_ = c1 - c0
                qu0 = rho + D * (a - 64 + c0) - QC0
                qsl = slice(qu0, qu0 + D * (nc_ - 1) + 1, D)
                S2 = pbanks[si]
                for h in range(2):
                    P.add("pe", lambda e, h=h, hp=hp: e.matmul(S2[0:nk, h * 512 + c0:h * 512 + c1], lhsT=KT[h * 64:(h + 1) * 64, hp, ksl],
                                                        rhs=QT[h * 64:(h + 1) * 64, hp, qsl], start=True, stop=False, skip_group_check=True),
                          reads=[B_KT, B_QT], writes=[PB[2 * si + h]])
                for h in range(2):
                    P.add("pe", lambda e, h=h: e.matmul(S2[0:nk, h * 512 + c0:h * 512 + c1], lhsT=ident[0:nk, 0:nk], rhs=bandmask[0:nk, c0:c1],
                                                        start=False, stop=True, skip_group_check=True),
                          reads=[B_const], writes=[PB[2 * si + h]])
                S3 = S2.rearrange("p (h n) -> p h n", n=512)
                P.add("act", lambda e: e.activation(out=Pt[pi_][0:nk, :, c0:c1], in_=S3[0:nk, :, c0:c1], func=AF.Exp, scale=0.125,
                                                    bias=valcols[0:nk, idx:idx + 1]),
                      reads=[PB[2 * si], PB[2 * si + 1], B_const], writes=[B_Pt[pi_]])

            def stPV(i):
                (D, rho, sq0, nqb, qbs, j, a, nk, idx) = kts[i]
                pi_ = i % 3
                vs = i % 4
                halves = []
                if j >= 1:
                    halves.append((j - 1, 0, qbs[j - 1]))
                if j < nqb:
                    halves.append((j, 128, qbs[j]))
                for (m, cl, nqm) in halves:
                    ob = 4 + (m % 2)
                    O3 = bank(ob).rearrange("p (h n) -> p h n", n=256)
                    for h in range(2):
                        first = (m == j) and h == 0
                        P.add("pe", lambda e, O3=O3, h=h, nqm=nqm, cl=cl, first=first: e.matmul(
                            O3[:, h, 0:nqm], lhsT=Vts[vs][0:nk, h * 128:(h + 1) * 128], rhs=Pt[pi_][0:nk, h, cl:cl + nqm],
                            start=first, stop=False, skip_group_check=True),
                            reads=[B_Vts[vs], B_Vvs[vs], B_Pt[pi_]], writes=[PB[ob]])
                    if m == j - 1:
                        qi0 = rho + D * (sq0 + 128 * m) - UQ0
                        isl = slice(qi0, qi0 + D * (nqm - 1) + 1, D)
                        if D == 1:
                            P.add("dve", lambda e, O3=O3, isl=isl, nqm=nqm: e.tensor_copy(out=acc[:, :, isl], in_=O3[:, :, 0:nqm]), reads=[PB[ob]], writes=acc_bufs(isl))
                        else:
                            P.add("dve", lambda e, O3=O3, isl=isl, nqm=nqm: e.tensor_tensor(out=acc[:, :, isl], in0=O3[:, :, 0:nqm], in1=acc[:, :, isl], op=ALU.add),
                                  reads=[PB[ob]] + acc_bufs(isl), writes=acc_bufs(isl))

            stV(0); stV(1); stV(2); stS(0); stS(1)
            for i in range(NK):
                if i + 3 < NK:
                    stV(i + 3)
                if i + 2 < NK:
                    stS(i + 2)
                stPV(i)
            if hp == 3:
                R3 = Region(22528, 107520)
                w_out = view(R3.alloc(8 * 1024 * 2), 8192, BF16).rearrange("p (c n) -> p c n", n=1024)
                Dg = view(R3.alloc(62 * 128 * 2), 62 * 128, BF16).rearrange("p (k n) -> p k n", n=128)
                yf = [view(R3.alloc(2 * 512 * 4), 1024, F32).rearrange("p (c n) -> p c n", n=512) for _ in range(3)]
                ybf = [view(R3.alloc(2 * 512 * 2), 1024, BF16).rearrange("p (c n) -> p c n", n=512) for _ in range(3)]
                ysq = [view(R3.alloc(2 * 512 * 2), 1024, BF16).rearrange("p (c n) -> p c n", n=512) for _ in range(3)]
                mean_sb = [view(R3.alloc(2048), 512, F32) for _ in range(2)]
                var_sb = [view(R3.alloc(2048), 512, F32) for _ in range(2)]
                Pm = [view(R3.alloc(2 * 512 * 2), 1024, BF16).rearrange("p (h n) -> p h n", n=512) for _ in range(2)]
                rdm = [view(R3.alloc(2048), 512, F32) for _ in range(2)]
                osb = [view(R3.alloc(2048), 512, F32) for _ in range(2)]
                B_osb = [Buf("osb0"), Buf("osb1")]
                B_wout = Buf("wout"); B_Dg = Buf("Dg")
                B_yf = [[Buf("yf%d%d" % (i, j)) for j in range(2)] for i in range(3)]; B_ybf = [Buf("ybf0"), Buf("ybf1"), Buf("ybf2")]; B_ysq = [Buf("ysq0"), Buf("ysq1"), Buf("ysq2")]
                B_mean = [Buf("mean0"), Buf("mean1")]; B_var = [Buf("var0"), Buf("var1")]
                B_Pm = [Buf("Pm0"), Buf("Pm1")]; B_rdm = [Buf("rdm0"), Buf("rdm1")]
                B34 = B_osb + [B_Dg] + B_yf[0] + B_yf[1] + B_yf[2] + B_ybf + B_ysq + B_mean + B_var + B_Pm + B_rdm
                P.guard(writes=[B_KT, B_VT, B_QT, B_wout] + B34)
                dw2 = P.dsem("w_out")
                wo_src = w_out_d.rearrange("(c p) n -> p c n", p=128)
                for c in range(0, 8, 2):
                    P.add("pool", lambda e, c=c: e.dma_start(out=w_out[:, c:c + 2, :], in_=wo_src[:, c:c + 2, :]), writes=[B_wout], dsem=dw2)
                P.add("dve", lambda e: e.tensor_tensor(out=Dg, in0=ident.unsqueeze(1).to_broadcast([128, 62, 128]), in1=cw.unsqueeze(2).to_broadcast([128, 62, 128]), op=ALU.mult),
                      reads=[B_const], writes=[B_Dg])

            for k in range(5):
                cs_ = slice(512 * k, min(512 * k + 512, NQ))
                P.add("dve", lambda e, cs_=cs_: e.reciprocal(out=rden[0:64, cs_], in_=acc[64:128, 0, cs_]), reads=[B_accs[k]], writes=[B_rdens[k]])
                P.add("dve", lambda e, cs_=cs_: e.reciprocal(out=rden[64:128, cs_], in_=acc[0:64, 1, cs_]), reads=[B_accs[k]], writes=[B_rdens[k]])
                P.add("pool", lambda e, hp=hp, cs_=cs_: e.tensor_tensor(out=mixedT[0:64, hp, cs_], in0=acc[0:64, 0, cs_], in1=rden[0:64, cs_], op=ALU.mult),
                      reads=[B_accs[k], B_rdens[k]], writes=[B_mixed])
                P.add("pool", lambda e, hp=hp, cs_=cs_: e.tensor_tensor(out=mixedT[64:128, hp, cs_], in0=acc[64:128, 1, cs_], in1=rden[64:128, cs_], op=ALU.mult),
                      reads=[B_accs[k], B_rdens[k]], writes=[B_mixed])

        QBLK = [(0, 512), (512, 512), (1024, 512), (1536, 512), (2048, 2)]

        def conv_a(b, chns=(0, 1)):
            (qi0, n) = QBLK[b]
            s_ = b % 3
            for chn in chns:
                pb = bank(6 + chn)
                for k in range(31):
                    P.add("pe", lambda e, pb=pb, chn=chn, k=k: e.matmul(pb[:, 0:n], lhsT=Dg[:, chn * 31 + k, :], rhs=cT[:, chn, qi0 + 48 + k:qi0 + 48 + k + n],
                                                                        start=(k == 0), stop=(k == 30)),
                          reads=[B_Dg, B_cT], writes=[PB[6 + chn]])
                P.add("act", lambda e, pb=pb, chn=chn: e.activation(out=yf[s_][:, chn, 0:n], in_=pb[:, 0:n], func=AF.Identity, bias=cp[:, chn:chn + 1]),
                      reads=[PB[6 + chn], B_const], writes=[B_yf[s_][chn]])
                P.add("pool", lambda e, chn=chn: e.tensor_copy(out=ybf[s_][:, chn, 0:n], in_=yf[s_][:, chn, 0:n]), reads=[B_yf[s_][chn]], writes=[B_ybf[s_]])
                P.add("act", lambda e, chn=chn: e.activation(out=ysq[s_][:, chn, 0:n], in_=yf[s_][:, chn, 0:n], func=AF.Square), reads=[B_yf[s_][chn]], writes=[B_ysq[s_]])

        def conv_b(b):
            (qi0, n) = QBLK[b]
            s_ = b % 3
            m_ = b % 2
            for chn in range(2):
                P.add("pe", lambda e, chn=chn: e.matmul(bank(6)[:, 0:n], lhsT=onesm, rhs=ybf[s_][:, chn, 0:n], start=(chn == 0), stop=(chn == 1)),
                      reads=[B_ybf[s_], B_const], writes=[PB[6]])
            for chn in range(2):
                P.add("pe", lambda e, chn=chn: e.matmul(bank(7)[:, 0:n], lhsT=onesm, rhs=ysq[s_][:, chn, 0:n], start=(chn == 0), stop=(chn == 1)),
                      reads=[B_ysq[s_], B_const], writes=[PB[7]])
            mean = mean_sb[m_]; var = var_sb[m_]
            P.add("act", lambda e: e.activation(out=mean[:, 0:n], in_=bank(6)[:, 0:n], func=AF.Copy), reads=[PB[6]], writes=[B_mean[m_]])
            P.add("dve", lambda e: e.tensor_tensor(out=var[:, 0:n], in0=mean[:, 0:n], in1=mean[:, 0:n], op=ALU.mult), reads=[B_mean[m_]], writes=[B_var[m_]])
            P.add("dve", lambda e: e.tensor_tensor(out=var[:, 0:n], in0=bank(7)[:, 0:n], in1=var[:, 0:n], op=ALU.subtract), reads=[PB[7], B_var[m_]], writes=[B_var[m_]])
            P.add("act", lambda e: e.activation(out=var[:, 0:n], in_=var[:, 0:n], func=AF.Sqrt, bias=epsc), reads=[B_var[m_], B_const], writes=[B_var[m_]])
            P.add("dve", lambda e: e.reciprocal(out=var[:, 0:n], in_=var[:, 0:n]), reads=[B_var[m_]], writes=[B_var[m_]])
            for chn in range(2):
                eng = "dve" if chn == 0 else "pool"
                P.add(eng, lambda e, chn=chn: e.tensor_tensor(out=yf[s_][:, chn, 0:n], in0=yf[s_][:, chn, 0:n], in1=mean[:, 0:n], op=ALU.subtract),
                      reads=[B_yf[s_][chn], B_mean[m_]], writes=[B_yf[s_][chn]])
                P.add(eng, lambda e, chn=chn: e.tensor_tensor(out=yf[s_][:, chn, 0:n], in0=yf[s_][:, chn, 0:n], in1=var[:, 0:n], op=ALU.mult),
                      reads=[B_yf[s_][chn], B_var[m_]], writes=[B_yf[s_][chn]])
                P.add("act", lambda e, chn=chn: e.activation(out=mixedT[:, 4 + chn, qi0:qi0 + n], in_=yf[s_][:, chn, 0:n], func=AF.Silu,
                                                             scale=cp[:, 2 + chn:3 + chn], bias=cp[:, 4 + chn:5 + chn]),
                      reads=[B_yf[s_][chn], B_const], writes=[B_mixed])

        def memS(hp, b):
            (qi0, n) = QBLK[b]
            for mt in range(2):
                S2 = pbanks[mt]
                for h in range(2):
                    P.add("pe", lambda e, S2=S2, h=h, mt=mt: e.matmul(
                        S2[:, h * 512:h * 512 + n], lhsT=KmT[h * 64:(h + 1) * 64, hp, 128 * mt:128 * mt + 128],
                        rhs=QmT[h * 64:(h + 1) * 64, hp, qi0 + UQ0 - QC0:qi0 + UQ0 - QC0 + n], start=True, stop=True),
                        reads=[B_KmT, B_QmT], writes=[PB[2 * mt + h]])
                S3 = S2.rearrange("p (h n) -> p h n", n=512)
                P.add("act", lambda e, S3=S3, mt=mt: e.activation(out=Pm[mt][:, :, 0:n], in_=S3[:, :, 0:n], func=AF.Exp, scale=0.125),
                      reads=[PB[2 * mt], PB[2 * mt + 1]], writes=[B_Pm[mt]])

        def memPV(hp, b):
            (qi0, n) = QBLK[b]
            for h in range(2):
                for mt in range(2):
                    P.add("pe", lambda e, h=h, mt=mt: e.matmul(bank(4 + h)[:, 0:n], lhsT=VmA[:, mt, hp * 256 + h * 128:hp * 256 + h * 128 + 128],
                                                               rhs=Pm[mt][:, h, 0:n], start=(mt == 0), stop=(mt == 1)),
                          reads=[B_VmA, B_Pm[mt]], writes=[PB[4 + h]])
            for h in range(2):
                P.add("act", lambda e, h=h: e.activation(out=osb[h][:, 0:n], in_=bank(4 + h)[:, 0:n], func=AF.Copy), reads=[PB[4 + h]], writes=[B_osb[h]])
            P.add("dve", lambda e: e.reciprocal(out=rdm[0][0:64, 0:n], in_=osb[0][64:128, 0:n]), reads=[B_osb[0]], writes=[B_rdm[0]])
            P.add("dve", lambda e: e.reciprocal(out=rdm[1][64:128, 0:n], in_=osb[1][0:64, 0:n]), reads=[B_osb[1]], writes=[B_rdm[1]])
            P.add("pool", lambda e: e.tensor_tensor(out=mixedT[0:64, 6 + hp, qi0:qi0 + n], in0=osb[0][0:64, 0:n], in1=rdm[0][0:64, 0:n], op=ALU.mult),
                  reads=[B_osb[0], B_rdm[0]], writes=[B_mixed])
            P.add("pool", lambda e: e.tensor_tensor(out=mixedT[64:128, 6 + hp, qi0:qi0 + n], in0=osb[1][64:128, 0:n], in1=rdm[1][64:128, 0:n], op=ALU.mult),
                  reads=[B_osb[1], B_rdm[1]], writes=[B_mixed])

        conv_a(0)
        conv_a(1)
        for b in range(5):
            memS(0, b)
            if b + 2 < 5:
                conv_a(b + 2, (0,))
            memPV(0, b)
            memS(1, b)
            if b + 2 < 5:
                conv_a(b + 2, (1,))
            memPV(1, b)
            conv_b(b)

        if DEBUG:
            dd = P.dsem("dbg")
            dbgf = view(128000, 8 * NQ, F32)
            P.add("dve", lambda e: e.tensor_copy(out=dbgf, in_=mixedT.rearrange("p c q -> p (c q)")), reads=[B_mixed, B_acc, B_rden], writes=[B_acc, B_rden])
            P.add("sp", lambda e: e.dma_start(out=dbg["mixedT"], in_=dbgf), reads=[B_acc], dsem=dd)

        R5 = Region(38912, 172000 - 32)
        xr = [view(R5.alloc(4096), 1024, F32) for _ in range(2)] + [view(165984, 1024, F32)]
        hmh = view(R5.alloc(4096), 1024, F32)
        hn2T = view(R5.alloc(8 * NQ * 2), 8 * NQ, BF16).rearrange("p (c q) -> p c q", q=NQ)
        hmid = view(R5.alloc(16 * 4096), 16 * 1024, F32).rearrange("p (t n) -> p t n", n=1024)
        B_xr = [Buf("xr0"), Buf("xr1"), Buf("xr2")]; B_hmh = Buf("hmh"); B_hn2T = Buf("hn2T"); B_hmid = [Buf("hmid%d" % i) for i in range(16)]
        dead34 = B34 + [B_QmT, B_cT, B_KmT, B_VmA] + B_accs + B_rdens + B_Pt + B_Vts + B_Vvs
        P.guard(writes=dead34 + B_xr + [B_hmh, B_hn2T] + B_hmid)
        R6w = Region(149600, 149600 + 16384)
        wup = [view(R6w.alloc(8 * 512 * 2), 4096, BF16).rearrange("p (c n) -> p c n", n=512) for _ in range(2)]
        B_wup = [Buf("wup%d" % i) for i in range(2)]
        dwu = [P.dsem("wup%d" % i) for i in range(2)]
        wu_src = w_up_d.rearrange("(c p) n -> p c n", p=128)

        def load_pp(pp):
            s_ = pp % 2
            P.add("pool", lambda e, s_=s_, pp=pp: e.dma_start(out=wup[s_][:, :, 0:256], in_=wu_src[:, :, 256 * pp:256 * pp + 256]), writes=[B_wup[s_]], dsem=dwu[s_])
            P.add("pool", lambda e, s_=s_, pp=pp: e.dma_start(out=wup[s_][:, :, 256:512], in_=wu_src[:, :, DFF + 256 * pp:DFF + 256 * pp + 256]), writes=[B_wup[s_]], dsem=dwu[s_])

        P.guard(writes=B_accs + B_rdens + B_Pt + B_Vts + B_Vvs + B_wup)
        load_pp(0)
        load_pp(1)
        dg2 = P.dsem("g2")
        P.add("sp", lambda e: e.dma_start(out=grep, in_=g2_d[0:1, :].to_broadcast([128, 1024])), writes=[B_grep], dsem=dg2)
        dxr = [P.dsem("xr0"), P.dsem("xr1"), P.dsem("xr2")]
        def p5_geom(i):
            if i < 16:
                q0 = 1 + 128 * i
                return 128, slice(q0, q0 + 128), hmid[:, i, :], B_hmid[i]
            return 2, slice(0, NQ, NQ - 1), hmh, B_hmh

        P5B = [(0, 1), (2, 3), (6, 7)]
        def p5_mm(i):
            xs = i % 3
            n, lsl, hdst, B_h = p5_geom(i)
            if i < 16:
                P.add("sp", lambda e: e.dma_start(out=xr[xs], in_=x_win[UQ0 + 1 + 128 * i:UQ0 + 1 + 128 * i + 128, :]), writes=[B_xr[xs]], dsem=dxr[xs])
            else:
                P.add("sp", lambda e: e.dma_start(out=xr[xs][0:1, :], in_=x_win[UQ0:UQ0 + 1, :]), writes=[B_xr[xs]], dsem=dxr[xs])
                P.add("sp", lambda e: e.dma_start(out=xr[xs][1:2, :], in_=x_win[UQ0 + NQ - 1:UQ0 + NQ, :]), writes=[B_xr[xs]], dsem=dxr[xs])
            for half in range(2):
                pb = bank(P5B[i % 3][half])
                for c in range(8):
                    P.add("pe", lambda e, pb=pb, c=c, half=half: e.matmul(pb[0:n, :], lhsT=mixedT[:, c, lsl], rhs=w_out[:, c, half * 512:half * 512 + 512],
                                                                         start=(c == 0), stop=(c == 7)),
                          reads=[B_mixed, B_wout], writes=[PB[P5B[i % 3][half]]])
                P.add("dve", lambda e, pb=pb, half=half: e.tensor_tensor(out=hdst[0:n, half * 512:half * 512 + 512], in0=pb[0:n, :],
                                                                        in1=xr[xs][0:n, half * 512:half * 512 + 512], op=ALU.add),
                      reads=[PB[P5B[i % 3][half]], B_xr[xs]], writes=[B_h])
            rms_stats_hn(hdst[0:n, :], n, B_h, xs, grep, B_grep)

        def p5_tr(i):
            xs = i % 3
            n, lsl, hdst, B_h = p5_geom(i)
            hn_transpose(n, xs, hn2T[:, :, lsl], [B_hn2T], 4 + (i % 2))
            if i == 16:
                P.add("pool", lambda e: e.tensor_tensor(out=hn2T[:, :, lsl], in0=hn2T[:, :, lsl], in1=hv.unsqueeze(1).to_broadcast([128, 8, 2]), op=ALU.mult),
                      reads=[B_hn2T, B_const], writes=[B_hn2T])

        p5_mm(0)
        p5_mm(1)
        for i in range(17):
            if i + 2 < 17:
                p5_mm(i + 2)
            p5_tr(i)

        if DEBUG:
            P.add("sp", lambda e: e.dma_start(out=dbg["hmid"], in_=hmid.rearrange("p t n -> p (t n)")), reads=B_hmid, dsem=dd)

        R6a = Region(22528, 51200)
        R6b = Region(171968, TOT)
        tg = [view(R6a.alloc(416 * 4), 416, F32) for _ in range(2)]
        tu = [view(R6a.alloc(416 * 4), 416, F32) for _ in range(2)]
        sg = [view(R6a.alloc(416 * 4), 416, F32) for _ in range(2)]
        actT = [None, None]
        actT[0] = view(R6a.alloc(4 * 2048 * 2), 4 * 2048, BF16).rearrange("p (k n) -> p k n", n=2048)
        actT[1] = view(R6b.alloc(4 * 2048 * 2), 4 * 2048, BF16).rearrange("p (k n) -> p k n", n=2048)
        wdn = [view(R6b.alloc(4 * 1024 * 2), 4 * 1024, BF16).rearrange("p (k n) -> p k n", n=1024) for _ in range(2)]
        B_tg = [Buf("tg0"), Buf("tg1")]; B_tu = [Buf("tu0"), Buf("tu1")]; B_sg = [Buf("sg0"), Buf("sg1")]
        B_actT = [Buf("actT0"), Buf("actT1")]; B_wdn = [Buf("wdn0"), Buf("wdn1")]
        P.guard(writes=[B_wout, B_mixed, B_hmh] + B_xr + B_tg + B_tu + B_sg + B_actT + B_wdn)
        dwd = [P.dsem("wdn%d" % i) for i in range(2)]
        wd_src = w_down_d.rearrange("(k p) n -> p k n", p=128)
        CHK = [(0, 410), (410, 410), (820, 410), (1230, 410), (1640, 408)]
        dout = P.dsem("out")
        out_ops = []
        pair_no = 0
        tcnt = 0

        def up_group(gi, pend=()):
            nonlocal tcnt
            pend = list(pend)
            (p0, p1) = FFN_GROUPS[gi]
            gs = gi % 2
            P.add("pool", lambda e, gs=gs, p0=p0, p1=p1: e.dma_start(out=wdn[gs][:, 0:p1 - p0, :], in_=wd_src[:, p0:p1, :]), writes=[B_wdn[gs]], dsem=dwd[gs])
            for pn in range(p0, p1):
                s_ = (pn // 2) % 2
                wo_ = 128 * (pn % 2)
                if pn % 2 == 0 and pn >= 2 and pn // 2 + 1 < NPAIR // 2:
                    load_pp(pn // 2 + 1)
                for (off, n) in CHK:
                    ts = tcnt % 2
                    tcnt += 1
                    bgk, buk = [(4, 5), (6, 7)][(tcnt - 1) % 2]
                    for (bk, wc) in [(bgk, wo_), (buk, 256 + wo_)]:
                        for c in range(8):
                            P.add("pe", lambda e, bk=bk, wc=wc, c=c, s_=s_, off=off, n=n: e.matmul(bank(bk)[:, 0:n + 2], lhsT=wup[s_][:, c, wc:wc + 128],
                                                                                                rhs=hn2T[:, c, off:off + n + 2], start=(c == 0), stop=(c == 7)),
                                  reads=[B_wup[s_], B_hn2T], writes=[PB[bk]])
                    halves_ = [(bgk, tg[ts], B_tg[ts], pn), (buk, tu[ts], B_tu[ts], NPAIR + pn)]
                    for (bk, tbuf, B_t, widx) in halves_:
                        P.add("act", lambda e, bk=bk, tbuf=tbuf, widx=widx, n=n: e.activation(out=tbuf[:, 0:n], in_=bank(bk)[:, 0:n], func=AF.Identity,
                                                                                           scale=fw[:, 3 * widx:3 * widx + 1], bias=fb[:, widx:widx + 1]),
                              reads=[PB[bk], B_const], writes=[B_t])
                    for tap in (1, 2):
                        for (bk, tbuf, B_t, widx) in halves_:
                            P.add("dve", lambda e, bk=bk, tbuf=tbuf, widx=widx, n=n, tap=tap: e.scalar_tensor_tensor(
                                out=tbuf[:, 0:n], in0=bank(bk)[:, tap:n + tap], scalar=fw[:, 3 * widx + tap:3 * widx + tap + 1],
                                in1=tbuf[:, 0:n], op0=ALU.mult, op1=ALU.add),
                                reads=[PB[bk], B_const, B_t], writes=[B_t])
                    P.add("act", lambda e, ts=ts, n=n: e.activation(out=sg[ts][:, 0:n], in_=tg[ts][:, 0:n], func=AF.Silu), reads=[B_tg[ts]], writes=[B_sg[ts]])
                    P.add("dve", lambda e, ts=ts, n=n, gs=gs, pn=pn, p0=p0, off=off: e.tensor_tensor(out=actT[gs][:, pn - p0, off:off + n], in0=sg[ts][:, 0:n], in1=tu[ts][:, 0:n], op=ALU.mult),
                          reads=[B_sg[ts], B_tu[ts]], writes=[B_actT[gs]])
                    if pend:
                        down_tile(*pend.pop(0))
            for rest_ in pend:
                down_tile(*rest_)

        def down_group(gi):
            for i in range(16):
                down_tile(gi, i)

        def down_tile(gi, i):
            (p0, p1) = FFN_GROUPS[gi]
            gs = gi % 2
            last = (gi == len(FFN_GROUPS) - 1)
            if True:
                for half in range(2):
                    bk = 2 * (i % 2) + half
                    for k in range(p1 - p0):
                        P.add("pe", lambda e, bk=bk, k=k, gs=gs, i=i, half=half, p0=p0, p1=p1: e.matmul(bank(bk)[:, :], lhsT=actT[gs][:, k, 128 * i:128 * i + 128],
                                                                                                   rhs=wdn[gs][:, k, half * 512:half * 512 + 512],
                                                                                                   start=(k == 0), stop=(k == p1 - p0 - 1)),
                              reads=[B_actT[gs], B_wdn[gs]], writes=[PB[bk]])
                P.add("dve", lambda e, i=i: e.tensor_tensor(out=hmid[:, i, :], in0=pbanks[i % 2][:, :], in1=hmid[:, i, :], op=ALU.add),
                      reads=[PB[2 * (i % 2)], PB[2 * (i % 2) + 1], B_hmid[i]], writes=[B_hmid[i]])
                if last:
                    out_ops.append(P.add("sp", lambda e, i=i: e.dma_start(out=y_d[128 * i:128 * i + 128, :], in_=hmid[:, i, :]), reads=[B_hmid[i]], dsem=dout))

        ng_ = len(FFN_GROUPS)
        up_group(0)
        for gi in range(1, ng_):
            up_group(gi, [(gi - 1, i) for i in range(16)])
        down_group(ng_ - 1)
        fin = P.add("sp", lambda e: e.nop())
        fin.deps.extend(out_ops)

        P.emit()
    return nc


_NC_CACHE = {}


def _get_nc():
    if "nc" not in _NC_CACHE:
        _NC_CACHE["nc"] = build_program()
    return _NC_CACHE["nc"]


def kernel(x, mem, positions, mix_norm_g, mem_norm_g, w_in, w_mem_kv, q_norm_g, k_norm_g, mq_norm_g, mk_norm_g,
           conv_dw_w, conv_dw_b, conv_ln_g, conv_ln_b, w_out, ffn_norm_g, w_up, ffn_dw_w, ffn_dw_b, w_down):
    f32 = np.float32
    x = np.asarray(x, f32); mem = np.asarray(mem, f32); positions = np.asarray(positions, np.int32)
    PT, NVT = pattern_tiles()
    ident = np.eye(128, dtype=f32)
    pp = np.arange(128)[:, None]; cc = np.arange(256)[None, :]
    bandmask = np.where((cc >= pp) & (cc <= pp + 128), 0.0, -30000.0).astype(f32)
    invf = (f32(500000.0) ** (-(np.arange(0, 16, 2, dtype=f32)) / f32(16))).astype(f32)
    invf = np.ascontiguousarray(np.broadcast_to(invf[None, :], (128, 8)))
    hg = np.concatenate([np.asarray(g, f32).reshape(-1) for g in (q_norm_g, k_norm_g, mq_norm_g, mk_norm_g)])[None, :]
    cwv = np.asarray(conv_dw_w, f32)[0]
    cw = np.ascontiguousarray(cwv.T.reshape(2, 128, 31).transpose(1, 0, 2).reshape(128, 62))
    def pc(v):
        return np.asarray(v, f32).reshape(2, 128).T
    cp = np.ascontiguousarray(np.concatenate([pc(conv_dw_b[0]), pc(conv_ln_g[0]), pc(conv_ln_b[0])], axis=1))
    fwv = np.asarray(ffn_dw_w, f32)[0]
    fw = np.ascontiguousarray(fwv.T.reshape(44, 128, 3).transpose(1, 0, 2).reshape(128, 132))
    fb = np.ascontiguousarray(np.asarray(ffn_dw_b, f32)[0].reshape(44, 128).T)
    shared = {
        "ident": ident, "bandmask": bandmask, "invf": invf,
        "g1": np.asarray(mix_norm_g, f32).reshape(1, 1024), "g2": np.asarray(ffn_norm_g, f32).reshape(1, 1024),
        "gm": np.asarray(mem_norm_g, f32).reshape(1, 1024), "hg": np.ascontiguousarray(hg),
        "w_in": np.asarray(w_in, f32)[0], "w_mkv": np.asarray(w_mem_kv, f32)[0], "w_out": np.asarray(w_out, f32)[0],
        "w_up": np.asarray(w_up, f32)[0], "w_down": np.asarray(w_down, f32)[0],
        "cw": cw, "cp": cp, "fw": fw, "fb": fb,
    }
    in_maps = []
    for ci in range(8):
        b = ci // 4
        T0 = (ci % 4) * 2048
        t_start = T0 - 1088
        tt = t_start + np.arange(NWIN)
        ok = (tt >= 0) & (tt < S)
        xw = np.zeros((NWIN, 1024), f32)
        xw[ok] = x[b, tt[ok]]
        pw = np.zeros((NWIN,), np.int32)
        pw[ok] = positions[b, tt[ok]]
        pos_win = np.ascontiguousarray(pw.reshape(NT, 128).T)
        val = np.zeros((128, NVT), f32)
        for (D, rho, sq0, nq, nqb, tiles) in PT:
            for (a, nk, idx) in tiles:
                u = rho + D * (a + np.arange(128))
                t = t_start + u
                val[:, idx] = np.where((t >= 0) & (t < S) & (np.arange(128) < nk), 0.0, -30000.0).astype(f32)
        m = dict(shared)
        hvv = np.zeros((128, 2), f32)
        hvv[:, 0] = 1.0 if (T0 - 1) >= 0 else 0.0
        hvv[:, 1] = 1.0 if (T0 + 2048) < S else 0.0
        m.update({"x_win": xw, "pos_win": pos_win, "valcols": val, "mem": np.ascontiguousarray(mem[b]), "hv": hvv})
        in_maps.append(m)
    nc = _get_nc()
    res = run_bass_kernel_spmd(nc, in_maps, core_ids=list(range(8)))
    out = np.zeros((2, S, 1024), f32)
    for ci in range(8):
        b = ci // 4
        T0 = (ci % 4) * 2048
        out[b, T0:T0 + 2048] = res.results[ci]["y"]
    if DEBUG:
        kernel.last_results = res.results
    return out
```

```python
import contextlib
import numpy as np
import concourse.bass as bass
import concourse.mybir as mybir
from concourse.bass_utils import run_bass_kernel_spmd

F32 = mybir.dt.float32
BF16 = mybir.dt.bfloat16
I32 = mybir.dt.int32
ALU = mybir.AluOpType
AF = mybir.ActivationFunctionType
AX = mybir.AxisListType

S = 8192
NWIN = 4224
NT = 33
UQ0 = 1087
NQ = 2050
QC0 = 1024
NQC = 2176
EPS = 1e-6
DFF = 2816
NPAIR = 22
FFN_GROUPS = [(0, 3), (3, 7), (7, 11), (11, 15), (15, 19), (19, 22)]
DEBUG = False

ENGS = ("pe", "act", "dve", "pool", "sp")


class Buf:
    __slots__ = ("name", "lw", "rd")

    def __init__(self, name):
        self.name = name
        self.lw = None
        self.rd = []


class Op:
    __slots__ = ("eng", "fn", "deps", "signal", "semval", "dsem", "is_dma", "virtual")

    def __init__(self, eng, fn, is_dma=False, dsem=None):
        self.eng = eng
        self.fn = fn
        self.deps = []
        self.signal = False
        self.semval = None
        self.dsem = dsem
        self.is_dma = is_dma
        self.virtual = False


class DSem:
    def __init__(self, name):
        self.name = name
        self.count = 0
        self.h = None


class Prog:
    def __init__(self, nc):
        self.nc = nc
        self.ops = {e: [] for e in ENGS}
        self.dsems = []

    def dsem(self, name):
        d = DSem(name)
        self.dsems.append(d)
        return d

    def add(self, eng, fn, reads=(), writes=(), dsem=None):
        is_dma = dsem is not None
        op = Op(eng, fn, is_dma=is_dma, dsem=dsem)
        deps = []
        for b in reads:
            if b.lw is not None:
                deps.append(b.lw)
        for b in writes:
            if b.lw is not None:
                deps.append(b.lw)
            deps.extend(b.rd)
        seen = set()
        for d in deps:
            if d is op or id(d) in seen:
                continue
            seen.add(id(d))
            if (not d.is_dma) and d.eng == "pe" and eng == "pe" and not is_dma:
                continue
            op.deps.append(d)
        for b in reads:
            b.rd.append(op)
        for b in writes:
            b.lw = op
            b.rd = []
        self.ops[eng].append(op)
        if is_dma:
            dsem.count += 1
            op.semval = 16 * dsem.count
        return op

    def guard(self, writes=(), extra_deps=()):
        op = Op(None, None)
        op.virtual = True
        seen = set()
        for b in writes:
            for d in ([b.lw] if b.lw is not None else []) + list(b.rd):
                if id(d) not in seen:
                    seen.add(id(d))
                    op.deps.append(d)
        for d in extra_deps:
            if id(d) not in seen:
                seen.add(id(d))
                op.deps.append(d)
        for b in writes:
            b.lw = op
            b.rd = []
        return op

    def _flatten(self):
        memo = {}

        def flat(op):
            k = id(op)
            if k in memo:
                return memo[k]
            out, seen = [], set()
            for d in op.deps:
                for r in (flat(d) if getattr(d, "virtual", False) else [d]):
                    if id(r) not in seen:
                        seen.add(id(r))
                        out.append(r)
            memo[k] = out
            return out

        for e in ENGS:
            for op in self.ops[e]:
                if any(getattr(d, "virtual", False) for d in op.deps):
                    real = []
                    seen = set()
                    for d in op.deps:
                        for r in (flat(d) if getattr(d, "virtual", False) else [d]):
                            if r is op or id(r) in seen:
                                continue
                            if (not r.is_dma) and r.eng == "pe" and op.eng == "pe" and not op.is_dma:
                                continue
                            seen.add(id(r))
                            real.append(r)
                    op.deps = real

    def emit(self):
        nc = self.nc
        self._flatten()
        for e in ENGS:
            for op in self.ops[e]:
                for d in op.deps:
                    if not d.is_dma:
                        d.signal = True
        for e in ENGS:
            c = 0
            for op in self.ops[e]:
                if op.is_dma:
                    continue
                if op.signal:
                    c += 1
                    op.semval = c
        with contextlib.ExitStack() as st:
            esem = {e: st.enter_context(nc.semaphore("s_" + e)) for e in ENGS}
            for d in self.dsems:
                d.h = st.enter_context(nc.semaphore("d_" + d.name))
            block = st.enter_context(nc.Block())

            def run(eng_name):
                def body(eng):
                    waited = {}
                    for op in self.ops[eng_name]:
                        need = {}
                        for d in op.deps:
                            if d.is_dma:
                                key = ("d", id(d.dsem)); h = d.dsem.h
                            else:
                                key = ("e", d.eng); h = esem[d.eng]
                            if key not in need or need[key][1] < d.semval:
                                need[key] = (h, d.semval)
                        for key, (h, val) in need.items():
                            if waited.get(key, 0) >= val:
                                continue
                            eng.wait_ge(h, val)
                            waited[key] = val
                        ins = op.fn(eng)
                        if op.is_dma:
                            ins.then_inc(op.dsem.h, 16)
                        elif op.signal:
                            ins.then_inc(esem[eng_name], 1)
                return body

            block.sync(run("sp"))
            block.tensor(run("pe"))
            block.scalar(run("act"))
            block.vector(run("dve"))
            block.gpsimd(run("pool"))


def pattern_tiles():
    res = []
    idx = 0
    for D in (1, 4, 16):
        for rho in range(D):
            sq0 = -((-(UQ0 - rho)) // D)
            sq1 = (UQ0 + NQ - 1 - rho) // D
            nq = sq1 - sq0 + 1
            nqb = (nq + 127) // 128
            tiles = []
            for j in range(nqb + 1):
                a = sq0 - 64 + 128 * j
                nk = min(128, sq1 + 64 - a + 1)
                tiles.append((a, nk, idx))
                idx += 1
            res.append((D, rho, sq0, nq, nqb, tiles))
    return res, idx


def build_program():
    nc = bass.Bass("TRN2", target_bir_lowering=False)
    P = Prog(nc)
    PT, NVT = pattern_tiles()

    def din(name, shape, dt=F32):
        return nc.dram_tensor(name, list(shape), dt, kind="ExternalInput").ap()

    x_win = din("x_win", [NWIN, 1024])
    pos_win = din("pos_win", [128, NT], I32)
    valcols_d = din("valcols", [128, NVT])
    mem_d = din("mem", [256, 1024])
    ident_d = din("ident", [128, 128])
    bandmask_d = din("bandmask", [128, 256])
    invf_d = din("invf", [128, 8])
    g1_d = din("g1", [1, 1024])
    g2_d = din("g2", [1, 1024])
    gm_d = din("gm", [1, 1024])
    hg_d = din("hg", [1, 256])
    w_in_d = din("w_in", [1024, 2304])
    w_mkv_d = din("w_mkv", [1024, 512])
    w_out_d = din("w_out", [1024, 1024])
    w_up_d = din("w_up", [1024, 2 * DFF])
    w_down_d = din("w_down", [DFF, 1024])
    cw_d = din("cw", [128, 62])
    cp_d = din("cp", [128, 6])
    fw_d = din("fw", [128, 44 * 3])
    fb_d = din("fb", [128, 44])
    hv_d = din("hv", [128, 2])
    y_d = nc.dram_tensor("y", [2048, 1024], F32, kind="ExternalOutput").ap()
    dbg = {}
    if DEBUG:
        dbg["mixedT"] = nc.dram_tensor("dbg_mixedT", [128, 8 * NQ], F32, kind="ExternalOutput").ap()
        dbg["hmid"] = nc.dram_tensor("dbg_hmid", [128, 16 * 1024], F32, kind="ExternalOutput").ap()

    st = contextlib.ExitStack()
    with st:
        ARENA_KB = 204
        arena = st.enter_context(nc.sbuf_tensor("arena", [128, ARENA_KB * 512], BF16))
        pbanks = [st.enter_context(nc.psum_tensor("pq%d" % i, [128, 1024], F32)) for i in range(4)]

        class Region:
            def __init__(self, start, end):
                self.pos = start
                self.end = end

            def alloc(self, nbytes):
                nbytes = (nbytes + 31) // 32 * 32
                off = self.pos
                self.pos += nbytes
                assert self.pos <= self.end, ("arena overflow", self.pos, self.end)
                return off

        def view(off, n, dt):
            if dt == BF16:
                return arena[:, off // 2: off // 2 + n]
            return arena[:, off // 2: off // 2 + 2 * n].bitcast(dt)

        TOT = ARENA_KB * 1024
        RC = Region(0, 22528)
        RM = Region(172000 - 32, TOT)

        def bank(i, dt=F32):
            t = pbanks[i // 2][:, (i % 2) * 512:(i % 2) * 512 + 512]
            return t if dt == F32 else t.bitcast(dt)

        PB = [Buf("psum%d" % i) for i in range(8)]

        ident = view(RC.alloc(256), 128, BF16)
        bandmask = view(RC.alloc(512), 256, BF16)
        onesm = view(RC.alloc(256), 128, BF16)
        valcols = view(RC.alloc(NVT * 4), NVT, F32)
        invf = view(RC.alloc(32), 8, F32)
        hg = view(RC.alloc(1024), 256, F32)
        cs_tab = view(RC.alloc(NT * 8 * 4), NT * 8, F32)
        sn_tab = view(RC.alloc(NT * 8 * 4), NT * 8, F32)
        cw = view(RC.alloc(62 * 4), 62, F32)
        cp = view(RC.alloc(32), 6, F32)
        fw = view(RC.alloc(132 * 4), 132, F32)
        fb = view(RC.alloc(44 * 4), 44, F32)
        grep = view(RC.alloc(4096), 1024, F32)
        stat = view(RC.alloc(128 * 4), 128, F32)
        epsc = view(RC.alloc(32), 1, F32)
        onesb = view(RC.alloc(256), 128, BF16)
        hv = view(RC.alloc(32), 2, F32)
        junk = view(RC.alloc(2048), 1024, BF16)
        hnb = [view(RC.alloc(2048), 1024, BF16) for _ in range(3)]
        B_const = Buf("const")
        B_grep = Buf("grep")
        B_tab = Buf("tab")

        mixedT = view(RM.alloc(8 * NQ * 2), 8 * NQ, BF16).rearrange("p (c q) -> p c q", q=NQ)
        B_mixed = Buf("mixedT")

        const_ops = []
        for k_, (dst, src) in enumerate([(ident, ident_d), (bandmask, bandmask_d)]):
            const_ops.append(P.add("pool", lambda e, dst=dst, src=src: e.dma_start(out=dst, in_=src), writes=[Buf("c")], dsem=P.dsem("cp%d" % k_)))
        for k_, (dst, src) in enumerate([(valcols, valcols_d), (invf, invf_d), (cw, cw_d), (cp, cp_d), (fw, fw_d), (fb, fb_d), (hv, hv_d)]):
            const_ops.append(P.add("sp", lambda e, dst=dst, src=src: e.dma_start(out=dst, in_=src), writes=[Buf("c")], dsem=P.dsem("cs%d" % k_)))
        const_ops.append(P.add("sp", lambda e: e.dma_start(out=hg, in_=hg_d[0:1, :].to_broadcast([128, 256])), writes=[Buf("c")], dsem=P.dsem("chg")))
        P.add("sp", lambda e: e.dma_start(out=grep, in_=g1_d[0:1, :].to_broadcast([128, 1024])), writes=[B_grep], dsem=P.dsem("cg1"))
        P.guard(writes=[B_const], extra_deps=const_ops)
        P.add("pool", lambda e: e.memset(onesm, 1.0 / 256.0), writes=[B_const])
        P.add("pool", lambda e: e.memset(epsc, EPS), writes=[B_const])
        P.add("pool", lambda e: e.memset(onesb, 1.0), writes=[B_const])

        R1 = Region(22528, TOT)
        KT = view(R1.alloc(4 * NWIN * 2), 4 * NWIN, BF16).rearrange("p (c n) -> p c n", n=NWIN)
        VT = view(R1.alloc(4 * NWIN * 2), 4 * NWIN, BF16).rearrange("p (c n) -> p c n", n=NWIN)
        QT = view(R1.alloc(4 * NQC * 2), 4 * NQC, BF16).rearrange("p (c n) -> p c n", n=NQC)
        QmT = view(R1.alloc(2 * NQC * 2), 2 * NQC, BF16).rearrange("p (c n) -> p c n", n=NQC)
        cT = view(R1.alloc(2 * NQC * 2), 2 * NQC, BF16).rearrange("p (c n) -> p c n", n=NQC)
        KmT = view(R1.alloc(2 * 256 * 2), 512, BF16).rearrange("p (c n) -> p c n", n=256)
        VmA = view(R1.alloc(2 * 512 * 2), 1024, BF16).rearrange("p (t n) -> p t n", n=512)
        B_KmT = Buf("KmT"); B_VmA = Buf("VmA")
        p1_mark = R1.pos
        assert p1_mark == 128000, p1_mark
        w_in = view(R1.alloc(8 * 2304 * 2), 8 * 2304, BF16).rearrange("p (c n) -> p c n", n=2304)
        RW = Region(22528 + 2 * 33792, 22528 + 2 * 33792 + 17408)
        memT = view(RW.alloc(8 * 128 * 2), 1024, BF16).rearrange("p (c n) -> p c n", n=128)
        w_mkv = view(RW.alloc(8 * 512 * 2), 4096, BF16).rearrange("p (c n) -> p c n", n=512)
        gmrep = view(RW.alloc(4096), 1024, F32)
        B_memT = Buf("memT"); B_wmkv = Buf("wmkv"); B_gm = Buf("gmrep")
        hnT = [view(R1.alloc(8 * 256 * 2), 8 * 256, BF16).rearrange("p (c n) -> p c n", n=256) for _ in range(2)]
        xt = [view(R1.alloc(4096), 1024, F32) for _ in range(2)]
        class HN:
            pass
        def mk_ctx(name, W, rotary, scol):
            c = HN()
            c.name = name
            c.stg = view(R1.alloc(2048), 512, F32)
            c.qb = view(R1.alloc(1024), 512, BF16)
            c.rot = view(R1.alloc(1024), 256, F32) if rotary else None
            c.B_stg = Buf(name + "_stg"); c.B_qb = Buf(name + "_qb"); c.B_rot = Buf(name + "_rot"); c.B_st = Buf(name + "_st")
            c.ssq = stat[:, scol:scol + 8]
            c.rs = stat[:, scol + 8:scol + 16]
            return c
        ctx_k = [mk_ctx("k0", 512, True, 8), mk_ctx("k1", 512, True, 24)]
        ctx_qs = [mk_ctx("q0", 512, True, 40), mk_ctx("q1", 512, True, 72)]
        ctx_qms = [mk_ctx("qm0", 512, False, 56), mk_ctx("qm1", 512, False, 88)]
        ctx_q = ctx_qs[0]; ctx_qm = ctx_qms[0]
        ALL_CTX = ctx_k + ctx_qs + ctx_qms
        print("R1 end before misc", R1.pos, TOT)
        sig = view(R1.alloc(1024), 256, F32)
        posi = view(R1.alloc(NT * 4), NT, I32)
        posf = view(R1.alloc(NT * 4), NT, F32)
        ang = view(R1.alloc(NT * 8 * 4), NT * 8, F32)
        angi = view(R1.alloc(NT * 8 * 4), NT * 8, I32)
        angf = view(R1.alloc(NT * 8 * 4), NT * 8, F32)

        B_wins = [Buf("w_in%d" % i) for i in range(8)]; B_win = B_wins[0]; B_KT = Buf("KT"); B_VT = Buf("VT"); B_QT = Buf("QT"); B_QmT = Buf("QmT"); B_cT = Buf("cT")
        B_hnT = [[Buf("hnT00"), Buf("hnT01")], [Buf("hnT10"), Buf("hnT11")]]; B_xt = [Buf("xt0"), Buf("xt1")]; B_hnb = [Buf("hnb0"), Buf("hnb1"), Buf("hnb2")]
        B_sig = Buf("sig"); B_rst = [Buf("rst0"), Buf("rst1"), Buf("rst2")]; B_pos = Buf("pos")


        dpos = P.dsem("pos")
        P.add("sp", lambda e: e.dma_start(out=posi, in_=pos_win), writes=[B_pos], dsem=dpos)
        P.add("dve", lambda e: e.tensor_copy(out=posf, in_=posi), reads=[B_pos], writes=[B_pos])
        ang3 = ang.rearrange("p (t f) -> p t f", f=8)
        P.add("dve", lambda e: e.tensor_tensor(out=ang3, in0=posf.unsqueeze(2).to_broadcast([128, NT, 8]),
                                               in1=invf.unsqueeze(1).to_broadcast([128, NT, 8]), op=ALU.mult),
              reads=[B_pos, B_const], writes=[B_pos])
        for (tab, shift) in [(sn_tab, 0.0), (cs_tab, 0.25)]:
            P.add("dve", lambda e, shift=shift: e.tensor_scalar(out=angf, in0=ang, scalar1=1.0 / (2 * np.pi), scalar2=shift,
                                                                op0=ALU.mult, op1=ALU.add), reads=[B_pos], writes=[B_tab])
            P.add("dve", lambda e: e.tensor_copy(out=angi, in_=angf), reads=[B_tab], writes=[B_tab])
            P.add("dve", lambda e, tab=tab: e.tensor_copy(out=tab, in_=angi), reads=[B_tab], writes=[B_tab])
            P.add("dve", lambda e, tab=tab: e.tensor_tensor(out=tab, in0=angf, in1=tab, op=ALU.subtract), reads=[B_tab], writes=[B_tab])
            P.add("act", lambda e, tab=tab: e.activation(out=tab, in_=tab, func=AF.Sin, scale=6.283185005187988), reads=[B_tab], writes=[B_tab])

        def rms_stats_hn(src_ap, n, B_src, slot, grep_ap, B_grep_):
            ss = stat[0:n, 2 * slot:2 * slot + 1]
            rs = stat[0:n, 2 * slot + 1:2 * slot + 2]
            P.add("act", lambda e: e.activation(out=junk[0:n, :], in_=src_ap, func=AF.Square, accum_out=ss), reads=[B_src], writes=[B_rst[slot]])
            P.add("act", lambda e: e.activation(out=rs, in_=ss, func=AF.Sqrt, scale=1.0 / 1024, bias=epsc[0:n, :]), reads=[B_rst[slot], B_const], writes=[B_rst[slot]])
            P.add("dve", lambda e: e.reciprocal(out=rs, in_=rs), reads=[B_rst[slot]], writes=[B_rst[slot]])
            hb = hnb[slot]
            P.add("dve", lambda e: e.scalar_tensor_tensor(out=hb[0:n, :], in0=src_ap, scalar=rs, in1=grep_ap[0:n, :], op0=ALU.mult, op1=ALU.mult),
                  reads=[B_src, B_rst[slot], B_grep_], writes=[B_hnb[slot]])

        def hn_transpose(n, slot, dst_ap, B_dsts, tp_bank):
            hb = hnb[slot]
            tp = bank(tp_bank, BF16).rearrange("p (c n) -> p c n", n=128)
            for c in range(8):
                P.add("pe", lambda e, c=c: e.transpose(out=tp[:, c, 0:n], in_=hb[0:n, c * 128:(c + 1) * 128], identity=ident[0:n, 0:n]),
                      reads=[B_hnb[slot], B_const], writes=[PB[tp_bank]])
            P.add("act", lambda e: e.activation(out=dst_ap, in_=tp[:, :, 0:n], func=AF.Copy), reads=[PB[tp_bank]], writes=B_dsts)

        def rmsnorm_transpose(src_ap, n, B_src, slot, dstT, B_dst, col_ap_fn, tp_bank, grep=grep, B_grep=B_grep):
            rms_stats_hn(src_ap, n, B_src, slot, grep, B_grep)
            hn_transpose(n, slot, col_ap_fn(dstT), [B_dst], tp_bank)

        def headnorm_staged(items, stage_lo, stage_hi):
            n = 128
            for stg_i in range(stage_lo, stage_hi):
                for (c, nheads, gain_idx, T) in items:
                    W = nheads * 64
                    s3 = c.stg[:, 0:W].rearrange("p (h d) -> p h d", d=64)
                    o3 = c.qb[:, 0:W].rearrange("p (h d) -> p h d", d=64)
                    ssq = c.ssq[:, 0:nheads]; rs = c.rs[:, 0:nheads]
                    g3 = hg[:, gain_idx * 64:(gain_idx + 1) * 64].unsqueeze(1).to_broadcast([n, nheads, 64])
                    if stg_i == 1:
                        P.add("act", lambda e, c=c, W=W: e.activation(out=c.qb[:, 0:W], in_=c.stg[:, 0:W], func=AF.Square), reads=[c.B_stg], writes=[c.B_qb])
                    elif stg_i == 2:
                        P.add("dve", lambda e, o3=o3, ssq=ssq: e.tensor_reduce(out=ssq, in_=o3, axis=AX.X, op=ALU.add), reads=[c.B_qb], writes=[c.B_st])
                    elif stg_i == 3:
                        P.add("act", lambda e, ssq=ssq, rs=rs: e.activation(out=rs, in_=ssq, func=AF.Sqrt, scale=1.0 / 64, bias=epsc), reads=[c.B_st, B_const], writes=[c.B_st])
                    elif stg_i == 4:
                        P.add("dve", lambda e, rs=rs: e.reciprocal(out=rs, in_=rs), reads=[c.B_st], writes=[c.B_st])
                    elif stg_i == 5:
                        P.add("dve", lambda e, s3=s3, rs=rs, nheads=nheads: e.tensor_tensor(out=s3, in0=s3, in1=rs.unsqueeze(2).to_broadcast([n, nheads, 64]), op=ALU.mult),
                              reads=[c.B_stg, c.B_st], writes=[c.B_stg])
                    elif stg_i == 6:
                        if T is None:
                            P.add("dve", lambda e, s3=s3, o3=o3, g3=g3: e.tensor_tensor(out=o3, in0=s3, in1=g3, op=ALU.mult), reads=[c.B_stg, B_const, c.B_qb], writes=[c.B_qb])
                        else:
                            P.add("dve", lambda e, s3=s3, g3=g3: e.tensor_tensor(out=s3, in0=s3, in1=g3, op=ALU.mult), reads=[c.B_stg, B_const], writes=[c.B_stg])
                    elif stg_i == 7 and T is not None:
                        cosb = cs_tab[:, T * 8:T * 8 + 8].unsqueeze(1).to_broadcast([n, nheads, 8])
                        sinb = sn_tab[:, T * 8:T * 8 + 8].unsqueeze(1).to_broadcast([n, nheads, 8])
                        x1 = s3[:, :, 0:8]
                        x2 = s3[:, :, 8:16]
                        r4 = c.rot[:, :].rearrange("p (k h f) -> p k h f", k=4, f=8)
                        P.add("act", lambda e, o3=o3, s3=s3: e.activation(out=o3[:, :, 16:64], in_=s3[:, :, 16:64], func=AF.Copy), reads=[c.B_stg, c.B_qb], writes=[c.B_qb])
                        P.add("pool", lambda e, r4=r4, x1=x1, cosb=cosb: e.tensor_tensor(out=r4[:, 0], in0=x1, in1=cosb, op=ALU.mult), reads=[c.B_stg, B_tab], writes=[c.B_rot])
                        P.add("pool", lambda e, r4=r4, x2=x2, sinb=sinb: e.tensor_tensor(out=r4[:, 1], in0=x2, in1=sinb, op=ALU.mult), reads=[c.B_stg, B_tab], writes=[c.B_rot])
                        P.add("pool", lambda e, r4=r4, x2=x2, cosb=cosb: e.tensor_tensor(out=r4[:, 2], in0=x2, in1=cosb, op=ALU.mult), reads=[c.B_stg, B_tab], writes=[c.B_rot])
                        P.add("pool", lambda e, r4=r4, x1=x1, sinb=sinb: e.tensor_tensor(out=r4[:, 3], in0=x1, in1=sinb, op=ALU.mult), reads=[c.B_stg, B_tab], writes=[c.B_rot])
                        P.add("pool", lambda e, r4=r4, o3=o3: e.tensor_tensor(out=o3[:, :, 0:8], in0=r4[:, 0], in1=r4[:, 1], op=ALU.subtract), reads=[c.B_rot, c.B_qb], writes=[c.B_qb])
                        P.add("pool", lambda e, r4=r4, o3=o3: e.tensor_tensor(out=o3[:, :, 8:16], in0=r4[:, 2], in1=r4[:, 3], op=ALU.add), reads=[c.B_rot, c.B_qb], writes=[c.B_qb])

        def transpose_to(srcb, B_srcb, n, nblk, dstT, B_dst, col0, tq_bank):
            tq = bank(tq_bank, BF16).rearrange("p (c n) -> p c n", n=128)
            for j in range(nblk):
                P.add("pe", lambda e, j=j: e.transpose(out=tq[:, j, 0:n], in_=srcb[0:n, j * 128:(j + 1) * 128], identity=ident[0:n, 0:n]),
                      reads=[B_srcb, B_const], writes=[PB[tq_bank]])
            P.add("dve", lambda e: e.tensor_copy(out=dstT[:, 0:nblk, col0:col0 + n], in_=tq[:, 0:nblk, 0:n]),
                  reads=[PB[tq_bank]], writes=[B_dst])


        dmem = P.dsem("mem")
        wm_src = w_mkv_d.rearrange("(c p) n -> p c n", p=128)
        P.add("pool", lambda e: e.dma_start(out=w_mkv, in_=wm_src), writes=[B_wmkv], dsem=dmem)
        dws = [P.dsem("w_in%d" % i) for i in range(8)]
        w_in_src = w_in_d.rearrange("(c p) n -> p c n", p=128)
        for c in range(8):
            P.add("pool", lambda e, c=c: e.dma_start(out=w_in[:, c, :], in_=w_in_src[:, c, :]), writes=[B_wins[c]], dsem=dws[c])

        dx = [P.dsem("x0"), P.dsem("x1")]
        P.add("sp", lambda e: e.dma_start(out=gmrep, in_=gm_d[0:1, :].to_broadcast([128, 1024])), writes=[B_gm], dsem=P.dsem("gm"))
        for mt in range(2):
            P.add("sp", lambda e, mt=mt: e.dma_start(out=xt[mt], in_=mem_d[128 * mt:128 * mt + 128, :]), writes=[B_xt[mt]], dsem=dx[mt])
            rmsnorm_transpose(xt[mt], 128, B_xt[mt], mt, memT, B_memT, lambda d: d[:, :, 0:128], 4, grep=gmrep, B_grep=B_gm)
            pb = bank(3)
            cm = ctx_qm
            for c in range(8):
                P.add("pe", lambda e, c=c, pb=pb: e.matmul(pb[:, 0:512], lhsT=memT[:, c, 0:128], rhs=w_mkv[:, c, :], start=(c == 0), stop=(c == 7)),
                      reads=[B_memT, B_wmkv], writes=[PB[3]])
            P.add("act", lambda e, pb=pb: e.activation(out=cm.stg[:, 0:512], in_=pb[:, 0:512], func=AF.Copy), reads=[PB[3]], writes=[cm.B_stg])
            vsrc = cm.stg[:, 256:512].rearrange("p (hp two d) -> p hp two d", two=2, d=64)
            vdst = VmA[:, mt, :].rearrange("p (hp s d) -> p hp s d", s=4, d=64)
            P.add("dve", lambda e, vdst=vdst, vsrc=vsrc: e.tensor_copy(out=vdst[:, :, 0, :], in_=vsrc[:, :, 0, :]), reads=[cm.B_stg], writes=[B_VmA])
            P.add("dve", lambda e, vdst=vdst, vsrc=vsrc: e.tensor_copy(out=vdst[:, :, 3, :], in_=vsrc[:, :, 1, :]), reads=[cm.B_stg], writes=[B_VmA])
            P.add("pool", lambda e, vdst=vdst: e.memset(vdst[:, :, 1:3, :], 1.0), writes=[B_VmA])
            headnorm_staged([(cm, 4, 3, None)], 1, 8)
            transpose_to(cm.qb, cm.B_qb, 128, 2, KmT, B_KmT, 128 * mt, 5)

        fm_rot = [0]
        FM_BANKS = [6, 7]

        def hslot(T):
            return ((T // 2) % 2, T % 2)

        def A1(T):
            xs = T % 2
            P.add("sp", lambda e, T=T, xs=xs: e.dma_start(out=xt[xs], in_=x_win[128 * T:128 * T + 128, :]), writes=[B_xt[xs]], dsem=dx[xs])
            rms_stats_hn(xt[xs], 128, B_xt[xs], xs, grep, B_grep)

        def A2(T):
            hs, part = hslot(T)
            hn_transpose(128, T % 2, hnT[hs][:, :, 128 * part:128 * part + 128], [B_hnT[hs][part]], 4)

        def tile_items(T):
            items = [(ctx_k[T % 2], 8, 1, T)]
            if 8 <= T <= 24:
                items += [(ctx_qs[T % 2], 8, 0, T), (ctx_qms[T % 2], 4, 2, None)]
            return items

        def Bst(T):
            hs, part = hslot(T)
            lc = 128 * part
            jobs = [(0, 512, 1024, 512, ctx_k[T % 2])]
            if 8 <= T <= 24:
                jobs += [(2, 0, 512, 512, ctx_qs[T % 2]), (3, 2048, 2304, 256, ctx_qms[T % 2])]
            for (bi, c0, c1, W, cx) in jobs:
                pb = bank(bi)
                for c in range(8):
                    P.add("pe", lambda e, c=c, pb=pb, c0=c0, c1=c1, W=W, lc=lc, hs=hs: e.matmul(pb[:, 0:W], lhsT=hnT[hs][:, c, lc:lc + 128], rhs=w_in[:, c, c0:c1],
                                                                                                start=(c == 0), stop=(c == 7)),
                          reads=[B_hnT[hs][part], B_wins[c]], writes=[PB[bi]])
                P.add("act", lambda e, pb=pb, cx=cx, W=W: e.activation(out=cx.stg[:, 0:W], in_=pb[:, 0:W], func=AF.Copy), reads=[PB[bi]], writes=[cx.B_stg])

        def Dk(T):
            transpose_to(ctx_k[T % 2].qb, ctx_k[T % 2].B_qb, 128, 4, KT, B_KT, 128 * T, 5)

        def Dq(T):
            if T == 8:
                P.guard(writes=[B_QT, B_wmkv, B_gm, B_memT])
            if 8 <= T <= 24:
                tq = bank(1, BF16).rearrange("p (c n) -> p c n", n=128)
                for j in range(4):
                    P.add("pe", lambda e, j=j: e.transpose(out=tq[:, j, :], in_=ctx_qs[T % 2].qb[:, j * 128:(j + 1) * 128], identity=ident),
                          reads=[ctx_qs[T % 2].B_qb, B_const], writes=[PB[1]])
                for j in range(2):
                    P.add("pe", lambda e, j=j: e.transpose(out=tq[:, 4 + j, :], in_=ctx_qms[T % 2].qb[:, j * 128:(j + 1) * 128], identity=ident),
                          reads=[ctx_qms[T % 2].B_qb, B_const], writes=[PB[1]])
                col0 = 128 * T - QC0
                P.add("dve", lambda e: e.tensor_copy(out=QT[:, 0:4, col0:col0 + 128], in_=tq[:, 0:4, :]), reads=[PB[1]], writes=[B_QT])
                P.add("dve", lambda e: e.tensor_copy(out=QmT[:, 0:2, col0:col0 + 128], in_=tq[:, 4:6, :]), reads=[PB[1]], writes=[B_QmT])

        def Fst(ch):
            tiles = list(range(2 * ch, min(2 * ch + 2, NT)))
            hs = ch % 2
            ntok = 128 * len(tiles)
            rd0 = [B_hnT[hs][t % 2] for t in tiles]
            u0 = 256 * ch
            for fp in range(2):
                bi = FM_BANKS[fm_rot[0] % 2]; fm_rot[0] += 1
                pb = bank(bi)
                for sub in range(2):
                    ft = 2 * fp + sub
                    for c in range(8):
                        P.add("pe", lambda e, c=c, pb=pb, ft=ft, sub=sub: e.matmul(pb[:, 256 * sub:256 * sub + ntok], lhsT=w_in[:, c, 1024 + 128 * ft:1024 + 128 * ft + 128],
                                                                                   rhs=hnT[hs][:, c, 0:ntok], start=(c == 0 and sub == 0), stop=(c == 7), skip_group_check=True),
                              reads=rd0 + [B_wins[c]], writes=[PB[bi]])
                pv = pb.rearrange("p (s n) -> p s n", n=256)
                P.add("dve", lambda e, pv=pv, fp=fp: e.tensor_copy(out=VT[:, 2 * fp:2 * fp + 2, u0:u0 + ntok], in_=pv[:, :, 0:ntok]), reads=[PB[bi]], writes=[B_VT])
            if QC0 <= u0 < QC0 + NQC:
                ng = min(256, QC0 + NQC - u0)
                for ft in range(2):
                    bi = FM_BANKS[fm_rot[0] % 2]; fm_rot[0] += 1
                    pb = bank(bi)
                    for (sub, cbase) in [(0, 1536), (1, 1792)]:
                        for c in range(8):
                            P.add("pe", lambda e, c=c, pb=pb, cbase=cbase, ft=ft, sub=sub: e.matmul(pb[:, 256 * sub:256 * sub + ng], lhsT=w_in[:, c, cbase + 128 * ft:cbase + 128 * ft + 128],
                                                                                                     rhs=hnT[hs][:, c, 0:ng], start=(c == 0 and sub == 0), stop=(c == 7), skip_group_check=True),
                                  reads=rd0 + [B_wins[c]], writes=[PB[bi]])
                    P.add("act", lambda e, pb=pb: e.activation(out=sig[:, 0:ng], in_=pb[:, 256:256 + ng], func=AF.Sigmoid), reads=[PB[bi]], writes=[B_sig])
                    P.add("dve", lambda e, pb=pb, ft=ft: e.tensor_tensor(out=cT[:, ft, u0 - QC0:u0 - QC0 + ng], in0=pb[:, 0:ng], in1=sig[:, 0:ng], op=ALU.mult),
                          reads=[PB[bi], B_sig], writes=[B_cT])

        A1(0); A2(0); A1(1)
        for T in range(NT):
            if T + 1 < NT:
                A2(T + 1)
            if T + 2 < NT:
                A1(T + 2)
            Bst(T)
            items = tile_items(T)
            if T % 2 == 1 or T == NT - 1:
                Fst(T // 2)
            headnorm_staged(items, 1, 8)
            if T >= 1:
                Dq(T - 1)
                Dk(T - 1)
        Dq(NT - 1)
        Dk(NT - 1)

        R2 = Region(128000, 172000 - 32)
        acc = view(R2.alloc(2 * NQ * 4), 2 * NQ, F32).rearrange("p (h q) -> p h q", q=NQ)
        rden = view(R2.alloc(NQ * 4), NQ, F32)
        Pt = [view(R2.alloc(1024), 512, BF16).rearrange("p (h n) -> p h n", n=256) for _ in range(3)]
        Vts = [view(R2.alloc(512), 256, BF16) for _ in range(4)]
        B_accs = [Buf("acc%d" % i) for i in range(5)]; B_rdens = [Buf("rden%d" % i) for i in range(5)]
        B_acc = B_accs[0]; B_rden = B_rdens[0]
        B_Pt = [Buf("Pt0"), Buf("Pt1"), Buf("Pt2")]; B_Vts = [Buf("Vt%d" % i) for i in range(4)]; B_Vvs = [Buf("Vv%d" % i) for i in range(4)]

        def acc_bufs(isl):
            return [B_accs[k] for k in range(isl.start // 512, (isl.stop - 1) // 512 + 1)]
        P1_TEMPS = B_wins + B_hnT[0] + B_hnT[1] + B_xt + [B_sig, B_pos, B_tab]
        for c_ in ALL_CTX:
            P1_TEMPS += [c_.B_stg, c_.B_qb, c_.B_rot, c_.B_st]
        P.guard(writes=P1_TEMPS + B_accs + B_rdens + [B_mixed] + B_Pt + B_Vts + B_Vvs)

        for vs_ in range(4):
            P.add("pool", lambda e, vs_=vs_: e.memset(Vts[vs_][:, 64:192], 1.0), writes=[B_Vvs[vs_]])
        for hp in range(4):
            kts = []
            for (D, rho, sq0, nq, nqb, tiles) in PT:
                qbs = [min(128, nq - 128 * m) for m in range(nqb)]
                for j, (a, nk, idx) in enumerate(tiles):
                    kts.append((D, rho, sq0, nqb, qbs, j, a, nk, idx))
            NK = len(kts)

            def ksl_of(i):
                (D, rho, sq0, nqb, qbs, j, a, nk, idx) = kts[i]
                ku0 = rho + D * a
                return slice(ku0, ku0 + D * (nk - 1) + 1, D)

            def stV(i):
                (D, rho, sq0, nqb, qbs, j, a, nk, idx) = kts[i]
                vs = i % 4
                ksl = ksl_of(i)
                vt = bank(6, BF16)[:, 0:128]
                P.add("pe", lambda e, hp=hp: e.transpose(out=vt[0:nk, :], in_=VT[:, hp, ksl], identity=ident), reads=[B_VT, B_const], writes=[PB[6]])
                V4 = Vts[vs].rearrange("p (s d) -> p s d", d=64)
                P.add("act", lambda e: e.activation(out=V4[0:nk, 0:4:3, :], in_=vt[0:nk, :].rearrange("p (s d) -> p s d", d=64), func=AF.Copy),
                      reads=[PB[6]], writes=[B_Vts[vs]])

            def cols_of(i):
                (D, rho, sq0, nqb, qbs, j, a, nk, idx) = kts[i]
                c0 = 0 if j >= 1 else 128
                c1 = (128 + qbs[j]) if j < nqb else qbs[j - 1]
                return c0, c1

            def stS(i):
                (D, rho, sq0, nqb, qbs, j, a, nk, idx) = kts[i]
                si = i % 2
                pi_ = i % 3
                ksl = ksl_of(i)
                c0, c1 = cols_of(i)
                nc_ = c1 - c0
                qu0 = rho + D * (a - 64 + c0) - QC0
                qsl = slice(qu0, qu0 + D * (nc_ - 1) + 1, D)
                S2 = pbanks[si]
                for h in range(2):
                    P.add("pe", lambda e, h=h, hp=hp: e.matmul(S2[0:nk, h * 512 + c0:h * 512 + c1], lhsT=KT[h * 64:(h + 1) * 64, hp, ksl],
                                                        rhs=QT[h * 64:(h + 1) * 64, hp, qsl], start=True, stop=False, skip_group_check=True),
                          reads=[B_KT, B_QT], writes=[PB[2 * si + h]])
                for h in range(2):
                    P.add("pe", lambda e, h=h: e.matmul(S2[0:nk, h * 512 + c0:h * 512 + c1], lhsT=ident[0:nk, 0:nk], rhs=bandmask[0:nk, c0:c1],
                                                        start=False, stop=True, skip_group_check=True),
                          reads=[B_const], writes=[PB[2 * si + h]])
                S3 = S2.rearrange("p (h n) -> p h n", n=512)
                P.add("act", lambda e: e.activation(out=Pt[pi_][0:nk, :, c0:c1], in_=S3[0:nk, :, c0:c1], func=AF.Exp, scale=0.125,
                                                    bias=valcols[0:nk, idx:idx + 1]),
                      reads=[PB[2 * si], PB[2 * si + 1], B_const], writes=[B_Pt[pi_]])

            def stPV(i):
                (D, rho, sq0, nqb, qbs, j, a, nk, idx) = kts[i]
                pi_ = i % 3
                vs = i % 4
                halves = []
                if j >= 1:
                    halves.append((j - 1, 0, qbs[j - 1]))
                if j < nqb:
                    halves.append((j, 128, qbs[j]))
                for (m, cl, nqm) in halves:
                    ob = 4 + (m % 2)
                    O3 = bank(ob).rearrange("p (h n) -> p h n", n=256)
                    for h in range(2):
                        first = (m == j) and h == 0
                        P.add("pe", lambda e, O3=O3, h=h, nqm=nqm, cl=cl, first=first: e.matmul(
                            O3[:, h, 0:nqm], lhsT=Vts[vs][0:nk, h * 128:(h + 1) * 128], rhs=Pt[pi_][0:nk, h, cl:cl + nqm],
                            start=first, stop=False, skip_group_check=True),
                            reads=[B_Vts[vs], B_Vvs[vs], B_Pt[pi_]], writes=[PB[ob]])
                    if m == j - 1:
                        qi0 = rho + D * (sq0 + 128 * m) - UQ0
                        isl = slice(qi0, qi0 + D * (nqm - 1) + 1, D)
                        if D == 1:
                            P.add("dve", lambda e, O3=O3, isl=isl, nqm=nqm: e.tensor_copy(out=acc[:, :, isl], in_=O3[:, :, 0:nqm]), reads=[PB[ob]], writes=acc_bufs(isl))
                        else:
                            P.add("dve", lambda e, O3=O3, isl=isl, nqm=nqm: e.tensor_tensor(out=acc[:, :, isl], in0=O3[:, :, 0:nqm], in1=acc[:, :, isl], op=ALU.add),
                                  reads=[PB[ob]] + acc_bufs(isl), writes=acc_bufs(isl))

            stV(0); stV(1); stV(2); stS(0); stS(1)
            for i in range(NK):
                if i + 3 < NK:
                    stV(i + 3)
                if i + 2 < NK:
                    stS(i + 2)
                stPV(i)
            if hp == 3:
                R3 = Region(22528, 107520)
                w_out = view(R3.alloc(8 * 1024 * 2), 8192, BF16).rearrange("p (c n) -> p c n", n=1024)
                Dg = view(R3.alloc(62 * 128 * 2), 62 * 128, BF16).rearrange("p (k n) -> p k n", n=128)
                yf = [view(R3.alloc(2 * 512 * 4), 1024, F32).rearrange("p (c n) -> p c n", n=512) for _ in range(3)]
                ybf = [view(R3.alloc(2 * 512 * 2), 1024, BF16).rearrange("p (c n) -> p c n", n=512) for _ in range(3)]
                ysq = [view(R3.alloc(2 * 512 * 2), 1024, BF16).rearrange("p (c n) -> p c n", n=512) for _ in range(3)]
                mean_sb = [view(R3.alloc(2048), 512, F32) for _ in range(2)]
                var_sb = [view(R3.alloc(2048), 512, F32) for _ in range(2)]
                Pm = [view(R3.alloc(2 * 512 * 2), 1024, BF16).rearrange("p (h n) -> p h n", n=512) for _ in range(2)]
                rdm = [view(R3.alloc(2048), 512, F32) for _ in range(2)]
                osb = [view(R3.alloc(2048), 512, F32) for _ in range(2)]
                B_osb = [Buf("osb0"), Buf("osb1")]
                B_wout = Buf("wout"); B_Dg = Buf("Dg")
                B_yf = [[Buf("yf%d%d" % (i, j)) for j in range(2)] for i in range(3)]; B_ybf = [Buf("ybf0"), Buf("ybf1"), Buf("ybf2")]; B_ysq = [Buf("ysq0"), Buf("ysq1"), Buf("ysq2")]
                B_mean = [Buf("mean0"), Buf("mean1")]; B_var = [Buf("var0"), Buf("var1")]
                B_Pm = [Buf("Pm0"), Buf("Pm1")]; B_rdm = [Buf("rdm0"), Buf("rdm1")]
                B34 = B_osb + [B_Dg] + B_yf[0] + B_yf[1] + B_yf[2] + B_ybf + B_ysq + B_mean + B_var + B_Pm + B_rdm
                P.guard(writes=[B_KT, B_VT, B_QT, B_wout] + B34)
                dw2 = P.dsem("w_out")
                wo_src = w_out_d.rearrange("(c p) n -> p c n", p=128)
                for c in range(0, 8, 2):
                    P.add("pool", lambda e, c=c: e.dma_start(out=w_out[:, c:c + 2, :], in_=wo_src[:, c:c + 2, :]), writes=[B_wout], dsem=dw2)
                P.add("dve", lambda e: e.tensor_tensor(out=Dg, in0=ident.unsqueeze(1).to_broadcast([128, 62, 128]), in1=cw.unsqueeze(2).to_broadcast([128, 62, 128]), op=ALU.mult),
                      reads=[B_const], writes=[B_Dg])

            for k in range(5):
                cs_ = slice(512 * k, min(512 * k + 512, NQ))
                P.add("dve", lambda e, cs_=cs_: e.reciprocal(out=rden[0:64, cs_], in_=acc[64:128, 0, cs_]), reads=[B_accs[k]], writes=[B_rdens[k]])
                P.add("dve", lambda e, cs_=cs_: e.reciprocal(out=rden[64:128, cs_], in_=acc[0:64, 1, cs_]), reads=[B_accs[k]], writes=[B_rdens[k]])
                P.add("pool", lambda e, hp=hp, cs_=cs_: e.tensor_tensor(out=mixedT[0:64, hp, cs_], in0=acc[0:64, 0, cs_], in1=rden[0:64, cs_], op=ALU.mult),
                      reads=[B_accs[k], B_rdens[k]], writes=[B_mixed])
                P.add("pool", lambda e, hp=hp, cs_=cs_: e.tensor_tensor(out=mixedT[64:128, hp, cs_], in0=acc[64:128, 1, cs_], in1=rden[64:128, cs_], op=ALU.mult),
                      reads=[B_accs[k], B_rdens[k]], writes=[B_mixed])

        QBLK = [(0, 512), (512, 512), (1024, 512), (1536, 512), (2048, 2)]

        def conv_a(b, chns=(0, 1)):
            (qi0, n) = QBLK[b]
            s_ = b % 3
            for chn in chns:
                pb = bank(6 + chn)
                for k in range(31):
                    P.add("pe", lambda e, pb=pb, chn=chn, k=k: e.matmul(pb[:, 0:n], lhsT=Dg[:, chn * 31 + k, :], rhs=cT[:, chn, qi0 + 48 + k:qi0 + 48 + k + n],
                                                                        start=(k == 0), stop=(k == 30)),
                          reads=[B_Dg, B_cT], writes=[PB[6 + chn]])
                P.add("act", lambda e, pb=pb, chn=chn: e.activation(out=yf[s_][:, chn, 0:n], in_=pb[:, 0:n], func=AF.Identity, bias=cp[:, chn:chn + 1]),
                      reads=[PB[6 + chn], B_const], writes=[B_yf[s_][chn]])
                P.add("pool", lambda e, chn=chn: e.tensor_copy(out=ybf[s_][:, chn, 0:n], in_=yf[s_][:, chn, 0:n]), reads=[B_yf[s_][chn]], writes=[B_ybf[s_]])
                P.add("act", lambda e, chn=chn: e.activation(out=ysq[s_][:, chn, 0:n], in_=yf[s_][:, chn, 0:n], func=AF.Square), reads=[B_yf[s_][chn]], writes=[B_ysq[s_]])

        def conv_b(b):
            (qi0, n) = QBLK[b]
            s_ = b % 3
            m_ = b % 2
            for chn in range(2):
                P.add("pe", lambda e, chn=chn: e.matmul(bank(6)[:, 0:n], lhsT=onesm, rhs=ybf[s_][:, chn, 0:n], start=(chn == 0), stop=(chn == 1)),
                      reads=[B_ybf[s_], B_const], writes=[PB[6]])
            for chn in range(2):
                P.add("pe", lambda e, chn=chn: e.matmul(bank(7)[:, 0:n], lhsT=onesm, rhs=ysq[s_][:, chn, 0:n], start=(chn == 0), stop=(chn == 1)),
                      reads=[B_ysq[s_], B_const], writes=[PB[7]])
            mean = mean_sb[m_]; var = var_sb[m_]
            P.add("act", lambda e: e.activation(out=mean[:, 0:n], in_=bank(6)[:, 0:n], func=AF.Copy), reads=[PB[6]], writes=[B_mean[m_]])
            P.add("dve", lambda e: e.tensor_tensor(out=var[:, 0:n], in0=mean[:, 0:n], in1=mean[:, 0:n], op=ALU.mult), reads=[B_mean[m_]], writes=[B_var[m_]])
            P.add("dve", lambda e: e.tensor_tensor(out=var[:, 0:n], in0=bank(7)[:, 0:n], in1=var[:, 0:n], op=ALU.subtract), reads=[PB[7], B_var[m_]], writes=[B_var[m_]])
            P.add("act", lambda e: e.activation(out=var[:, 0:n], in_=var[:, 0:n], func=AF.Sqrt, bias=epsc), reads=[B_var[m_], B_const], writes=[B_var[m_]])
            P.add("dve", lambda e: e.reciprocal(out=var[:, 0:n], in_=var[:, 0:n]), reads=[B_var[m_]], writes=[B_var[m_]])
            for chn in range(2):
                eng = "dve" if chn == 0 else "pool"
                P.add(eng, lambda e, chn=chn: e.tensor_tensor(out=yf[s_][:, chn, 0:n], in0=yf[s_][:, chn, 0:n], in1=mean[:, 0:n], op=ALU.subtract),
                      reads=[B_yf[s_][chn], B_mean[m_]], writes=[B_yf[s_][chn]])
                P.add(eng, lambda e, chn=chn: e.tensor_tensor(out=yf[s_][:, chn, 0:n], in0=yf[s_][:, chn, 0:n], in1=var[:, 0:n], op=ALU.mult),
                      reads=[B_yf[s_][chn], B_var[m_]], writes=[B_yf[s_][chn]])
                P.add("act", lambda e, chn=chn: e.activation(out=mixedT[:, 4 + chn, qi0:qi0 + n], in_=yf[s_][:, chn, 0:n], func=AF.Silu,
                                                             scale=cp[:, 2 + chn:3 + chn], bias=cp[:, 4 + chn:5 + chn]),
                      reads=[B_yf[s_][chn], B_const], writes=[B_mixed])

        def memS(hp, b):
            (qi0, n) = QBLK[b]
            for mt in range(2):
                S2 = pbanks[mt]
                for h in range(2):
                    P.add("pe", lambda e, S2=S2, h=h, mt=mt: e.matmul(
                        S2[:, h * 512:h * 512 + n], lhsT=KmT[h * 64:(h + 1) * 64, hp, 128 * mt:128 * mt + 128],
                        rhs=QmT[h * 64:(h + 1) * 64, hp, qi0 + UQ0 - QC0:qi0 + UQ0 - QC0 + n], start=True, stop=True),
                        reads=[B_KmT, B_QmT], writes=[PB[2 * mt + h]])
                S3 = S2.rearrange("p (h n) -> p h n", n=512)
                P.add("act", lambda e, S3=S3, mt=mt: e.activation(out=Pm[mt][:, :, 0:n], in_=S3[:, :, 0:n], func=AF.Exp, scale=0.125),
                      reads=[PB[2 * mt], PB[2 * mt + 1]], writes=[B_Pm[mt]])

        def memPV(hp, b):
            (qi0, n) = QBLK[b]
            for h in range(2):
                for mt in range(2):
                    P.add("pe", lambda e, h=h, mt=mt: e.matmul(bank(4 + h)[:, 0:n], lhsT=VmA[:, mt, hp * 256 + h * 128:hp * 256 + h * 128 + 128],
                                                               rhs=Pm[mt][:, h, 0:n], start=(mt == 0), stop=(mt == 1)),
                          reads=[B_VmA, B_Pm[mt]], writes=[PB[4 + h]])
            for h in range(2):
                P.add("act", lambda e, h=h: e.activation(out=osb[h][:, 0:n], in_=bank(4 + h)[:, 0:n], func=AF.Copy), reads=[PB[4 + h]], writes=[B_osb[h]])
            P.add("dve", lambda e: e.reciprocal(out=rdm[0][0:64, 0:n], in_=osb[0][64:128, 0:n]), reads=[B_osb[0]], writes=[B_rdm[0]])
            P.add("dve", lambda e: e.reciprocal(out=rdm[1][64:128, 0:n], in_=osb[1][0:64, 0:n]), reads=[B_osb[1]], writes=[B_rdm[1]])
            P.add("pool", lambda e: e.tensor_tensor(out=mixedT[0:64, 6 + hp, qi0:qi0 + n], in0=osb[0][0:64, 0:n], in1=rdm[0][0:64, 0:n], op=ALU.mult),
                  reads=[B_osb[0], B_rdm[0]], writes=[B_mixed])
            P.add("pool", lambda e: e.tensor_tensor(out=mixedT[64:128, 6 + hp, qi0:qi0 + n], in0=osb[1][64:128, 0:n], in1=rdm[1][64:128, 0:n], op=ALU.mult),
                  reads=[B_osb[1], B_rdm[1]], writes=[B_mixed])

        conv_a(0)
        conv_a(1)
        for b in range(5):
            memS(0, b)
            if b + 2 < 5:
                conv_a(b + 2, (0,))
            memPV(0, b)
            memS(1, b)
            if b + 2 < 5:
                conv_a(b + 2, (1,))
            memPV(1, b)
            conv_b(b)

        if DEBUG:
            dd = P.dsem("dbg")
            dbgf = view(128000, 8 * NQ, F32)
            P.add("dve", lambda e: e.tensor_copy(out=dbgf, in_=mixedT.rearrange("p c q -> p (c q)")), reads=[B_mixed, B_acc, B_rden], writes=[B_acc, B_rden])
            P.add("sp", lambda e: e.dma_start(out=dbg["mixedT"], in_=dbgf), reads=[B_acc], dsem=dd)

        R5 = Region(38912, 172000 - 32)
        xr = [view(R5.alloc(4096), 1024, F32) for _ in range(2)] + [view(165984, 1024, F32)]
        hmh = view(R5.alloc(4096), 1024, F32)
        hn2T = view(R5.alloc(8 * NQ * 2), 8 * NQ, BF16).rearrange("p (c q) -> p c q", q=NQ)
        hmid = view(R5.alloc(16 * 4096), 16 * 1024, F32).rearrange("p (t n) -> p t n", n=1024)
        B_xr = [Buf("xr0"), Buf("xr1"), Buf("xr2")]; B_hmh = Buf("hmh"); B_hn2T = Buf("hn2T"); B_hmid = [Buf("hmid%d" % i) for i in range(16)]
        dead34 = B34 + [B_QmT, B_cT, B_KmT, B_VmA] + B_accs + B_rdens + B_Pt + B_Vts + B_Vvs
        P.guard(writes=dead34 + B_xr + [B_hmh, B_hn2T] + B_hmid)
        R6w = Region(149600, 149600 + 16384)
        wup = [view(R6w.alloc(8 * 512 * 2), 4096, BF16).rearrange("p (c n) -> p c n", n=512) for _ in range(2)]
        B_wup = [Buf("wup%d" % i) for i in range(2)]
        dwu = [P.dsem("wup%d" % i) for i in range(2)]
        wu_src = w_up_d.rearrange("(c p) n -> p c n", p=128)

        def load_pp(pp):
            s_ = pp % 2
            P.add("pool", lambda e, s_=s_, pp=pp: e.dma_start(out=wup[s_][:, :, 0:256], in_=wu_src[:, :, 256 * pp:256 * pp + 256]), writes=[B_wup[s_]], dsem=dwu[s_])
            P.add("pool", lambda e, s_=s_, pp=pp: e.dma_start(out=wup[s_][:, :, 256:512], in_=wu_src[:, :, DFF + 256 * pp:DFF + 256 * pp + 256]), writes=[B_wup[s_]], dsem=dwu[s_])

        P.guard(writes=B_accs + B_rdens + B_Pt + B_Vts + B_Vvs + B_wup)
        load_pp(0)
        load_pp(1)
        dg2 = P.dsem("g2")
        P.add("sp", lambda e: e.dma_start(out=grep, in_=g2_d[0:1, :].to_broadcast([128, 1024])), writes=[B_grep], dsem=dg2)
        dxr = [P.dsem("xr0"), P.dsem("xr1"), P.dsem("xr2")]
        def p5_geom(i):
            if i < 16:
                q0 = 1 + 128 * i
                return 128, slice(q0, q0 + 128), hmid[:, i, :], B_hmid[i]
            return 2, slice(0, NQ, NQ - 1), hmh, B_hmh

        P5B = [(0, 1), (2, 3), (6, 7)]
        def p5_mm(i):
            xs = i % 3
            n, lsl, hdst, B_h = p5_geom(i)
            if i < 16:
                P.add("sp", lambda e: e.dma_start(out=xr[xs], in_=x_win[UQ0 + 1 + 128 * i:UQ0 + 1 + 128 * i + 128, :]), writes=[B_xr[xs]], dsem=dxr[xs])
            else:
                P.add("sp", lambda e: e.dma_start(out=xr[xs][0:1, :], in_=x_win[UQ0:UQ0 + 1, :]), writes=[B_xr[xs]], dsem=dxr[xs])
                P.add("sp", lambda e: e.dma_start(out=xr[xs][1:2, :], in_=x_win[UQ0 + NQ - 1:UQ0 + NQ, :]), writes=[B_xr[xs]], dsem=dxr[xs])
            for half in range(2):
                pb = bank(P5B[i % 3][half])
                for c in range(8):
                    P.add("pe", lambda e, pb=pb, c=c, half=half: e.matmul(pb[0:n, :], lhsT=mixedT[:, c, lsl], rhs=w_out[:, c, half * 512:half * 512 + 512],
                                                                         start=(c == 0), stop=(c == 7)),
                          reads=[B_mixed, B_wout], writes=[PB[P5B[i % 3][half]]])
            pp2 = pbanks[P5B[i % 3][0] // 2]
            P.add("dve", lambda e: e.tensor_tensor(out=hdst[0:n, :], in0=pp2[0:n, :], in1=xr[xs][0:n, :], op=ALU.add),
                  reads=[PB[P5B[i % 3][0]], PB[P5B[i % 3][1]], B_xr[xs]], writes=[B_h])
            rms_stats_hn(hdst[0:n, :], n, B_h, xs, grep, B_grep)

        def p5_tr(i):
            xs = i % 3
            n, lsl, hdst, B_h = p5_geom(i)
            hn_transpose(n, xs, hn2T[:, :, lsl], [B_hn2T], 4 + (i % 2))
            if i == 16:
                P.add("pool", lambda e: e.tensor_tensor(out=hn2T[:, :, lsl], in0=hn2T[:, :, lsl], in1=hv.unsqueeze(1).to_broadcast([128, 8, 2]), op=ALU.mult),
                      reads=[B_hn2T, B_const], writes=[B_hn2T])

        p5_mm(0)
        p5_mm(1)
        for i in range(17):
            if i + 2 < 17:
                p5_mm(i + 2)
            p5_tr(i)

        if DEBUG:
            P.add("sp", lambda e: e.dma_start(out=dbg["hmid"], in_=hmid.rearrange("p t n -> p (t n)")), reads=B_hmid, dsem=dd)

        R6a = Region(22528, 51200)
        R6b = Region(171968, TOT)
        tg = [view(R6a.alloc(416 * 4), 416, F32) for _ in range(2)]
        tu = [view(R6a.alloc(416 * 4), 416, F32) for _ in range(2)]
        sg = [view(R6a.alloc(416 * 4), 416, F32) for _ in range(2)]
        actT = [None, None]
        actT[0] = view(R6a.alloc(4 * 2048 * 2), 4 * 2048, BF16).rearrange("p (k n) -> p k n", n=2048)
        actT[1] = view(R6b.alloc(4 * 2048 * 2), 4 * 2048, BF16).rearrange("p (k n) -> p k n", n=2048)
        wdn = [view(R6b.alloc(4 * 1024 * 2), 4 * 1024, BF16).rearrange("p (k n) -> p k n", n=1024) for _ in range(2)]
        B_tg = [Buf("tg0"), Buf("tg1")]; B_tu = [Buf("tu0"), Buf("tu1")]; B_sg = [Buf("sg0"), Buf("sg1")]
        B_actT = [Buf("actT0"), Buf("actT1")]; B_wdn = [Buf("wdn0"), Buf("wdn1")]
        P.guard(writes=[B_wout, B_mixed, B_hmh] + B_xr + B_tg + B_tu + B_sg + B_actT + B_wdn)
        dwd = [P.dsem("wdn%d" % i) for i in range(2)]
        wd_src = w_down_d.rearrange("(k p) n -> p k n", p=128)
        CHK = [(0, 410), (410, 410), (820, 410), (1230, 410), (1640, 408)]
        dout = P.dsem("out")
        out_ops = []
        pair_no = 0
        tcnt = 0

        def up_group(gi, pend=()):
            nonlocal tcnt
            pend = list(pend)
            (p0, p1) = FFN_GROUPS[gi]
            gs = gi % 2
            P.add("pool", lambda e, gs=gs, p0=p0, p1=p1: e.dma_start(out=wdn[gs][:, 0:p1 - p0, :], in_=wd_src[:, p0:p1, :]), writes=[B_wdn[gs]], dsem=dwd[gs])
            for pn in range(p0, p1):
                s_ = (pn // 2) % 2
                wo_ = 128 * (pn % 2)
                if pn % 2 == 0 and pn >= 2 and pn // 2 + 1 < NPAIR // 2:
                    load_pp(pn // 2 + 1)
                for (off, n) in CHK:
                    ts = tcnt % 2
                    tcnt += 1
                    bgk, buk = [(4, 5), (6, 7)][(tcnt - 1) % 2]
                    for (bk, wc) in [(bgk, wo_), (buk, 256 + wo_)]:
                        for c in range(8):
                            P.add("pe", lambda e, bk=bk, wc=wc, c=c, s_=s_, off=off, n=n: e.matmul(bank(bk)[:, 0:n + 2], lhsT=wup[s_][:, c, wc:wc + 128],
                                                                                                rhs=hn2T[:, c, off:off + n + 2], start=(c == 0), stop=(c == 7)),
                                  reads=[B_wup[s_], B_hn2T], writes=[PB[bk]])
                    halves_ = [(bgk, tg[ts], B_tg[ts], pn), (buk, tu[ts], B_tu[ts], NPAIR + pn)]
                    for (bk, tbuf, B_t, widx) in halves_:
                        P.add("act", lambda e, bk=bk, tbuf=tbuf, widx=widx, n=n: e.activation(out=tbuf[:, 0:n], in_=bank(bk)[:, 0:n], func=AF.Identity,
                                                                                           scale=fw[:, 3 * widx:3 * widx + 1], bias=fb[:, widx:widx + 1]),
                              reads=[PB[bk], B_const], writes=[B_t])
                    for tap in (1, 2):
                        for (bk, tbuf, B_t, widx) in halves_:
                            P.add("dve", lambda e, bk=bk, tbuf=tbuf, widx=widx, n=n, tap=tap: e.scalar_tensor_tensor(
                                out=tbuf[:, 0:n], in0=bank(bk)[:, tap:n + tap], scalar=fw[:, 3 * widx + tap:3 * widx + tap + 1],
                                in1=tbuf[:, 0:n], op0=ALU.mult, op1=ALU.add),
                                reads=[PB[bk], B_const, B_t], writes=[B_t])
                    P.add("act", lambda e, ts=ts, n=n: e.activation(out=sg[ts][:, 0:n], in_=tg[ts][:, 0:n], func=AF.Silu), reads=[B_tg[ts]], writes=[B_sg[ts]])
                    P.add("dve", lambda e, ts=ts, n=n, gs=gs, pn=pn, p0=p0, off=off: e.tensor_tensor(out=actT[gs][:, pn - p0, off:off + n], in0=sg[ts][:, 0:n], in1=tu[ts][:, 0:n], op=ALU.mult),
                          reads=[B_sg[ts], B_tu[ts]], writes=[B_actT[gs]])
                    if pend:
                        down_tile(*pend.pop(0))
            for rest_ in pend:
                down_tile(*rest_)

        def down_group(gi):
            for i in range(16):
                down_tile(gi, i)

        def down_tile(gi, i):
            (p0, p1) = FFN_GROUPS[gi]
            gs = gi % 2
            last = (gi == len(FFN_GROUPS) - 1)
            if True:
                for half in range(2):
                    bk = 2 * (i % 2) + half
                    for k in range(p1 - p0):
                        P.add("pe", lambda e, bk=bk, k=k, gs=gs, i=i, half=half, p0=p0, p1=p1: e.matmul(bank(bk)[:, :], lhsT=actT[gs][:, k, 128 * i:128 * i + 128],
                                                                                                   rhs=wdn[gs][:, k, half * 512:half * 512 + 512],
                                                                                                   start=(k == 0), stop=(k == p1 - p0 - 1)),
                              reads=[B_actT[gs], B_wdn[gs]], writes=[PB[bk]])
                P.add("dve", lambda e, i=i: e.tensor_tensor(out=hmid[:, i, :], in0=pbanks[i % 2][:, :], in1=hmid[:, i, :], op=ALU.add),
                      reads=[PB[2 * (i % 2)], PB[2 * (i % 2) + 1], B_hmid[i]], writes=[B_hmid[i]])
                if last:
                    out_ops.append(P.add("sp", lambda e, i=i: e.dma_start(out=y_d[128 * i:128 * i + 128, :], in_=hmid[:, i, :]), reads=[B_hmid[i]], dsem=dout))

        ng_ = len(FFN_GROUPS)
        up_group(0)
        for gi in range(1, ng_):
            up_group(gi, [(gi - 1, i) for i in range(16)])
        down_group(ng_ - 1)
        fin = P.add("sp", lambda e: e.nop())
        fin.deps.extend(out_ops)

        P.emit()
    return nc


_NC_CACHE = {}


def _get_nc():
    if "nc" not in _NC_CACHE:
        _NC_CACHE["nc"] = build_program()
    return _NC_CACHE["nc"]


def kernel(x, mem, positions, mix_norm_g, mem_norm_g, w_in, w_mem_kv, q_norm_g, k_norm_g, mq_norm_g, mk_norm_g,
           conv_dw_w, conv_dw_b, conv_ln_g, conv_ln_b, w_out, ffn_norm_g, w_up, ffn_dw_w, ffn_dw_b, w_down):
    f32 = np.float32
    x = np.asarray(x, f32); mem = np.asarray(mem, f32); positions = np.asarray(positions, np.int32)
    PT, NVT = pattern_tiles()
    ident = np.eye(128, dtype=f32)
    pp = np.arange(128)[:, None]; cc = np.arange(256)[None, :]
    bandmask = np.where((cc >= pp) & (cc <= pp + 128), 0.0, -30000.0).astype(f32)
    invf = (f32(500000.0) ** (-(np.arange(0, 16, 2, dtype=f32)) / f32(16))).astype(f32)
    invf = np.ascontiguousarray(np.broadcast_to(invf[None, :], (128, 8)))
    hg = np.concatenate([np.asarray(g, f32).reshape(-1) for g in (q_norm_g, k_norm_g, mq_norm_g, mk_norm_g)])[None, :]
    cwv = np.asarray(conv_dw_w, f32)[0]
    cw = np.ascontiguousarray(cwv.T.reshape(2, 128, 31).transpose(1, 0, 2).reshape(128, 62))
    def pc(v):
        return np.asarray(v, f32).reshape(2, 128).T
    cp = np.ascontiguousarray(np.concatenate([pc(conv_dw_b[0]), pc(conv_ln_g[0]), pc(conv_ln_b[0])], axis=1))
    fwv = np.asarray(ffn_dw_w, f32)[0]
    fw = np.ascontiguousarray(fwv.T.reshape(44, 128, 3).transpose(1, 0, 2).reshape(128, 132))
    fb = np.ascontiguousarray(np.asarray(ffn_dw_b, f32)[0].reshape(44, 128).T)
    shared = {
        "ident": ident, "bandmask": bandmask, "invf": invf,
        "g1": np.asarray(mix_norm_g, f32).reshape(1, 1024), "g2": np.asarray(ffn_norm_g, f32).reshape(1, 1024),
        "gm": np.asarray(mem_norm_g, f32).reshape(1, 1024), "hg": np.ascontiguousarray(hg),
        "w_in": np.asarray(w_in, f32)[0], "w_mkv": np.asarray(w_mem_kv, f32)[0], "w_out": np.asarray(w_out, f32)[0],
        "w_up": np.asarray(w_up, f32)[0], "w_down": np.asarray(w_down, f32)[0],
        "cw": cw, "cp": cp, "fw": fw, "fb": fb,
    }
    in_maps = []
    for ci in range(8):
        b = ci // 4
        T0 = (ci % 4) * 2048
        t_start = T0 - 1088
        tt = t_start + np.arange(NWIN)
        ok = (tt >= 0) & (tt < S)
        xw = np.zeros((NWIN, 1024), f32)
        xw[ok] = x[b, tt[ok]]
        pw = np.zeros((NWIN,), np.int32)
        pw[ok] = positions[b, tt[ok]]
        pos_win = np.ascontiguousarray(pw.reshape(NT, 128).T)
        val = np.zeros((128, NVT), f32)
        for (D, rho, sq0, nq, nqb, tiles) in PT:
            for (a, nk, idx) in tiles:
                u = rho + D * (a + np.arange(128))
                t = t_start + u
                val[:, idx] = np.where((t >= 0) & (t < S) & (np.arange(128) < nk), 0.0, -30000.0).astype(f32)
        m = dict(shared)
        hvv = np.zeros((128, 2), f32)
        hvv[:, 0] = 1.0 if (T0 - 1) >= 0 else 0.0
        hvv[:, 1] = 1.0 if (T0 + 2048) < S else 0.0
        m.update({"x_win": xw, "pos_win": pos_win, "valcols": val, "mem": np.ascontiguousarray(mem[b]), "hv": hvv})
        in_maps.append(m)
    nc = _get_nc()
    res = run_bass_kernel_spmd(nc, in_maps, core_ids=list(range(8)))
    out = np.zeros((2, S, 1024), f32)
    for ci in range(8):
        b = ci // 4
        T0 = (ci % 4) * 2048
        out[b, T0:T0 + 2048] = res.results[ci]["y"]
    if DEBUG:
        kernel.last_results = res.results
    return out
```
